# Optimizing a Trainium2 kernel written in Bass

```python
import math
import jax
import jax.numpy as jnp
from jax import lax
import numpy as np

D_MODEL = 1024
BATCH = 2
SEQ = 8192
DEPTH = 4

N_A_LAYERS = DEPTH // 2
N_B_LAYERS = DEPTH - N_A_LAYERS

A_HEADS = 8
A_DK = 128
A_DV = 128
A_KEY_W = A_HEADS * A_DK
A_VAL_W = A_HEADS * A_DV
A_CONV_CH = 2 * A_KEY_W + A_VAL_W
A_IN_W = A_CONV_CH + A_VAL_W + 2 * A_HEADS
CONV_W = 4
CHUNK = 64

B_HEADS = 8
B_DH = 128
B_W = B_HEADS * B_DH
MOBA_BLOCK = 256
MOBA_TOPK = 3
Q_SUB = 32

N_BUCKETS = 32
MAX_DIST = 2048
EPS = 1e-6

kernel_name = "hybrid_gdn_moba_yoco"


def rms_norm(x, g):
    xf = x.astype(jnp.float32)
    y = xf * lax.rsqrt(jnp.mean(xf * xf, axis=-1, keepdims=True) + EPS)
    return (y * g.astype(jnp.float32)).astype(x.dtype)


def l2_normalize(t):
    tf = t.astype(jnp.float32)
    return tf * lax.rsqrt(jnp.sum(tf * tf, axis=-1, keepdims=True) + EPS)


def causal_conv_silu(u, w):
    c = u.shape[-1]
    up = jnp.pad(u, ((0, 0), (CONV_W - 1, 0), (0, 0)))
    out = lax.conv_general_dilated(up, w[:, None, :].astype(u.dtype), window_strides=(1,), padding="VALID", dimension_numbers=("NWC", "WIO", "NWC"), feature_group_count=c)
    return jax.nn.silu(out)


def gated_delta_rule(q, k, v, g, beta):
    bsz, s, h, dk = q.shape
    dv = v.shape[-1]
    n = s // CHUNK
    f32 = jnp.float32

    def to_chunks(t):
        t = t.astype(f32).reshape((bsz, n, CHUNK) + t.shape[2:])
        return jnp.moveaxis(t, 3, 1)

    q, k, v, g, beta = (to_chunks(t) for t in (q, k, v, g, beta))
    g = jnp.cumsum(g, axis=-1)
    idx = jnp.arange(CHUNK)
    tril = idx[:, None] >= idx[None, :]
    strict = idx[:, None] > idx[None, :]
    decay = jnp.exp(jnp.where(tril, g[..., :, None] - g[..., None, :], -jnp.inf))
    k_beta = k * beta[..., None]
    a_mat = jnp.where(strict, jnp.einsum("bhncd,bhnsd->bhncs", k_beta, k) * decay, 0.0)
    rhs = jnp.concatenate([v * beta[..., None], k_beta * jnp.exp(g)[..., None]], axis=-1)
    sol = lax.linalg.triangular_solve(a_mat + jnp.eye(CHUNK, dtype=f32), rhs, left_side=True, lower=True, unit_diagonal=True)
    u, w = sol[..., :dv], sol[..., dv:]
    qk_intra = jnp.where(tril, jnp.einsum("bhncd,bhnsd->bhncs", q, k) * decay, 0.0)
    q_dec = q * jnp.exp(g)[..., None]
    k_dec = k * jnp.exp(g[..., -1:] - g)[..., None]
    g_last = jnp.exp(g[..., -1])

    def step(state, xs):
        u_c, w_c, qd_c, qk_c, kd_c, gl_c = xs
        v_new = u_c - jnp.einsum("bhck,bhkv->bhcv", w_c, state)
        o = jnp.einsum("bhck,bhkv->bhcv", qd_c, state) + jnp.einsum("bhcs,bhsv->bhcv", qk_c, v_new)
        state = state * gl_c[..., None, None] + jnp.einsum("bhck,bhcv->bhkv", kd_c, v_new)
        return state, o

    xs = tuple(jnp.moveaxis(t, 2, 0) for t in (u, w, q_dec, qk_intra, k_dec, g_last))
    state0 = jnp.zeros((bsz, h, dk, dv), f32)
    _, o = lax.scan(step, state0, xs)
    return o.transpose(1, 0, 3, 2, 4).reshape(bsz, s, h, dv)


def gated_deltanet_mixer(hn, w_in, conv_w, a_log, dt_bias, out_norm_g, w_out):
    bsz, s, _ = hn.shape
    f32 = jnp.float32
    proj = hn @ w_in
    qkv = causal_conv_silu(proj[..., :A_CONV_CH], conv_w)
    z = proj[..., A_CONV_CH:A_CONV_CH + A_VAL_W]
    a = proj[..., A_CONV_CH + A_VAL_W:A_CONV_CH + A_VAL_W + A_HEADS]
    b = proj[..., A_CONV_CH + A_VAL_W + A_HEADS:]
    q = l2_normalize(qkv[..., :A_KEY_W].reshape(bsz, s, A_HEADS, A_DK)) * (A_DK ** -0.5)
    k = l2_normalize(qkv[..., A_KEY_W:2 * A_KEY_W].reshape(bsz, s, A_HEADS, A_DK))
    v = qkv[..., 2 * A_KEY_W:].reshape(bsz, s, A_HEADS, A_DV)
    g = -jnp.exp(a_log.astype(f32)) * jax.nn.softplus(a.astype(f32) + dt_bias.astype(f32))
    beta = jax.nn.sigmoid(b.astype(f32))
    o = gated_delta_rule(q, k, v, g, beta)
    o = rms_norm(o, out_norm_g) * jax.nn.silu(z.astype(f32)).reshape(bsz, s, A_HEADS, A_DV)
    return o.reshape(bsz, s, A_VAL_W).astype(hn.dtype) @ w_out


def t5_bucket(rel):
    n = jnp.maximum(rel, 0)
    max_exact = N_BUCKETS // 2
    nf = jnp.maximum(n, 1).astype(jnp.float32)
    large = max_exact + (jnp.log(nf / max_exact) / math.log(MAX_DIST / max_exact) * (N_BUCKETS - max_exact)).astype(jnp.int32)
    large = jnp.minimum(large, N_BUCKETS - 1)
    return jnp.where(n < max_exact, n, large)


def shared_kv(x, kv_norm_g, w_kv):
    bsz, s, _ = x.shape
    kv = rms_norm(x, kv_norm_g) @ w_kv
    n_blk = -(-s // MOBA_BLOCK)
    s_pad = n_blk * MOBA_BLOCK

    def blocks(t):
        t = jnp.pad(t.reshape(bsz, s, B_HEADS, B_DH), ((0, 0), (0, s_pad - s), (0, 0), (0, 0)))
        return t.reshape(bsz, n_blk, MOBA_BLOCK, B_HEADS, B_DH).transpose(0, 3, 1, 2, 4)

    kb = blocks(kv[..., :B_W])
    vb = blocks(kv[..., B_W:])
    k_mean = jnp.mean(kb.astype(jnp.float32), axis=3)
    return kb, vb, k_mean


def moba_attention(q, kb, vb, k_mean, rel_bias):
    bsz, s, h, d = q.shape
    n_blk = kb.shape[2]
    topk = min(MOBA_TOPK, n_blk)
    scale = d ** -0.5
    f32 = jnp.float32
    n_sub = s // Q_SUB
    q_sub = q.transpose(0, 2, 1, 3).reshape(bsz, h, n_sub, Q_SUB, d).transpose(2, 0, 1, 3, 4)
    bias_t = rel_bias.T.astype(f32)
    b_ix = jnp.arange(bsz)[:, None, None, None]
    h_ix = jnp.arange(h)[None, :, None, None]
    blk_ids = jnp.arange(n_blk)
    offs = jnp.arange(MOBA_BLOCK)

    def one_sub(args):
        qs, s_idx = args
        q_pos = s_idx * Q_SUB + jnp.arange(Q_SUB)
        cur = (s_idx * Q_SUB) // MOBA_BLOCK
        qf = qs.astype(f32)
        gate = jnp.einsum("bhqd,bhnd->bhqn", qf, k_mean)
        gate = jnp.where(blk_ids < cur, gate, -jnp.inf)
        _, sel = lax.top_k(gate, topk)
        sel_ok = jnp.arange(topk) < cur
        k_sel = kb[b_ix, h_ix, sel].astype(f32)
        v_sel = vb[b_ix, h_ix, sel].astype(f32)
        k_pos_sel = sel[..., None] * MOBA_BLOCK + offs
        s_sel = jnp.einsum("bhqd,bhqkjd->bhqkj", qf, k_sel) * scale
        s_sel = s_sel + bias_t[h_ix[..., None], t5_bucket(q_pos[:, None, None] - k_pos_sel)]
        s_sel = jnp.where(sel_ok[:, None], s_sel, -jnp.inf).reshape(bsz, h, Q_SUB, topk * MOBA_BLOCK)
        k_own = lax.dynamic_index_in_dim(kb, cur, axis=2, keepdims=False).astype(f32)
        v_own = lax.dynamic_index_in_dim(vb, cur, axis=2, keepdims=False).astype(f32)
        rel_own = q_pos[:, None] - (cur * MOBA_BLOCK + offs)[None, :]
        s_own = jnp.einsum("bhqd,bhjd->bhqj", qf, k_own) * scale + bias_t[:, t5_bucket(rel_own)]
        s_own = jnp.where(rel_own >= 0, s_own, -jnp.inf)
        p = jax.nn.softmax(jnp.concatenate([s_sel, s_own], axis=-1), axis=-1)
        p_sel = p[..., :topk * MOBA_BLOCK].reshape(bsz, h, Q_SUB, topk, MOBA_BLOCK)
        p_own = p[..., topk * MOBA_BLOCK:]
        o = jnp.einsum("bhqkj,bhqkjd->bhqd", p_sel, v_sel) + jnp.einsum("bhqj,bhjd->bhqd", p_own, v_own)
        return o.astype(qs.dtype)

    o = lax.map(one_sub, (q_sub, jnp.arange(n_sub)))
    return o.transpose(1, 0, 3, 2, 4).reshape(bsz, s, h * d)


def moba_mixer(hn, w_in, w_out, kb, vb, k_mean, rel_bias):
    bsz, s, _ = hn.shape
    proj = hn @ w_in
    q = proj[..., :B_W].reshape(bsz, s, B_HEADS, B_DH)
    z = proj[..., B_W:]
    o = moba_attention(q, kb, vb, k_mean, rel_bias)
    return (o * jax.nn.silu(z)).astype(hn.dtype) @ w_out


def setup_inputs(seed: int = 0) -> dict:
    key = jax.random.key(seed)
    ks = jax.random.split(key, 16)
    f32 = jnp.float32
    na, nb = N_A_LAYERS, N_B_LAYERS

    def gain(k, shape):
        return 1.0 + 0.01 * jax.random.normal(k, shape, f32)

    x = jax.random.normal(ks[0], (BATCH, SEQ, D_MODEL), f32)
    a_norm_g = gain(ks[1], (na, D_MODEL))
    a_w_in = jax.random.normal(ks[2], (na, D_MODEL, A_IN_W), f32) * D_MODEL ** -0.5
    a_conv_w = jax.random.normal(ks[3], (na, CONV_W, A_CONV_CH), f32) * CONV_W ** -0.5
    a_log = jnp.log(jax.random.uniform(ks[4], (na, A_HEADS), f32, 1.0, 16.0))
    dt = jnp.exp(jax.random.uniform(ks[5], (na, A_HEADS), f32, math.log(1e-3), math.log(1e-1)))
    a_dt_bias = dt + jnp.log(-jnp.expm1(-dt))
    a_out_norm_g = gain(ks[6], (na, A_DV))
    a_w_out = jax.random.normal(ks[7], (na, A_VAL_W, D_MODEL), f32) * A_VAL_W ** -0.5
    kv_norm_g = gain(ks[8], (D_MODEL,))
    w_kv = jax.random.normal(ks[9], (D_MODEL, 2 * B_W), f32) * D_MODEL ** -0.5
    b_norm_g = gain(ks[10], (nb, D_MODEL))
    b_w_in = jax.random.normal(ks[11], (nb, D_MODEL, 2 * B_W), f32) * D_MODEL ** -0.5
    b_w_out = jax.random.normal(ks[12], (nb, B_W, D_MODEL), f32) * B_W ** -0.5
    rel_bias = 0.2 * jax.random.normal(ks[13], (N_BUCKETS, B_HEADS), f32)
    final_norm_g = gain(ks[14], (D_MODEL,))
    return {"x": x, "a_norm_g": a_norm_g, "a_w_in": a_w_in, "a_conv_w": a_conv_w, "a_log": a_log, "a_dt_bias": a_dt_bias, "a_out_norm_g": a_out_norm_g, "a_w_out": a_w_out, "kv_norm_g": kv_norm_g, "w_kv": w_kv, "b_norm_g": b_norm_g, "b_w_in": b_w_in, "b_w_out": b_w_out, "rel_bias": rel_bias, "final_norm_g": final_norm_g}


def reference(x, a_norm_g, a_w_in, a_conv_w, a_log, a_dt_bias, a_out_norm_g, a_w_out, kv_norm_g, w_kv, b_norm_g, b_w_in, b_w_out, rel_bias, final_norm_g):
    shared = None
    for layer in range(DEPTH):
        if layer < N_A_LAYERS:
            i = layer
            x = x + gated_deltanet_mixer(rms_norm(x, a_norm_g[i]), a_w_in[i], a_conv_w[i], a_log[i], a_dt_bias[i], a_out_norm_g[i], a_w_out[i])
        else:
            if shared is None:
                shared = shared_kv(x, kv_norm_g, w_kv)
            j = layer - N_A_LAYERS
            kb, vb, k_mean = shared
            x = x + moba_mixer(rms_norm(x, b_norm_g[j]), b_w_in[j], b_w_out[j], kb, vb, k_mean, rel_bias)
    return rms_norm(x, final_norm_g)
```

```python
import numpy as np
import concourse.bass as bass
import concourse.mybir as mybir
from concourse.bass_utils import run_bass_kernel_spmd
from contextlib import ExitStack

F32 = mybir.dt.float32
BF16 = mybir.dt.bfloat16
AF = mybir.ActivationFunctionType
ALU = mybir.AluOpType
AX = mybir.AxisListType


class Buf:
    __slots__ = ("name", "last_w", "readers", "excl")

    def __init__(self, name="", excl=False):
        self.name = name
        self.last_w = None
        self.readers = []
        self.excl = excl


class Q:
    def __init__(self, name, is_pe=False):
        self.name = name
        self.is_pe = is_pe
        self.sem = None
        self.cnt = 0
        self.known = {}
        self.prog = []
        self.dsems = []
        self.dcnt = []
        self.dnext = 0


class Sched:
    NDMA = 8

    def __init__(self, nc, es):
        self.nc = nc
        self.es = es
        self.q = {}
        for n in ("pe", "act", "dve", "pool", "sp"):
            q = Q(n, is_pe=(n == "pe"))
            q.sem = es.enter_context(nc.semaphore("s_" + n))
            self.q[n] = q
        for n in ("act", "pool", "sp"):
            q = self.q[n]
            for i in range(self.NDMA):
                q.dsems.append(es.enter_context(nc.semaphore("d_%s%d" % (n, i))))
                q.dcnt.append(0)
        self.nops = 0

    def _waits(self, q, deps):
        waits = []
        for tok, skip_same in deps:
            sem, val, src = tok
            if src is q and (q.is_pe or skip_same):
                continue
            key = id(sem)
            if q.known.get(key, 0) >= val:
                continue
            q.known[key] = val
            waits.append((sem, val))
        return waits

    def _deps(self, reads, writes):
        deps = []
        for b in reads:
            if b.last_w is not None:
                deps.append((b.last_w, b.excl))
        for b in writes:
            if b.last_w is not None:
                deps.append((b.last_w, b.excl))
            deps.extend((r, False) for r in b.readers)
        return deps

    def _commit(self, tok, reads, writes):
        for b in reads:
            if b.excl:
                b.last_w = tok
            else:
                b.readers.append(tok)
        for b in writes:
            b.last_w = tok
            b.readers = []

    def op(self, qn, fn, reads=(), writes=()):
        q = self.q[qn]
        waits = self._waits(q, self._deps(reads, writes))
        q.cnt += 1
        tok = (q.sem, q.cnt, q)
        q.prog.append((waits, fn, (q.sem, 1)))
        self._commit(tok, reads, writes)
        self.nops += 1
        return tok

    def dma(self, qn, fn, reads=(), writes=()):
        q = self.q[qn]
        deps = self._deps(reads, writes)
        k = q.dnext
        q.dnext = (k + 1) % len(q.dsems)
        sem = q.dsems[k]
        if q.dcnt[k] > 0:
            deps.append(((sem, q.dcnt[k], None), False))
        waits = self._waits(q, deps)
        q.dcnt[k] += 16
        tok = (sem, q.dcnt[k], None)
        q.prog.append((waits, fn, (sem, 16)))
        self._commit(tok, reads, writes)
        self.nops += 1
        return tok

    def finalize(self, final_toks):
        nc = self.nc
        q = self.q["sp"]
        waits = self._waits(q, [(t, False) for t in final_toks])
        q.prog.append((waits, None, None))
        with nc.Block() as block:
            def replay(qn, eng):
                for waits, fn, inc in self.q[qn].prog:
                    for sem, val in waits:
                        eng.wait_ge(sem, val)
                    if fn is not None:
                        inst = fn(eng)
                        inst.then_inc(inc[0], inc[1])

            @block.tensor
            def _(e):
                replay("pe", e)

            @block.scalar
            def _(e):
                replay("act", e)

            @block.vector
            def _(e):
                replay("dve", e)

            @block.gpsimd
            def _(e):
                replay("pool", e)

            @block.sync
            def _(e):
                replay("sp", e)


EPS = 1e-6


class Ctx:
    def __init__(self, nc, es):
        self.nc = nc
        self.es = es
        self.S = Sched(nc, es)

    def sb(self, name, shape, dt):
        return self.es.enter_context(self.nc.sbuf_tensor(name, shape, dt))

    def ps(self, name, shape, dt):
        return self.es.enter_context(self.nc.psum_tensor(name, shape, dt))


def build_tok(has_proj, final):
    nc = bass.Bass("TRN2", target_bir_lowering=False)
    NT = 16
    x = nc.dram_tensor("x", [2048, 1024], F32, kind="ExternalInput").ap()
    ident_d = nc.dram_tensor("ident", [128, 128], BF16, kind="ExternalInput").ap()
    if has_proj:
        og = nc.dram_tensor("og", [8, 128, 2048], BF16, kind="ExternalInput").ap()
        wo = nc.dram_tensor("wo", [1024, 1024], F32, kind="ExternalInput").ap()
    if final:
        gf = nc.dram_tensor("gf", [1024], F32, kind="ExternalInput").ap()
        y = nc.dram_tensor("y", [2048, 1024], F32, kind="ExternalOutput").ap()
    else:
        xo = nc.dram_tensor("xo", [2048, 1024], F32, kind="ExternalOutput").ap()
        xh = nc.dram_tensor("xh", [8, 128, 2048], BF16, kind="ExternalOutput").ap()
    with ExitStack() as es:
        C = Ctx(nc, es)
        S = C.S
        fin = []
        ident = C.sb("identb", [128, 128], BF16)
        b_ident = Buf()
        S.dma("sp", lambda e: e.dma_start(out=ident[:], in_=ident_d[:, :]), [], [b_ident])
        if has_proj:
            wof = C.sb("wof", [128, 8, 1024], F32)
            wob = C.sb("wob", [128, 8, 1024], BF16)
            b_wof, b_wob = Buf(), Buf()
            wo_v = wo.rearrange("(h p) n -> p h n", p=128)
            for h in range(8):
                S.dma("sp" if h % 2 == 0 else "pool",
                      lambda e, h=h: e.dma_start(out=wof[:, h, :], in_=wo_v[:, h, :]), [], [b_wof])
            for h in range(8):
                S.op("pool" if h % 2 == 0 else "dve",
                     lambda e, h=h: e.tensor_copy(out=wob[:, h, :], in_=wof[:, h, :]), [b_wof], [b_wob])
        if final:
            gfb = C.sb("gfb", [128, 1024], F32)
            b_gfb = Buf()
            S.dma("pool", lambda e: e.dma_start(out=gfb[:], in_=gf.partition_broadcast(128)), [], [b_gfb])
        NB = 2
        epsb = C.sb("epsb", [128, 1], F32)
        b_eps = Buf()
        S.op("pool", lambda e: e.memset(epsb[:], EPS), [], [b_eps])
        xt = [C.sb("xt%d" % i, [128, 1024], F32) for i in range(NB)]
        b_xt = [Buf() for _ in range(NB)]
        sq = [C.sb("sq%d" % i, [128, 1024], F32) for i in range(NB)]
        b_sq = [Buf() for _ in range(NB)]
        st = [C.sb("st%d" % i, [128, 4], F32) for i in range(NB)]
        b_st = [Buf() for _ in range(NB)]
        if has_proj:
            ot = [C.sb("ot%d" % i, [128, 8, 128], BF16) for i in range(NB)]
            b_ot = [Buf() for _ in range(NB)]
            pp = [C.ps("pp%d" % i, [128, 1024], F32) for i in range(NB)]
            b_pp = [Buf() for _ in range(NB)]
        if final:
            yt = [C.sb("yt%d" % i, [128, 1024], F32) for i in range(NB)]
            b_yt = [Buf() for _ in range(NB)]
        else:
            xb = [C.sb("xb%d" % i, [128, 1024], BF16) for i in range(NB)]
            b_xb = [Buf() for _ in range(NB)]
            pt = [C.ps("pt%d" % i, [128, 8, 128], BF16) for i in range(NB)]
            b_pt = [Buf() for _ in range(NB)]
            stg = [C.sb("stg%d" % i, [128, 8, 512], BF16) for i in range(2)]
            b_stg = [Buf() for _ in range(2)]
        for t in range(NT):
            i = t % NB
            t0 = t * 128
            S.dma("sp", lambda e, i=i, t0=t0: e.dma_start(out=xt[i][:], in_=x[t0:t0 + 128, :]), [], [b_xt[i]])
            if has_proj:
                S.dma("pool", lambda e, i=i, t0=t0: e.dma_start(
                    out=ot[i][:], in_=og[:, :, t0:t0 + 128].rearrange("h p t -> p h t")), [], [b_ot[i]])
                for nch in range(2):
                    for h in range(8):
                        S.op("pe", lambda e, i=i, h=h, nch=nch: e.matmul(
                            pp[i][:, nch * 512:(nch + 1) * 512], lhsT=ot[i][:, h, :],
                            rhs=wob[:, h, nch * 512:(nch + 1) * 512], start=(h == 0), stop=(h == 7)),
                            [b_ot[i], b_wob], [b_pp[i]])
                S.op("dve", lambda e, i=i: e.tensor_tensor(out=xt[i][:], in0=xt[i][:], in1=pp[i][:], op=ALU.add),
                     [b_xt[i], b_pp[i]], [b_xt[i]])
            if not final:
                S.dma("pool", lambda e, i=i, t0=t0: e.dma_start(out=xo[t0:t0 + 128, :], in_=xt[i][:]), [b_xt[i]], [])
                fin.append(None)
            S.op("act", lambda e, i=i: e.activation(out=sq[i][:], in_=xt[i][:], func=AF.Square,
                                                    accum_out=st[i][:, 0:1]), [b_xt[i]], [b_sq[i], b_st[i]])
            S.op("act", lambda e, i=i: e.activation(out=st[i][:, 1:2], in_=st[i][:, 0:1], func=AF.Ln,
                                                    scale=1.0 / 1024, bias=epsb[:, 0:1]), [b_st[i], b_eps], [b_st[i]])
            S.op("act", lambda e, i=i: e.activation(out=st[i][:, 2:3], in_=st[i][:, 1:2], func=AF.Exp,
                                                    scale=-0.5), [b_st[i]], [b_st[i]])
            if final:
                S.op("dve", lambda e, i=i: e.scalar_tensor_tensor(out=yt[i][:], in0=xt[i][:], scalar=st[i][:, 2:3],
                                                                   in1=gfb[:], op0=ALU.mult, op1=ALU.mult),
                     [b_xt[i], b_st[i], b_gfb], [b_yt[i]])
                fin.append(S.dma("sp", lambda e, i=i, t0=t0: e.dma_start(out=y[t0:t0 + 128, :], in_=yt[i][:]),
                                 [b_yt[i]], []))
            else:
                S.op("act", lambda e, i=i: e.activation(out=xb[i][:], in_=xt[i][:], func=AF.Copy,
                                                        scale=st[i][:, 2:3]), [b_xt[i], b_st[i]], [b_xb[i]])
                for dc in range(8):
                    S.op("pe", lambda e, i=i, dc=dc: e.transpose(pt[i][:, dc, :], xb[i][:, dc * 128:(dc + 1) * 128],
                                                                 ident[:]), [b_xb[i], b_ident], [b_pt[i]])
                g = (t // 4) % 2
                tq = t % 4
                S.op("dve", lambda e, i=i, g=g, tq=tq: e.tensor_copy(out=stg[g][:, :, tq * 128:(tq + 1) * 128],
                                                                     in_=pt[i][:]), [b_pt[i]], [b_stg[g]])
                if tq == 3:
                    tb = (t // 4) * 512
                    fin.append(S.dma("sp", lambda e, g=g, tb=tb: e.dma_start(
                        out=xh[:, :, tb:tb + 512].rearrange("c p t -> p c t"), in_=stg[g][:]), [b_stg[g]], []))
        S.finalize([f for f in fin if f is not None])
    return nc


def roundrobin(gens):
    gens = list(gens)
    while gens:
        nxt = []
        for g in gens:
            try:
                next(g)
                nxt.append(g)
            except StopIteration:
                pass
        gens = nxt


def build_gdn(ntiles=32):
    nc = bass.Bass("TRN2", target_bir_lowering=False)
    xh = nc.dram_tensor("xh", [4, 8, 128, 2048], BF16, kind="ExternalInput").ap()
    wqkv = nc.dram_tensor("wqkv", [1024, 768], F32, kind="ExternalInput").ap()
    wzab = nc.dram_tensor("wzab", [1024, 260], F32, kind="ExternalInput").ap()
    gn = nc.dram_tensor("gn", [128, 8], F32, kind="ExternalInput").ap()
    cw = nc.dram_tensor("cw", [128, 24], F32, kind="ExternalInput").ap()
    hp = nc.dram_tensor("hp", [128, 4], F32, kind="ExternalInput").ap()
    ong = nc.dram_tensor("ong", [128, 256], F32, kind="ExternalInput").ap()
    ident_d = nc.dram_tensor("ident", [128, 128], BF16, kind="ExternalInput").ap()
    cf_d = nc.dram_tensor("cf", [128, 4, 128], F32, kind="ExternalInput").ap()
    cm_d = nc.dram_tensor("cm", [128, 10, 128], BF16, kind="ExternalInput").ap()
    og = nc.dram_tensor("og", [2, 128, 8192], BF16, kind="ExternalOutput").ap()
    with ExitStack() as es:
        C = Ctx(nc, es)
        S = C.S
        fin = []
        identb = C.sb("identb", [128, 128], BF16)
        cf = C.sb("cfs", [128, 4, 128], F32)
        gnt = C.sb("gnt", [128, 8], F32)
        cwt = C.sb("cwt", [128, 24], F32)
        hpt = C.sb("hpt", [128, 4], F32)
        ongt = C.sb("ongt", [128, 256], F32)
        b_const = Buf()
        for dst, src in ((identb, ident_d), (gnt, gn), (cwt, cw), (hpt, hp), (ongt, ong)):
            S.dma("sp", lambda e, dst=dst, src=src: e.dma_start(out=dst[:], in_=src[:, :]), [], [b_const])
        S.dma("sp", lambda e: e.dma_start(out=cf[:], in_=cf_d[:, :, :]), [], [b_const])
        cmt = C.sb("cmt", [128, 10, 128], BF16)
        S.dma("sp", lambda e: e.dma_start(out=cmt[:], in_=cm_d[:, :, :]), [], [b_const])
        Uf = cf[:, 0, :]
        NMA = cf[:, 1, :]
        NMQ = cf[:, 2, :]
        identf = cf[:, 3, :]
        onesf = C.sb("onesf", [128, 128], F32)
        onesb = C.sb("onesb", [128, 128], BF16)
        misc = C.sb("misc", [128, 8], F32)
        S.op("pool", lambda e: e.memset(onesf[:], 1.0), [], [b_const])
        S.op("pool", lambda e: e.memset(onesb[:], 1.0), [], [b_const])
        S.op("pool", lambda e: e.memset(misc[:, 0:1], EPS), [], [b_const])
        S.op("pool", lambda e: e.memset(misc[:, 1:2], 128 * EPS), [], [b_const])
        S.op("pool", lambda e: e.memset(misc[:, 2:3], 1.0), [], [b_const])
        S.op("act", lambda e: e.activation(out=misc[:, 5:7], in_=hpt[:, 0:2], func=AF.Exp), [b_const], [b_const])
        S.op("dve", lambda e: e.tensor_scalar(out=misc[:, 3:5], in0=misc[:, 5:7], scalar1=-1.0, scalar2=None,
                                              op0=ALU.mult), [b_const], [b_const])
        dg = C.sb("dg", [128, 24, 128], BF16)
        for k in range(24):
            S.op("dve" if k % 2 else "pool", lambda e, k=k: e.tensor_scalar(
                out=dg[:, k, :], in0=identf, scalar1=cwt[:, k:k + 1], scalar2=None, op0=ALU.mult),
                [b_const], [b_const])
        wq = C.sb("wq", [128, 8, 768], BF16)
        wz = C.sb("wz", [128, 8, 260], BF16)
        wst = [C.sb("wst%d" % i, [128, 1028], F32) for i in range(2)]
        b_wst = [Buf(), Buf()]
        b_w = Buf()
        for dc in range(8):
            i = dc % 2
            S.dma("sp", lambda e, i=i, dc=dc: e.dma_start(out=wst[i][:, 0:768], in_=wqkv[dc * 128:(dc + 1) * 128, :]),
                  [], [b_wst[i]])
            S.dma("pool", lambda e, i=i, dc=dc: e.dma_start(out=wst[i][:, 768:1028], in_=wzab[dc * 128:(dc + 1) * 128, :]),
                  [], [b_wst[i]])
            S.op("dve", lambda e, i=i, dc=dc: e.tensor_scalar(out=wq[:, dc, :], in0=wst[i][:, 0:768],
                                                              scalar1=gnt[:, dc:dc + 1], scalar2=None, op0=ALU.mult),
                 [b_wst[i], b_const], [b_w])
            S.op("dve", lambda e, i=i, dc=dc: e.tensor_scalar(out=wz[:, dc, :], in0=wst[i][:, 768:1028],
                                                              scalar1=gnt[:, dc:dc + 1], scalar2=None, op0=ALU.mult),
                 [b_wst[i], b_const], [b_w])
        TW = 256
        xt = [C.sb("xt%d" % i, [128, 8, TW], BF16) for i in range(2)]
        b_xt = [Buf(), Buf()]
        Pc = [[C.sb("Pc%d_%d" % (g, i), [128, TW + 3], BF16) for i in range(2)] for g in range(6)]
        b_Pc = [[Buf(), Buf()] for g in range(6)]
        for g in range(6):
            S.op("pool", lambda e, g=g: e.memset(Pc[g][0][:, 0:3], 0.0), [], [b_Pc[g][0]])
        s1 = [[C.sb("s1_%d_%d" % (g, i), [128, TW], BF16) for i in range(2)] for g in range(6)]
        b_s1 = [[Buf(), Buf()] for g in range(6)]
        qn = [[C.sb("qn_%d_%d" % (g, i), [128, TW], BF16) for i in range(2)] for g in range(4)]
        b_qn = [[Buf(), Buf()] for g in range(4)]
        sqb = [C.sb("sqb%d" % i, [128, TW], BF16) for i in range(2)]
        b_sqb = [Buf(), Buf()]
        lnb = [C.sb("lnb%d" % i, [128, TW], F32) for i in range(2)]
        b_lnb = [Buf(), Buf()]
        banks = [C.ps("bank%d" % i, [128, 512], F32) for i in range(8)]

        def reg(b, lo, hi, bf=False):
            ap = banks[b][:, lo:hi]
            return ap.bitcast(BF16) if bf else ap

        bankb = [Buf("bank%d" % i, excl=True) for i in range(8)]
        pj = [reg(0, 0, 256), reg(1, 0, 256)]
        b_pj = [bankb[0], bankb[1]]
        pc = [reg(0, 256, 512), reg(1, 256, 512)]
        b_pc = [bankb[0], bankb[1]]
        pz = reg(2, 0, 260)
        b_pz = bankb[2]
        gz = [C.sb("gz%d" % i, [128, 256], F32) for i in range(2)]
        b_gz = [Buf(), Buf()]
        sz = C.sb("sz", [128, 256], F32)
        b_sz = Buf()
        gt = [C.sb("gt%d" % i, [128, 16], F32) for i in range(2)]
        b_gt = [Buf(), Buf()]
        X1 = [reg(3 + 2 * h, 0, 256) for h in range(2)]
        X5 = [reg(3 + 2 * h, 256, 512) for h in range(2)]
        X2 = [reg(4 + 2 * h, 0, 256) for h in range(2)]
        X3 = [reg(4 + 2 * h, 256, 384) for h in range(2)]
        X4 = [reg(4 + 2 * h, 384, 512, True) for h in range(2)]
        X3b = [reg(7, 64 * h, 64 * h + 64, True) for h in range(2)]
        XT = [reg(7, 128 + 64 * h, 192 + 64 * h, True) for h in range(2)]
        b_X1 = [bankb[3], bankb[5]]
        b_X5a = b_X1
        b_X5b = b_X1
        b_X2 = [bankb[4], bankb[6]]
        b_X3 = b_X2
        b_X4 = b_X2
        b_X3b = [bankb[7], bankb[7]]
        b_XT = [bankb[7], bankb[7]]

        def hs(name, shape, dt):
            return [C.sb("%s_%d" % (name, h), shape, dt) for h in range(2)]

        gb = hs("gb", [128, 128], F32); b_gb = [Buf(), Buf()]
        Rsb = hs("Rsb", [128, 128], F32); b_Rsb = [Buf(), Buf()]
        egb = hs("egb", [128, 128], F32); b_egb = [Buf(), Buf()]
        gc2 = hs("gc2", [128, 8], F32); b_gc2 = [Buf(), Buf()]
        RA = hs("RA", [128, 128], F32); b_RA = [Buf(), Buf()]
        RQ = hs("RQ", [128, 128], F32); b_RQ = [Buf(), Buf()]
        DA = hs("DA", [128, 128], F32); b_DA = [Buf(), Buf()]
        DQ = hs("DQ", [128, 128], F32); b_DQ = [Buf(), Buf()]
        QKT = hs("QKT", [128, 128], BF16); b_QKT = [Buf(), Buf()]
        YZ = [[C.sb("YZ_%d_%d" % (h, i), [128, 2, 128], BF16) for i in range(2)] for h in range(2)]
        b_YZ = [[Buf(), Buf()] for h in range(2)]
        PQ = [[C.sb("PQ_%d_%d" % (h, i), [128, 2, 128], BF16) for i in range(2)] for h in range(2)]
        b_PQ = [[Buf(), Buf()] for h in range(2)]
        AB = hs("AB", [128, 2, 128], BF16); b_AB = [Buf(), Buf()]
        Wm = hs("Wm", [128, 2, 128], BF16); b_Wm = [Buf(), Buf()]
        kbg = hs("kbg", [128, 128], BF16); b_kbg = [Buf(), Buf()]
        kdec = hs("kdec", [128, 128], BF16); b_kdec = [Buf(), Buf()]
        vb = hs("vb", [128, 128], BF16); b_vb = [Buf(), Buf()]
        Usb = hs("Usb", [128, 128], F32); b_Usb = [Buf(), Buf()]
        wT = hs("wT", [128, 128], BF16); b_wT = [Buf(), Buf()]
        qdT = hs("qdT", [128, 128], BF16); b_qdT = [Buf(), Buf()]
        vnew = hs("vnew", [128, 128], BF16); b_vnew = [Buf(), Buf()]
        Sf = hs("Sf", [128, 128], F32); b_Sf = [Buf(), Buf()]
        Sb = hs("Sb", [128, 128], BF16); b_Sb = [Buf(), Buf()]
        junk = hs("junk", [128, 128], F32); b_junk = [Buf(), Buf()]
        ost = hs("ost", [128, 8], F32); b_ost = [Buf(), Buf()]
        ogt = hs("ogt", [128, 128], BF16); b_ogt = [Buf(), Buf()]
        ostg = [[C.sb("ostg_%d_%d" % (h, i), [128, 512], BF16) for i in range(2)] for h in range(2)]
        b_ostg = [[Buf(), Buf()] for h in range(2)]
        for h in range(2):
            S.op("pool", lambda e, h=h: e.memset(Sf[h][:], 0.0), [], [b_Sf[h]])
            S.op("pool", lambda e, h=h: e.memset(Sb[h][:], 0.0), [], [b_Sb[h]])

        def chunk_head(h, ti, ci, cg):
            cs = slice(ci * 128, (ci + 1) * 128)
            kT = qn[2 + h][ti][:, cs]
            qT = qn[h][ti][:, cs]
            vT = s1[4 + h][ti][:, cs]
            r_kT = [b_qn[2 + h][ti]]
            r_qT = [b_qn[h][ti]]
            r_vT = [b_s1[4 + h][ti]]
            g_ = gt[cg % 2]
            bg = b_gt[cg % 2]
            gcol = g_[:, 6 + h:7 + h]
            beta = g_[:, 12 + h:13 + h]
            S.op("dve", lambda e: e.tensor_scalar(out=gb[h][:], in0=onesf[:], scalar1=gcol, scalar2=None, op0=ALU.mult),
                 [bg, b_const], [b_gb[h]])
            S.op("pe", lambda e: e.matmul(X1[h][:, 0:128], lhsT=gb[h][:], rhs=Uf, start=True, stop=True),
                 [b_gb[h], b_const], [b_X1[h]])
            S.op("pe", lambda e: e.matmul(X1[h][:, 128:129], lhsT=Uf, rhs=gcol, start=True, stop=True),
                 [bg, b_const], [b_X1[h]])
            yield
            S.op("act", lambda e: e.activation(out=Rsb[h][:], in_=X1[h][:, 0:128], func=AF.Copy), [b_X1[h]], [b_Rsb[h]])
            S.op("act", lambda e: e.activation(out=egb[h][:], in_=X1[h][:, 0:128], func=AF.Exp), [b_X1[h]], [b_egb[h]])
            S.op("dve", lambda e: e.tensor_copy(out=gc2[h][:, 0:1], in_=X1[h][:, 128:129]), [b_X1[h]], [b_gc2[h]])
            S.op("dve", lambda e: e.tensor_scalar(out=gc2[h][:, 1:2], in0=X1[h][:, 128:129], scalar1=-1.0, scalar2=None,
                                                  op0=ALU.mult), [b_X1[h]], [b_gc2[h]])
            S.op("act", lambda e: e.activation(out=gc2[h][:, 2:3], in_=X1[h][:, 128:129], func=AF.Exp),
                 [b_X1[h]], [b_gc2[h]])
            S.op("dve", lambda e: e.tensor_tensor(out=gc2[h][:, 3:4], in0=gc2[h][:, 2:3], in1=beta, op=ALU.mult),
                 [b_gc2[h], bg], [b_gc2[h]])
            yield
            S.op("pool", lambda e: e.tensor_tensor(out=RA[h][:], in0=Rsb[h][:], in1=NMA, op=ALU.add),
                 [b_Rsb[h], b_const], [b_RA[h]])
            S.op("pool", lambda e: e.tensor_tensor(out=RQ[h][:], in0=Rsb[h][:], in1=NMQ, op=ALU.add),
                 [b_Rsb[h], b_const], [b_RQ[h]])
            S.op("pe", lambda e: e.matmul(X2[h][:, 0:128], lhsT=kT, rhs=kT, start=True, stop=True), r_kT, [b_X2[h]])
            S.op("pe", lambda e: e.matmul(X2[h][:, 128:256], lhsT=kT, rhs=qT, start=True, stop=True),
                 r_kT + r_qT, [b_X2[h]])
            yield
            S.op("act", lambda e: e.activation(out=DA[h][:], in_=RA[h][:], func=AF.Exp, scale=-1.0, bias=gc2[h][:, 0:1]),
                 [b_RA[h], b_gc2[h]], [b_DA[h]])
            S.op("act", lambda e: e.activation(out=DQ[h][:], in_=RQ[h][:], func=AF.Exp, scale=1.0, bias=gc2[h][:, 1:2]),
                 [b_RQ[h], b_gc2[h]], [b_DQ[h]])
            yield
            Abf = AB[h][:, 0, :]
            Bbf = AB[h][:, 1, :]
            S.op("dve", lambda e: e.scalar_tensor_tensor(out=Abf, in0=X2[h][:, 0:128], scalar=beta,
                                                         in1=DA[h][:], op0=ALU.mult, op1=ALU.mult),
                 [b_X2[h], bg, b_DA[h]], [b_AB[h]])
            S.op("dve", lambda e: e.tensor_tensor(out=QKT[h][:], in0=X2[h][:, 128:256], in1=DQ[h][:], op=ALU.mult),
                 [b_X2[h], b_DQ[h]], [b_QKT[h]])
            S.op("pe", lambda e: e.transpose(X3b[h][:], Abf, identb[:]), [b_AB[h], b_const], [b_X3b[h]])
            yield
            S.op("act", lambda e: e.activation(out=Bbf, in_=X3b[h][:], func=AF.Copy), [b_X3b[h]], [b_AB[h]])
            S.op("pe", lambda e: e.transpose(X4[h][:, 0:128], kT, identb[:]), r_kT + [b_const], [b_X4[h]])
            S.op("pe", lambda e: e.transpose(X4[h][:, 128:256], vT, identb[:]), r_vT + [b_const], [b_X4[h]])
            yield
            yz0 = YZ[h][0]
            pq0 = PQ[h][0]
            S.op("pool", lambda e: e.tensor_tensor(out=yz0[:, 1, :], in0=Abf, in1=cmt[:, 0, :], op=ALU.mult),
                 [b_AB[h], b_const], [b_YZ[h][0]])
            S.op("pool", lambda e: e.tensor_tensor(out=yz0[:, 0, :], in0=Bbf, in1=cmt[:, 1, :], op=ALU.mult),
                 [b_AB[h], b_const], [b_YZ[h][0]])
            S.op("pool", lambda e: e.tensor_tensor(out=pq0[:, 0, :], in0=identb[:], in1=yz0[:, 0, :], op=ALU.subtract),
                 [b_YZ[h][0], b_const], [b_PQ[h][0]])
            S.op("pool", lambda e: e.tensor_tensor(out=pq0[:, 1, :], in0=identb[:], in1=yz0[:, 1, :], op=ALU.subtract),
                 [b_YZ[h][0], b_const], [b_PQ[h][0]])
            S.op("act", lambda e: e.activation(out=kbg[h][:], in_=X4[h][:, 0:128], func=AF.Copy, scale=gc2[h][:, 3:4]),
                 [b_X4[h], b_gc2[h]], [b_kbg[h]])
            S.op("dve", lambda e: e.tensor_scalar(out=kdec[h][:], in0=X4[h][:, 0:128], scalar1=DQ[h][:, 127:128],
                                                  scalar2=None, op0=ALU.mult), [b_X4[h], b_DQ[h]], [b_kdec[h]])
            S.op("act", lambda e: e.activation(out=vb[h][:], in_=X4[h][:, 128:256], func=AF.Copy, scale=beta),
                 [b_X4[h], bg], [b_vb[h]])
            S.op("pool", lambda e: e.tensor_tensor(out=qdT[h][:], in0=qT, in1=egb[h][:], op=ALU.mult),
                 r_qT + [b_egb[h]], [b_qdT[h]])
            yield
            yz = 0
            pm = 0
            for lev in range(2):
                cur = YZ[h][yz]
                S.op("pe", lambda e, cur=cur: e.matmul(X1[h][:, 0:128], lhsT=cur[:, 1, :], rhs=cur[:, 0, :],
                                                       start=True, stop=True), [b_YZ[h][yz]], [b_X1[h]])
                S.op("pe", lambda e, cur=cur: e.matmul(X1[h][:, 128:256], lhsT=cur[:, 0, :], rhs=cur[:, 1, :],
                                                       start=True, stop=True), [b_YZ[h][yz]], [b_X1[h]])
                yield
                nyz = 1 - yz
                nxt = YZ[h][nyz]
                S.op("act", lambda e, nxt=nxt: e.activation(out=nxt[:, 0, :], in_=X1[h][:, 0:128], func=AF.Copy),
                     [b_X1[h]], [b_YZ[h][nyz]])
                S.op("dve", lambda e, nxt=nxt: e.tensor_copy(out=nxt[:, 1, :], in_=X1[h][:, 128:256]),
                     [b_X1[h]], [b_YZ[h][nyz]])
                yz = nyz
                yield
                pq = PQ[h][pm]
                S.op("pe", lambda e, nxt=nxt, pq=pq: e.matmul(X3[h][:], lhsT=nxt[:, 1, :], rhs=pq[:, 0, :],
                                                              start=True, stop=True),
                     [b_YZ[h][yz], b_PQ[h][pm]], [b_X3[h]])
                S.op("pe", lambda e, nxt=nxt, pq=pq: e.matmul(X2[h][:, 0:128], lhsT=nxt[:, 0, :], rhs=pq[:, 1, :],
                                                              start=True, stop=True),
                     [b_YZ[h][yz], b_PQ[h][pm]], [b_X2[h]])
                yield
                npm = 1 - pm
                npq = PQ[h][npm]
                S.op("dve", lambda e, pq=pq, npq=npq: e.tensor_tensor(out=npq[:, 0, :], in0=pq[:, 0, :], in1=X3[h][:],
                                                                      op=ALU.add),
                     [b_PQ[h][pm], b_X3[h]], [b_PQ[h][npm]])
                S.op("dve", lambda e, pq=pq, npq=npq: e.tensor_tensor(out=npq[:, 1, :], in0=pq[:, 1, :],
                                                                      in1=X2[h][:, 0:128], op=ALU.add),
                     [b_PQ[h][pm], b_X2[h]], [b_PQ[h][npm]])
                pm = npm
                yield
            for m in range(4):
                lastm = m == 3
                pq = PQ[h][pm]
                S.op("pe", lambda e, pq=pq: e.matmul(X1[h][:, 0:128], lhsT=Abf, rhs=pq[:, 0, :], start=True, stop=True),
                     [b_AB[h], b_PQ[h][pm]], [b_X1[h]])
                if not lastm:
                    S.op("pe", lambda e, pq=pq: e.matmul(X1[h][:, 128:256], lhsT=Bbf, rhs=pq[:, 1, :],
                                                         start=True, stop=True), [b_AB[h], b_PQ[h][pm]], [b_X1[h]])
                yield
                S.op("dve", lambda e, m=m: e.tensor_tensor(out=Wm[h][:, 0, :], in0=X1[h][:, 0:128],
                                                           in1=cmt[:, 3 + 2 * m, :], op=ALU.mult),
                     [b_X1[h], b_const], [b_Wm[h]])
                if not lastm:
                    S.op("dve", lambda e, m=m: e.tensor_tensor(out=Wm[h][:, 1, :], in0=X1[h][:, 128:256],
                                                               in1=cmt[:, 2 + 2 * m, :], op=ALU.mult),
                         [b_X1[h], b_const], [b_Wm[h]])
                yield
                S.op("pe", lambda e, pq=pq: e.matmul(X3[h][:], lhsT=pq[:, 1, :], rhs=Wm[h][:, 0, :], start=True, stop=True),
                     [b_PQ[h][pm], b_Wm[h]], [b_X3[h]])
                if not lastm:
                    S.op("pe", lambda e, pq=pq: e.matmul(X2[h][:, 0:128], lhsT=pq[:, 0, :], rhs=Wm[h][:, 1, :],
                                                         start=True, stop=True), [b_PQ[h][pm], b_Wm[h]], [b_X2[h]])
                yield
                npm = 1 - pm
                npq = PQ[h][npm]
                S.op("dve", lambda e, pq=pq, npq=npq: e.tensor_tensor(out=npq[:, 0, :], in0=pq[:, 0, :], in1=X3[h][:],
                                                                      op=ALU.subtract),
                     [b_PQ[h][pm], b_X3[h]], [b_PQ[h][npm]])
                if not lastm:
                    S.op("dve", lambda e, pq=pq, npq=npq: e.tensor_tensor(out=npq[:, 1, :], in0=pq[:, 1, :],
                                                                          in1=X2[h][:, 0:128], op=ALU.subtract),
                         [b_PQ[h][pm], b_X2[h]], [b_PQ[h][npm]])
                pm = npm
                yield
            TT = PQ[h][pm][:, 0, :]
            bTT = b_PQ[h][pm]
            S.op("pe", lambda e: e.matmul(X2[h][:, 0:128], lhsT=TT, rhs=vb[h][:], start=True, stop=True),
                 [bTT, b_vb[h]], [b_X2[h]])
            S.op("pe", lambda e: e.matmul(X2[h][:, 128:256], lhsT=kbg[h][:], rhs=TT, start=True, stop=True),
                 [bTT, b_kbg[h]], [b_X2[h]])
            yield
            S.op("act", lambda e: e.activation(out=Usb[h][:], in_=X2[h][:, 0:128], func=AF.Copy), [b_X2[h]], [b_Usb[h]])
            S.op("dve", lambda e: e.tensor_copy(out=wT[h][:], in_=X2[h][:, 128:256]), [b_X2[h]], [b_wT[h]])
            yield
            S.op("pe", lambda e: e.matmul(X5[h][:, 0:128], lhsT=wT[h][:], rhs=Sb[h][:], start=True, stop=True),
                 [b_wT[h], b_Sb[h]], [b_X5a[h]])
            yield
            S.op("dve", lambda e: e.tensor_tensor(out=vnew[h][:], in0=Usb[h][:], in1=X5[h][:, 0:128], op=ALU.subtract),
                 [b_Usb[h], b_X5a[h]], [b_vnew[h]])
            S.op("pe", lambda e: e.matmul(X5[h][:, 128:256], lhsT=qdT[h][:], rhs=Sb[h][:], start=True, stop=False),
                 [b_qdT[h], b_Sb[h]], [b_X5b[h]])
            yield
            S.op("pe", lambda e: e.matmul(X5[h][:, 128:256], lhsT=QKT[h][:], rhs=vnew[h][:], start=False, stop=True),
                 [b_QKT[h], b_vnew[h]], [b_X5b[h]])
            S.op("pe", lambda e: e.matmul(X5[h][:, 0:128], lhsT=kdec[h][:], rhs=vnew[h][:], start=True, stop=True),
                 [b_kdec[h], b_vnew[h]], [b_X5a[h]])
            yield
            S.op("dve", lambda e: e.scalar_tensor_tensor(out=Sf[h][:], in0=Sf[h][:], scalar=egb[h][:, 127:128],
                                                         in1=X5[h][:, 0:128], op0=ALU.mult, op1=ALU.add),
                 [b_Sf[h], b_egb[h], b_X5a[h]], [b_Sf[h]])
            S.op("act", lambda e: e.activation(out=junk[h][:], in_=X5[h][:, 128:256], func=AF.Square,
                                               accum_out=ost[h][:, 0:1]), [b_X5b[h]], [b_junk[h], b_ost[h]])
            yield
            S.op("act", lambda e: e.activation(out=Sb[h][:], in_=Sf[h][:], func=AF.Copy), [b_Sf[h]], [b_Sb[h]])
            S.op("act", lambda e: e.activation(out=ost[h][:, 1:2], in_=ost[h][:, 0:1], func=AF.Ln, scale=1.0 / 128,
                                               bias=misc[:, 0:1]), [b_ost[h], b_const], [b_ost[h]])
            S.op("act", lambda e: e.activation(out=ost[h][:, 2:3], in_=ost[h][:, 1:2], func=AF.Exp, scale=-0.5),
                 [b_ost[h]], [b_ost[h]])
            yield
            S.op("dve", lambda e: e.scalar_tensor_tensor(out=ogt[h][:], in0=X5[h][:, 128:256], scalar=ost[h][:, 2:3],
                                                         in1=gz[cg % 2][:, h * 128:(h + 1) * 128], op0=ALU.mult,
                                                         op1=ALU.mult),
                 [b_X5b[h], b_ost[h], b_gz[cg % 2]], [b_ogt[h]])
            S.op("pe", lambda e: e.transpose(XT[h][:], ogt[h][:], identb[:]), [b_ogt[h], b_const], [b_XT[h]])
            yield
            sg = (cg // 4) % 2
            sp_ = cg % 4
            S.op("act", lambda e: e.activation(out=ostg[h][sg][:, sp_ * 128:(sp_ + 1) * 128], in_=XT[h][:], func=AF.Copy),
                 [b_XT[h]], [b_ostg[h][sg]])
            if sp_ == 3:
                tb = (cg // 4) * 512
                fin.append(S.dma("sp", lambda e: e.dma_start(out=og[h, :, tb:tb + 512], in_=ostg[h][sg][:]),
                                 [b_ostg[h][sg]], []))

        import os
        lvl = int(os.environ.get("GDN_S1", "9"))

        def do_tile(t):
            if lvl < 2:
                return
            ti = t % 2
            rank = (t * TW) // 2048
            toff = (t * TW) % 2048
            S.dma("sp" if t % 2 == 0 else "pool", lambda e, ti=ti, rank=rank, toff=toff: e.dma_start(
                out=xt[ti][:], in_=xh[rank, :, :, toff:toff + TW].rearrange("c p t -> p c t")), [], [b_xt[ti]])
            def grp(g):
                pi = g % 2
                for dc in range(8):
                    S.op("pe", lambda e, g=g, dc=dc, pi=pi: e.matmul(pj[pi][:], lhsT=wq[:, dc, g * 128:(g + 1) * 128],
                                                                      rhs=xt[ti][:, dc, :], start=(dc == 0), stop=(dc == 7)),
                         [b_w, b_xt[ti]], [b_pj[pi]])
                if t > 0 and lvl >= 3:
                    S.op("pool", lambda e, g=g: e.tensor_copy(out=Pc[g][ti][:, 0:3], in_=Pc[g][1 - ti][:, TW:TW + 3]),
                         [b_Pc[g][1 - ti]], [b_Pc[g][ti]])
                S.op("act", lambda e, g=g, pi=pi: e.activation(out=Pc[g][ti][:, 3:TW + 3], in_=pj[pi][:], func=AF.Copy),
                     [b_pj[pi]], [b_Pc[g][ti]])
                if lvl < 3:
                    return
                for j in range(4):
                    S.op("pe", lambda e, g=g, j=j, pi=pi: e.matmul(pc[pi][:], lhsT=dg[:, g * 4 + j, :],
                                                                    rhs=Pc[g][ti][:, j:j + TW], start=(j == 0), stop=(j == 3)),
                         [b_const, b_Pc[g][ti]], [b_pc[pi]])
                S.op("act", lambda e, g=g, pi=pi: e.activation(out=s1[g][ti][:], in_=pc[pi][:], func=AF.Silu),
                     [b_pc[pi]], [b_s1[g][ti]])
            for g in range(6):
                grp(g)

            def zab(ci):
                if lvl < 4:
                    return
                cg = t * 2 + ci
                gi = cg % 2
                for dc in range(8):
                    S.op("pe", lambda e, dc=dc, ci=ci: e.matmul(pz[:], lhsT=xt[ti][:, dc, ci * 128:(ci + 1) * 128],
                                                                rhs=wz[:, dc, :], start=(dc == 0), stop=(dc == 7)),
                         [b_w, b_xt[ti]], [b_pz])
                S.op("act", lambda e: e.activation(out=sz[:], in_=pz[:, 0:256], func=AF.Silu), [b_pz], [b_sz])
                S.op("dve", lambda e, gi=gi: e.tensor_tensor(out=gt[gi][:, 0:2], in0=pz[:, 256:258], in1=hpt[:, 2:4],
                                                             op=ALU.add), [b_pz, b_const], [b_gt[gi]])
                S.op("dve", lambda e, gi=gi: e.tensor_copy(out=gt[gi][:, 8:10], in_=pz[:, 258:260]), [b_pz], [b_gt[gi]])
                S.op("pool", lambda e, gi=gi: e.tensor_tensor(out=gz[gi][:], in0=sz[:], in1=ongt[:], op=ALU.mult),
                     [b_sz, b_const], [b_gz[gi]])
                S.op("act", lambda e, gi=gi: e.activation(out=gt[gi][:, 2:4], in_=gt[gi][:, 0:2], func=AF.Exp),
                     [b_gt[gi]], [b_gt[gi]])
                S.op("act", lambda e, gi=gi: e.activation(out=gt[gi][:, 4:6], in_=gt[gi][:, 2:4], func=AF.Ln,
                                                          bias=misc[:, 2:3]), [b_gt[gi], b_const], [b_gt[gi]])
                S.op("act", lambda e, gi=gi: e.activation(out=gt[gi][:, 10:12], in_=gt[gi][:, 8:10], func=AF.Exp,
                                                          scale=-1.0), [b_gt[gi]], [b_gt[gi]])
                S.op("dve", lambda e, gi=gi: e.tensor_tensor(out=gt[gi][:, 6:8], in0=gt[gi][:, 4:6], in1=misc[:, 3:5],
                                                             op=ALU.mult), [b_gt[gi], b_const], [b_gt[gi]])
                S.op("dve", lambda e, gi=gi: e.tensor_scalar(out=gt[gi][:, 10:12], in0=gt[gi][:, 10:12], scalar1=1.0,
                                                             scalar2=None, op0=ALU.add), [b_gt[gi]], [b_gt[gi]])
                S.op("dve", lambda e, gi=gi: e.reciprocal(out=gt[gi][:, 12:14], in_=gt[gi][:, 10:12]),
                     [b_gt[gi]], [b_gt[gi]])
            for ci in range(2):
                zab(ci)

            def l2n(g):
                if lvl < 5:
                    return
                pi = g % 2
                isq = g < 2
                S.op("act", lambda e, g=g, pi=pi, isq=isq: e.activation(
                    out=sqb[pi][:], in_=s1[g][ti][:], func=AF.Square, scale=(float(np.sqrt(128.0)) if isq else 1.0)),
                    [b_s1[g][ti]], [b_sqb[pi]])
                S.op("pe", lambda e, pi=pi: e.matmul(pc[pi][:], lhsT=onesb[:], rhs=sqb[pi][:], start=True, stop=True),
                     [b_const, b_sqb[pi]], [b_pc[pi]])
                S.op("act", lambda e, pi=pi, isq=isq: e.activation(out=lnb[pi][:], in_=pc[pi][:], func=AF.Ln,
                                                                   bias=(misc[:, 1:2] if isq else misc[:, 0:1])),
                     [b_pc[pi], b_const], [b_lnb[pi]])
                S.op("act", lambda e, pi=pi: e.activation(out=lnb[pi][:], in_=lnb[pi][:], func=AF.Exp, scale=-0.5),
                     [b_lnb[pi]], [b_lnb[pi]])
                S.op("dve", lambda e, g=g, pi=pi: e.tensor_tensor(out=qn[g][ti][:], in0=s1[g][ti][:], in1=lnb[pi][:],
                                                                  op=ALU.mult), [b_s1[g][ti], b_lnb[pi]], [b_qn[g][ti]])
            for g in range(4):
                l2n(g)
            for ci in range(2):
                cg = t * 2 + ci
                import os, itertools
                dbg = int(os.environ.get("GDN_DBG", "99"))
                if dbg >= 99:
                    roundrobin([chunk_head(0, ti, ci, cg), chunk_head(1, ti, ci, cg)])
                elif dbg >= 2:
                    roundrobin([itertools.islice(chunk_head(0, ti, ci, cg), dbg - 2),
                                itertools.islice(chunk_head(1, ti, ci, cg), dbg - 2)])
        for t in range(ntiles):
            do_tile(t)
        S.finalize(fin)
    return nc


import ml_dtypes
NPBF = ml_dtypes.bfloat16


def _consts():
    i = np.arange(128)
    U = (i[:, None] <= i[None, :]).astype(np.float32)
    NMA = np.where(i[None, :] >= i[:, None], 30000.0, 0.0).astype(np.float32)
    NMQ = np.where(i[None, :] < i[:, None], -30000.0, 0.0).astype(np.float32)
    I = np.eye(128, dtype=np.float32)
    cf = np.ascontiguousarray(np.stack([U, NMA, NMQ, I], axis=1))
    ms = []
    bi = i[:, None]
    bj = i[None, :]
    m8 = ((bi // 8) == (bj // 8)).astype(np.float32)
    ms += [m8, m8.T]
    for m in range(4):
        sz = 8 << m
        ml = (((bi // sz) == (bj // sz) + 1) & ((bi // (2 * sz)) == (bj // (2 * sz)))).astype(np.float32)
        ms += [ml, ml.T]
    cm = np.ascontiguousarray(np.stack(ms, axis=1)).astype(NPBF)
    return {"ident": I.astype(NPBF), "cf": cf, "cm": cm}


def gdn_inputs(inp, layer, r):
    h0, h1 = 2 * r, 2 * r + 1
    w = inp["a_w_in"][layer]
    cols = []
    for base in (0, 1024, 2048):
        for h in (h0, h1):
            cols.append(np.arange(base + h * 128, base + (h + 1) * 128))
    qkv_cols = np.concatenate(cols)
    zcols = np.concatenate([np.arange(3072 + h * 128, 3072 + (h + 1) * 128) for h in (h0, h1)])
    ab_cols = np.array([4096 + h0, 4096 + h1, 4104 + h0, 4104 + h1])
    wqkv = np.ascontiguousarray(w[:, qkv_cols])
    wzab = np.ascontiguousarray(w[:, np.concatenate([zcols, ab_cols])])
    gn = np.ascontiguousarray(inp["a_norm_g"][layer].reshape(8, 128).T)
    cwf = inp["a_conv_w"][layer][:, qkv_cols]
    cw = np.ascontiguousarray(cwf.reshape(4, 6, 128).transpose(2, 1, 0).reshape(128, 24))
    hp = np.broadcast_to(np.array([inp["a_log"][layer][h0], inp["a_log"][layer][h1],
                                   inp["a_dt_bias"][layer][h0], inp["a_dt_bias"][layer][h1]], np.float32), (128, 4))
    ong = np.broadcast_to(np.tile(inp["a_out_norm_g"][layer], 2), (128, 256))
    d = {"wqkv": wqkv, "wzab": wzab, "gn": gn, "cw": cw, "hp": np.ascontiguousarray(hp),
         "ong": np.ascontiguousarray(ong)}
    d.update(_consts())
    return d


TZL = 2432


def build_moba(ngroups=16):
    nc = bass.Bass("TRN2", target_bir_lowering=False)
    xh = nc.dram_tensor("xh", [4, 8, 128, 2048], BF16, kind="ExternalInput").ap()
    xh2 = nc.dram_tensor("xh2", [4, 8, 128, 2048], BF16, kind="ExternalInput").ap()
    wqz = nc.dram_tensor("wqz", [1024, 512], F32, kind="ExternalInput").ap()
    wkv = nc.dram_tensor("wkv", [1024, 512], F32, kind="ExternalInput").ap()
    gn = nc.dram_tensor("gn", [128, 16], F32, kind="ExternalInput").ap()
    tz = nc.dram_tensor("tz", [2, 128, TZL], F32, kind="ExternalInput").ap()
    b31 = nc.dram_tensor("b31", [128, 2], F32, kind="ExternalInput").ap()
    lsel_d = nc.dram_tensor("lsel", [32, 32, 128], BF16, kind="ExternalInput").ap()
    ident_d = nc.dram_tensor("ident", [128, 128], BF16, kind="ExternalInput").ap()
    og = nc.dram_tensor("og", [2, 128, 8192], BF16, kind="ExternalOutput").ap()
    SCALE = float(128 ** -0.5)
    BIG = 30000.0
    with ExitStack() as es:
        C = Ctx(nc, es)
        S = C.S
        fin = []
        b_const = Buf()
        identb = C.sb("identb", [128, 128], BF16)
        gnt = C.sb("gnt", [128, 16], F32)
        b31t = C.sb("b31t", [128, 4], F32)
        lsel = C.sb("lselt", [32, 32, 128], BF16)
        S.dma("sp", lambda e: e.dma_start(out=identb[:], in_=ident_d[:, :]), [], [b_const])
        S.dma("sp", lambda e: e.dma_start(out=gnt[:], in_=gn[:, :]), [], [b_const])
        S.dma("sp", lambda e: e.dma_start(out=b31t[:, 0:2], in_=b31[:, :]), [], [b_const])
        S.dma("sp", lambda e: e.dma_start(out=lsel[:], in_=lsel_d[:, :, :]), [], [b_const])
        S.op("pool", lambda e: e.memset(b31t[:, 2:3], 0.0), [], [b_const])
        ebT = [C.sb("ebT%d" % h, [128, TZL], BF16) for h in range(2)]
        tzs = C.sb("tzs", [128, TZL], F32)
        b_tzs = Buf()
        for h in range(2):
            S.dma("sp", lambda e, h=h: e.dma_start(out=tzs[:], in_=tz[h, :, :]), [], [b_tzs])
            S.op("act", lambda e, h=h: e.activation(out=ebT[h][:], in_=tzs[:], func=AF.Exp), [b_tzs], [b_const])
        wq = C.sb("wq", [128, 8, 512], BF16)
        wk = C.sb("wk", [128, 8, 512], BF16)
        wst = [C.sb("wst%d" % i, [128, 1024], F32) for i in range(2)]
        b_wst = [Buf(), Buf()]
        b_w = Buf()
        for dc in range(8):
            i = dc % 2
            S.dma("sp", lambda e, i=i, dc=dc: e.dma_start(out=wst[i][:, 0:512], in_=wqz[dc * 128:(dc + 1) * 128, :]),
                  [], [b_wst[i]])
            S.dma("pool", lambda e, i=i, dc=dc: e.dma_start(out=wst[i][:, 512:1024], in_=wkv[dc * 128:(dc + 1) * 128, :]),
                  [], [b_wst[i]])
            S.op("dve", lambda e, i=i, dc=dc: e.tensor_scalar(out=wq[:, dc, :], in0=wst[i][:, 0:512],
                                                              scalar1=gnt[:, dc:dc + 1], scalar2=None, op0=ALU.mult),
                 [b_wst[i], b_const], [b_w])
            S.op("dve", lambda e, i=i, dc=dc: e.tensor_scalar(out=wk[:, dc, :], in0=wst[i][:, 512:1024],
                                                              scalar1=gnt[:, 8 + dc:9 + dc], scalar2=None, op0=ALU.mult),
                 [b_wst[i], b_const], [b_w])
        banks = [C.ps("bank%d" % i, [128, 512], F32) for i in range(8)]
        bankb = [Buf("bank%d" % i, excl=True) for i in range(8)]
        KT = C.sb("KT", [128, 2, 8192], BF16)
        b_KT = [Buf() for _ in range(16)]
        Vaug = C.sb("Vaug", [128, 2, 64, 130], BF16)
        b_V = [Buf() for _ in range(16)]
        kmT = C.sb("kmT", [128, 2, 32], F32)
        b_km = Buf()
        S.op("pool", lambda e: e.memset(Vaug[:, :, :, 128:130], 1.0), [], [b_V[0]])
        xt = [C.sb("xt%d" % i, [128, 8, 512], BF16) for i in range(2)]
        b_xt = [Buf(), Buf()]

        def load_x(src, T, i):
            rank = (T * 512) // 2048
            toff = (T * 512) % 2048
            S.dma("sp" if T % 2 == 0 else "pool", lambda e: e.dma_start(
                out=xt[i][:], in_=src[rank, :, :, toff:toff + 512].rearrange("c p t -> p c t")), [], [b_xt[i]])

        def phase_a(T):
            i = T % 2
            load_x(xh2, T, i)

            def kproj(h):
                for dc in range(8):
                    S.op("pe", lambda e, dc=dc: e.matmul(banks[4][:], lhsT=wk[:, dc, h * 128:(h + 1) * 128],
                                                         rhs=xt[i][:, dc, :], start=(dc == 0), stop=(dc == 7)),
                         [b_w, b_xt[i]], [bankb[4]])
                S.op("act", lambda e: e.activation(out=KT[:, h, T * 512:(T + 1) * 512], in_=banks[4][:], func=AF.Copy),
                     [bankb[4]], [b_KT[T]])
                S.op("dve", lambda e: e.tensor_reduce(out=kmT[:, h, 2 * T:2 * T + 2],
                                                      in_=banks[4][:].rearrange("p (b t) -> p b t", b=2),
                                                      axis=AX.X, op=ALU.add), [bankb[4]], [b_km])

            def vproj(s):
                for dc in range(8):
                    S.op("pe", lambda e, dc=dc: e.matmul(banks[5][:, 0:256], lhsT=xt[i][:, dc, s * 128:(s + 1) * 128],
                                                         rhs=wk[:, dc, 256:512], start=(dc == 0), stop=(dc == 7)),
                         [b_w, b_xt[i]], [bankb[5]])
                S.op("dve", lambda e: e.tensor_copy(out=Vaug[:, :, 4 * T + s, 0:128],
                                                    in_=banks[5][:, 0:256].rearrange("p (h d) -> p h d", h=2)),
                     [bankb[5]], [b_V[T]])
            for h in range(2):
                kproj(h)
            for s in range(4):
                vproj(s)
        for T in range(16):
            phase_a(T)

        q_bf = [C.sb("q_bf%d" % h, [128, 512], BF16) for h in range(2)]
        q_f = [C.sb("q_f%d" % h, [128, 512], F32) for h in range(2)]
        b_q = [Buf(), Buf()]
        gzt = C.sb("gzt", [128, 4, 256], F32)
        b_gz = Buf()
        MT = [C.sb("MT%d" % h, [32, 512], BF16) for h in range(2)]
        b_MT = [Buf(), Buf()]
        gsb = [C.sb("gsb%d" % i, [128, 32], F32) for i in range(2)]
        m8 = [C.sb("m8_%d" % i, [128, 8], F32) for i in range(2)]
        mbf = [C.sb("mbf%d" % i, [128, 32], F32) for i in range(2)]
        mbb = [C.sb("mbb%d" % i, [128, 32], BF16) for i in range(2)]
        b_gs = [Buf(), Buf()]
        pT = [C.sb("pT%d" % i, [128, 512], BF16) for i in range(2)]
        b_pT = [Buf(), Buf()]
        rinv = C.sb("rinv", [128, 4], F32)
        b_rinv = Buf()
        ogt = C.sb("ogt", [128, 4, 128], BF16)
        b_ogt = Buf()
        ostg = [C.sb("ostg%d" % i, [128, 512], BF16) for i in range(2)]
        b_ostg = [Buf(), Buf()]
        mtp = banks[6][:, 64:128].bitcast(BF16)
        otp = banks[7][:, 0:256].bitcast(BF16)

        def oacc(s):
            return banks[2 + s // 2][:, (s % 2) * 130:(s % 2) * 130 + 129]

        def group(G):
            i = G % 2
            load_x(xh, G, i)

            def qproj(h):
                for dc in range(8):
                    S.op("pe", lambda e, dc=dc: e.matmul(banks[4][:], lhsT=wq[:, dc, h * 128:(h + 1) * 128],
                                                         rhs=xt[i][:, dc, :], start=(dc == 0), stop=(dc == 7)),
                         [b_w, b_xt[i]], [bankb[4]])
                S.op("act", lambda e: e.activation(out=q_bf[h][:], in_=banks[4][:], func=AF.Copy), [bankb[4]], [b_q[h]])
                S.op("dve", lambda e: e.tensor_copy(out=q_f[h][:], in_=banks[4][:]), [bankb[4]], [b_q[h]])

            def zproj(s):
                for dc in range(8):
                    S.op("pe", lambda e, dc=dc: e.matmul(banks[5][:, 0:256], lhsT=xt[i][:, dc, s * 128:(s + 1) * 128],
                                                         rhs=wq[:, dc, 256:512], start=(dc == 0), stop=(dc == 7)),
                         [b_w, b_xt[i]], [bankb[5]])
                S.op("act", lambda e: e.activation(out=gzt[:, s, :], in_=banks[5][:, 0:256], func=AF.Silu),
                     [bankb[5]], [b_gz])

            def gate(h, s, k):
                cur = 2 * G + s // 2
                S.op("pool", lambda e: e.memset(gsb[k][:], -3.0e38), [], [b_gs[k]])
                if cur > 0:
                    S.op("pe", lambda e: e.matmul(banks[6][:, 0:32], lhsT=q_f[h][:, s * 128:(s + 1) * 128],
                                                  rhs=kmT[:, h, :], start=True, stop=True), [b_q[h], b_km], [bankb[6]])
                    S.op("dve", lambda e: e.tensor_copy(out=gsb[k][:, 0:cur], in_=banks[6][:, 0:cur]),
                         [bankb[6]], [b_gs[k]])
                S.op("dve", lambda e: e.max(out=m8[k][:], in_=gsb[k][:]), [b_gs[k]], [b_gs[k]])
                S.op("dve", lambda e: e.tensor_scalar(out=mbf[k][:], in0=gsb[k][:], scalar1=m8[k][:, 2:3], scalar2=None,
                                                      op0=ALU.is_ge), [b_gs[k]], [b_gs[k]])
                S.op("dve", lambda e: e.tensor_scalar(out=mbb[k][:], in0=mbf[k][:], scalar1=-1.0, scalar2=BIG,
                                                      op0=ALU.add, op1=ALU.mult), [b_gs[k]], [b_gs[k]])
                S.op("pool", lambda e: e.memset(mbb[k][:, cur:cur + 1], 0.0), [b_gs[k]], [b_gs[k]])
                if cur < 31:
                    S.op("pool", lambda e: e.memset(mbb[k][:, cur + 1:32], -BIG), [b_gs[k]], [b_gs[k]])
                S.op("pe", lambda e: e.transpose(mtp[0:32, :], mbb[k][:], identb[:]), [b_gs[k], b_const], [bankb[6]])
                S.op("act", lambda e: e.activation(out=MT[h][:, s * 128:(s + 1) * 128], in_=mtp[0:32, :], func=AF.Copy),
                     [bankb[6]], [b_MT[h]])

            def attend(h):
                NJ = 4 * G + 4

                def qk(j):
                    n = j // 2
                    S.op("pe", lambda e: e.matmul(banks[j % 2][:], lhsT=KT[:, h, j * 128:(j + 1) * 128], rhs=q_bf[h][:],
                                                  start=True, stop=False), [b_KT[j // 4], b_q[h]], [bankb[j % 2]])
                    S.op("pe", lambda e: e.matmul(banks[j % 2][:], lhsT=lsel[:, n, :], rhs=MT[h][:],
                                                  start=False, stop=True), [b_const, b_MT[h]], [bankb[j % 2]])

                def ex(j):
                    d0 = 512 * G - 128 * j
                    far = d0 >= 1664
                    bias = b31t[:, h:h + 1] if far else b31t[:, 2:3]
                    S.op("act", lambda e: e.activation(out=pT[j % 2][:], in_=banks[j % 2][:], func=AF.Exp, scale=SCALE,
                                                       bias=bias), [bankb[j % 2], b_const], [b_pT[j % 2]])
                    if not far:
                        off = d0 + 384
                        S.op("pool", lambda e: e.tensor_tensor(out=pT[j % 2][:], in0=pT[j % 2][:],
                                                               in1=ebT[h][:, off:off + 512], op=ALU.mult),
                             [b_pT[j % 2], b_const], [b_pT[j % 2]])

                def pv(j):
                    for s in range(4):
                        S.op("pe", lambda e, s=s: e.matmul(oacc(s), lhsT=pT[j % 2][:, s * 128:(s + 1) * 128],
                                                           rhs=Vaug[:, h, j, 0:129], start=(j == 0 and s % 2 == 0),
                                                           stop=(j == NJ - 1), skip_group_check=True),
                             [b_pT[j % 2], b_V[j // 4]], [bankb[2 + s // 2]])
                qk(0)
                for j in range(NJ):
                    if j + 1 < NJ:
                        qk(j + 1)
                    ex(j)
                    pv(j)

                def fin_s(s):
                    S.op("dve", lambda e: e.reciprocal(out=rinv[:, s:s + 1], in_=oacc(s)[:, 128:129]),
                         [bankb[2 + s // 2]], [b_rinv])
                    S.op("dve", lambda e: e.scalar_tensor_tensor(out=ogt[:, s, :], in0=oacc(s)[:, 0:128],
                                                                 scalar=rinv[:, s:s + 1],
                                                                 in1=gzt[:, s, h * 128:(h + 1) * 128],
                                                                 op0=ALU.mult, op1=ALU.mult),
                         [bankb[2 + s // 2], b_rinv, b_gz], [b_ogt])
                    S.op("pe", lambda e: e.transpose(otp[:, s * 128:(s + 1) * 128], ogt[:, s, :], identb[:]),
                         [b_ogt, b_const], [bankb[7]])
                for s in range(4):
                    fin_s(s)
                k = (2 * G + h) % 2
                S.op("act", lambda e: e.activation(out=ostg[k][:], in_=otp[:, 0:512], func=AF.Copy),
                     [bankb[7]], [b_ostg[k]])
                fin.append(S.dma("sp", lambda e: e.dma_start(out=og[h, :, G * 512:(G + 1) * 512], in_=ostg[k][:]),
                                 [b_ostg[k]], []))

            for h in range(2):
                qproj(h)
            for s in range(4):
                zproj(s)
            kk = 0
            for h in range(2):
                for s in range(4):
                    gate(h, s, kk % 2)
                    kk += 1
            for h in range(2):
                attend(h)
        for G in range(ngroups):
            group(G)
        S.finalize(fin)
    return nc


def t5_bucket_np(rel):
    import math
    n = np.maximum(rel, 0)
    nf = np.maximum(n, 1).astype(np.float32)
    large = 16 + (np.log(nf / np.float32(16)) / np.float32(math.log(2048 / 16)) * np.float32(16)).astype(np.int32)
    large = np.minimum(large, 31)
    return np.where(n < 16, n, large)


def moba_inputs(inp, j, r):
    h0, h1 = 2 * r, 2 * r + 1
    w = inp["b_w_in"][j]
    qc = np.concatenate([np.arange(h * 128, (h + 1) * 128) for h in (h0, h1)])
    wqz = np.ascontiguousarray(np.concatenate([w[:, qc], w[:, 1024 + qc]], axis=1))
    wkv = np.ascontiguousarray(np.concatenate([inp["w_kv"][:, qc], inp["w_kv"][:, 1024 + qc]], axis=1))
    gn = np.ascontiguousarray(np.concatenate([inp["b_norm_g"][j].reshape(8, 128).T,
                                              inp["kv_norm_g"].reshape(8, 128).T], axis=1))
    p = np.arange(128)[:, None]
    m = np.arange(TZL)[None, :]
    dist = m - p - 384
    idx = np.where(dist < 0, 32, t5_bucket_np(dist))
    tz = []
    for h in (h0, h1):
        ext = np.concatenate([inp["rel_bias"][:, h], np.array([-30000.0], np.float32)])
        tz.append(ext[idx])
    tz = np.ascontiguousarray(np.stack(tz).astype(np.float32))
    b31 = np.ascontiguousarray(np.broadcast_to(inp["rel_bias"][31, [h0, h1]], (128, 2)))
    lsel = np.zeros((32, 32, 128), np.float32)
    for n in range(32):
        lsel[n, n, :] = 1.0
    d = {"wqz": wqz, "wkv": wkv, "gn": gn, "tz": tz, "b31": b31, "lsel": lsel.astype(NPBF),
         "ident": np.eye(128, dtype=np.float32).astype(NPBF)}
    return d


def _run(nc, maps):
    res = run_bass_kernel_spmd(nc, maps, core_ids=list(range(8)))
    return res.results


def kernel(**inp):
    inp = {k: np.asarray(v) for k, v in inp.items()}
    x = inp["x"].astype(np.float32)
    ident = np.eye(128, dtype=np.float32).astype(NPBF)
    cores = [(c // 4, c % 4) for c in range(8)]

    def own(a, b, r):
        return np.ascontiguousarray(a[b, 2048 * r:2048 * (r + 1)])

    def gather_xh(results):
        return [np.ascontiguousarray(np.stack([results[b * 4 + r]["xh"] for r in range(4)])) for b in range(2)]

    def gather_og(results):
        return [np.concatenate([results[b * 4 + r]["og"] for r in range(4)], axis=0) for b in range(2)]

    res = _run(build_tok(False, False), [{"x": own(x, b, r), "ident": ident} for b, r in cores])
    xs = [res[c]["xo"] for c in range(8)]
    XH = gather_xh(res)
    XH2 = None
    y = None
    for layer in range(4):
        if layer < 2:
            maps = []
            for b, r in cores:
                d = gdn_inputs(inp, layer, r)
                d["xh"] = XH[b]
                maps.append(d)
            res = _run(build_gdn(), maps)
            wo = inp["a_w_out"][layer]
        else:
            j = layer - 2
            if XH2 is None:
                XH2 = XH
            maps = []
            for b, r in cores:
                d = moba_inputs(inp, j, r)
                d["xh"] = XH[b]
                d["xh2"] = XH2[b]
                maps.append(d)
            res = _run(build_moba(), maps)
            wo = inp["b_w_out"][j]
        OG = gather_og(res)
        final = layer == 3
        maps = []
        for c, (b, r) in enumerate(cores):
            d = {"x": xs[c], "ident": ident, "wo": np.ascontiguousarray(wo),
                 "og": np.ascontiguousarray(OG[b][:, :, 2048 * r:2048 * (r + 1)])}
            if final:
                d["gf"] = np.ascontiguousarray(inp["final_norm_g"])
            maps.append(d)
        res = _run(build_tok(True, final), maps)
        if final:
            y = np.stack([np.concatenate([res[b * 4 + r]["y"] for r in range(4)], axis=0) for b in range(2)])
        else:
            xs = [res[c]["xo"] for c in range(8)]
            XH = gather_xh(res)
    return y.astype(np.float32)
```

```python
import numpy as np
import concourse.bass as bass
import concourse.mybir as mybir
from concourse.bass_utils import run_bass_kernel_spmd
from contextlib import ExitStack

F32 = mybir.dt.float32
BF16 = mybir.dt.bfloat16
AF = mybir.ActivationFunctionType
ALU = mybir.AluOpType
AX = mybir.AxisListType


class Buf:
    __slots__ = ("name", "last_w", "readers", "excl")

    def __init__(self, name="", excl=False):
        self.name = name
        self.last_w = None
        self.readers = []
        self.excl = excl


class Q:
    def __init__(self, name, is_pe=False):
        self.name = name
        self.is_pe = is_pe
        self.sem = None
        self.cnt = 0
        self.known = {}
        self.prog = []
        self.dsems = []
        self.dcnt = []
        self.dnext = 0


class Sched:
    NDMA = 8

    def __init__(self, nc, es):
        self.nc = nc
        self.es = es
        self.q = {}
        for n in ("pe", "act", "dve", "pool", "sp"):
            q = Q(n, is_pe=(n == "pe"))
            q.sem = es.enter_context(nc.semaphore("s_" + n))
            self.q[n] = q
        for n in ("act", "pool", "sp"):
            q = self.q[n]
            for i in range(self.NDMA):
                q.dsems.append(es.enter_context(nc.semaphore("d_%s%d" % (n, i))))
                q.dcnt.append(0)
        self.nops = 0

    def _waits(self, q, deps):
        waits = []
        for tok, skip_same in deps:
            sem, val, src = tok
            if src is q and (q.is_pe or skip_same):
                continue
            key = id(sem)
            if q.known.get(key, 0) >= val:
                continue
            q.known[key] = val
            waits.append((sem, val))
        return waits

    def _deps(self, reads, writes):
        deps = []
        for b in reads:
            if b.last_w is not None:
                deps.append((b.last_w, b.excl))
        for b in writes:
            if b.last_w is not None:
                deps.append((b.last_w, b.excl))
            deps.extend((r, False) for r in b.readers)
        return deps

    def _commit(self, tok, reads, writes):
        for b in reads:
            if b.excl:
                b.last_w = tok
            else:
                b.readers.append(tok)
        for b in writes:
            b.last_w = tok
            b.readers = []

    def op(self, qn, fn, reads=(), writes=()):
        q = self.q[qn]
        waits = self._waits(q, self._deps(reads, writes))
        q.cnt += 1
        tok = (q.sem, q.cnt, q)
        q.prog.append((waits, fn, (q.sem, 1)))
        self._commit(tok, reads, writes)
        self.nops += 1
        return tok

    def dma(self, qn, fn, reads=(), writes=()):
        q = self.q[qn]
        deps = self._deps(reads, writes)
        k = q.dnext
        q.dnext = (k + 1) % len(q.dsems)
        sem = q.dsems[k]
        if q.dcnt[k] > 0:
            deps.append(((sem, q.dcnt[k], None), False))
        waits = self._waits(q, deps)
        q.dcnt[k] += 16
        tok = (sem, q.dcnt[k], None)
        q.prog.append((waits, fn, (sem, 16)))
        self._commit(tok, reads, writes)
        self.nops += 1
        return tok

    def coll(self, fn, reads=(), writes=()):
        return self.dma("pool", fn, reads, writes)

    def flush(self):
        nc = self.nc
        with nc.Block() as block:
            def replay(qn, eng):
                for waits, fn, inc in self.q[qn].prog:
                    for sem, val in waits:
                        eng.wait_ge(sem, val)
                    if fn is not None:
                        inst = fn(eng)
                        inst.then_inc(inc[0], inc[1])
                self.q[qn].prog = []

            @block.tensor
            def _(e):
                replay("pe", e)

            @block.scalar
            def _(e):
                replay("act", e)

            @block.vector
            def _(e):
                replay("dve", e)

            @block.gpsimd
            def _(e):
                replay("pool", e)

            @block.sync
            def _(e):
                replay("sp", e)

    def finalize(self, final_toks):
        q = self.q["sp"]
        waits = self._waits(q, [(t, False) for t in final_toks])
        q.prog.append((waits, None, None))
        self.flush()


EPS = 1e-6


class Ctx:
    def __init__(self, nc, es, parent=None, tag=""):
        self.nc = nc
        self.es = es
        self.tag = tag
        if parent is None:
            self.S = Sched(nc, es)
            self.psum_all = es.enter_context(nc.psum_tensor("psum_all", [128, 4096], F32))
            self.bankb = [Buf("bank%d" % i, excl=True) for i in range(8)]
        else:
            self.S = parent.S
            self.psum_all = parent.psum_all
            self.bankb = parent.bankb
        self.banks = [self.psum_all[:, 512 * i:512 * (i + 1)] for i in range(8)]

    def sb(self, name, shape, dt):
        return self.es.enter_context(self.nc.sbuf_tensor(self.tag + name, shape, dt))


class Dram:
    def __init__(self, nc, io=None):
        self.nc = nc
        self.io = io
        self.bufs = {} if io is None else io.get("_bufs", {})

    def __call__(self, name, shape, dt, kind):
        if self.io is None:
            return self.nc.dram_tensor(name, shape, dt, kind=kind).ap()
        return self.io[name]

    def buf(self, name):
        if name not in self.bufs:
            self.bufs[name] = Buf(name)
        return self.bufs[name]


def build_tok(has_proj, final, P=None, io=None, tag=""):
    standalone = P is None
    nc = bass.Bass("TRN2", target_bir_lowering=False) if standalone else P.nc
    D = Dram(nc, io)
    NT = 16
    x = D("x", [2048, 1024], F32, "ExternalInput")
    ident_d = D("ident", [128, 128], BF16, "ExternalInput")
    direct_og = standalone or (io is not None and "og" in io)
    if has_proj:
        if direct_og:
            og = D("og", [8, 128, 2048], BF16, "ExternalInput")
        wo = D("wo", [1024, 1024], F32, "ExternalInput")
    if final:
        gf = D("gf", [1024], F32, "ExternalInput")
        y = D("y", [2048, 1024], F32, "ExternalOutput")
    else:
        xo = D("xo", [2048, 1024], F32, "ExternalOutput")
        xh = D("xh", [8, 128, 2048], BF16, "ExternalOutput")
    with ExitStack() as es:
        C = Ctx(nc, es, parent=P, tag=tag)
        S = C.S
        fin = []
        ident = C.sb("identb", [128, 128], BF16)
        b_ident = Buf()
        S.dma("sp", lambda e: e.dma_start(out=ident[:], in_=ident_d[:, :]), [], [b_ident])
        if has_proj:
            if direct_og:
                def og_dma(e, dst, t0):
                    return e.dma_start(out=dst[:], in_=og[:, :, t0:t0 + 128].rearrange("h p t -> p h t"))
                og_deps = [D.buf("og")]
            else:
                ogv = io["ogfull"].rearrange("(c j p) t -> c j p t", c=8, j=4)
                oht = C.sb("oht", [128, 4], F32)
                b_oh = Buf()
                S.dma("sp", lambda e: e.dma_start(out=oht[:], in_=io["oh"][:, :]), [], [b_oh])
                cand = [[C.sb("cand%d_%d" % (k, j), [128, 8, 128], BF16) for j in range(4)] for k in range(2)]
                b_cand = [[Buf() for j in range(4)] for k in range(2)]
                og_deps = [D.buf("og")]
            wof = C.sb("wof", [128, 8, 1024], F32)
            wob = C.sb("wob", [128, 8, 1024], BF16)
            b_wof, b_wob = Buf(), Buf()
            wo_v = wo.rearrange("(h p) n -> p h n", p=128)
            for h in range(8):
                S.dma("sp" if h % 2 == 0 else "pool",
                      lambda e, h=h: e.dma_start(out=wof[:, h, :], in_=wo_v[:, h, :]), [], [b_wof])
            for h in range(8):
                S.op("pool" if h % 2 == 0 else "dve",
                     lambda e, h=h: e.tensor_copy(out=wob[:, h, :], in_=wof[:, h, :]), [b_wof], [b_wob])
        if final:
            gfb = C.sb("gfb", [128, 1024], F32)
            b_gfb = Buf()
            S.dma("pool", lambda e: e.dma_start(out=gfb[:], in_=gf.partition_broadcast(128)), [], [b_gfb])
        NB = 2
        epsb = C.sb("epsb", [128, 1], F32)
        b_eps = Buf()
        S.op("pool", lambda e: e.memset(epsb[:], EPS), [], [b_eps])
        xt = [C.sb("xt%d" % i, [128, 1024], F32) for i in range(NB)]
        b_xt = [Buf() for _ in range(NB)]
        sq = [C.sb("sq%d" % i, [128, 1024], F32) for i in range(NB)]
        b_sq = [Buf() for _ in range(NB)]
        st = [C.sb("st%d" % i, [128, 4], F32) for i in range(NB)]
        b_st = [Buf() for _ in range(NB)]
        if has_proj:
            ot = [C.sb("ot%d" % i, [128, 8, 128], BF16) for i in range(NB)]
            b_ot = [Buf() for _ in range(NB)]
            pp = [C.psum_all[:, 1024 * i:1024 * (i + 1)] for i in range(NB)]
            b_pp = [[C.bankb[2 * i], C.bankb[2 * i + 1]] for i in range(NB)]
        if final:
            yt = [C.sb("yt%d" % i, [128, 1024], F32) for i in range(NB)]
            b_yt = [Buf() for _ in range(NB)]
        else:
            xb = [C.sb("xb%d" % i, [128, 1024], BF16) for i in range(NB)]
            b_xb = [Buf() for _ in range(NB)]
            pt = [C.banks[4 + i].bitcast(BF16).rearrange("p (c t) -> p c t", c=8) for i in range(NB)]
            b_pt = [C.bankb[4 + i] for i in range(NB)]
            stg = [C.sb("stg%d" % i, [128, 8, 512], BF16) for i in range(2)]
            b_stg = [Buf() for _ in range(2)]
        for t in range(NT):
            i = t % NB
            t0 = t * 128
            S.dma("sp", lambda e, i=i, t0=t0: e.dma_start(out=xt[i][:], in_=x[t0:t0 + 128, :]), [D.buf("x")], [b_xt[i]])
            if has_proj:
                if direct_og:
                    S.dma("pool", lambda e, i=i, t0=t0: og_dma(e, ot[i], t0), og_deps, [b_ot[i]])
                else:
                    for j in range(4):
                        S.dma("pool" if j % 2 else "sp", lambda e, i=i, t0=t0, j=j: e.dma_start(
                            out=cand[i][j][:], in_=ogv[:, j, :, t0:t0 + 128].rearrange("c p t -> p c t")),
                            og_deps, [b_cand[i][j]])
                    S.op("dve", lambda e, i=i: e.tensor_scalar(out=ot[i][:], in0=cand[i][0][:], scalar1=oht[:, 0:1],
                                                               scalar2=None, op0=ALU.mult),
                         [b_cand[i][0], b_oh], [b_ot[i]])
                    for j in range(1, 4):
                        S.op("dve", lambda e, i=i, j=j: e.scalar_tensor_tensor(
                            out=ot[i][:], in0=cand[i][j][:], scalar=oht[:, j:j + 1], in1=ot[i][:],
                            op0=ALU.mult, op1=ALU.add), [b_cand[i][j], b_oh, b_ot[i]], [b_ot[i]])
                for nch in range(2):
                    for h in range(8):
                        S.op("pe", lambda e, i=i, h=h, nch=nch: e.matmul(
                            pp[i][:, nch * 512:(nch + 1) * 512], lhsT=ot[i][:, h, :],
                            rhs=wob[:, h, nch * 512:(nch + 1) * 512], start=(h == 0), stop=(h == 7)),
                            [b_ot[i], b_wob], b_pp[i])
                S.op("dve", lambda e, i=i: e.tensor_tensor(out=xt[i][:], in0=xt[i][:], in1=pp[i][:], op=ALU.add),
                     [b_xt[i]] + b_pp[i], [b_xt[i]])
            if not final:
                S.dma("pool", lambda e, i=i, t0=t0: e.dma_start(out=xo[t0:t0 + 128, :], in_=xt[i][:]), [b_xt[i]], [D.buf("xo")])
                fin.append(None)
            S.op("act", lambda e, i=i: e.activation(out=sq[i][:], in_=xt[i][:], func=AF.Square,
                                                    accum_out=st[i][:, 0:1]), [b_xt[i]], [b_sq[i], b_st[i]])
            S.op("act", lambda e, i=i: e.activation(out=st[i][:, 1:2], in_=st[i][:, 0:1], func=AF.Ln,
                                                    scale=1.0 / 1024, bias=epsb[:, 0:1]), [b_st[i], b_eps], [b_st[i]])
            S.op("act", lambda e, i=i: e.activation(out=st[i][:, 2:3], in_=st[i][:, 1:2], func=AF.Exp,
                                                    scale=-0.5), [b_st[i]], [b_st[i]])
            if final:
                S.op("dve", lambda e, i=i: e.scalar_tensor_tensor(out=yt[i][:], in0=xt[i][:], scalar=st[i][:, 2:3],
                                                                   in1=gfb[:], op0=ALU.mult, op1=ALU.mult),
                     [b_xt[i], b_st[i], b_gfb], [b_yt[i]])
                fin.append(S.dma("sp", lambda e, i=i, t0=t0: e.dma_start(out=y[t0:t0 + 128, :], in_=yt[i][:]),
                                 [b_yt[i]], [D.buf("y")]))
            else:
                S.op("act", lambda e, i=i: e.activation(out=xb[i][:], in_=xt[i][:], func=AF.Copy,
                                                        scale=st[i][:, 2:3]), [b_xt[i], b_st[i]], [b_xb[i]])
                for dc in range(8):
                    S.op("pe", lambda e, i=i, dc=dc: e.transpose(pt[i][:, dc, :], xb[i][:, dc * 128:(dc + 1) * 128],
                                                                 ident[:]), [b_xb[i], b_ident], [b_pt[i]])
                g = (t // 4) % 2
                tq = t % 4
                S.op("dve", lambda e, i=i, g=g, tq=tq: e.tensor_copy(out=stg[g][:, :, tq * 128:(tq + 1) * 128],
                                                                     in_=pt[i][:]), [b_pt[i]], [b_stg[g]])
                if tq == 3:
                    tb = (t // 4) * 512
                    fin.append(S.dma("sp", lambda e, g=g, tb=tb: e.dma_start(
                        out=xh[:, :, tb:tb + 512].rearrange("c p t -> p c t"), in_=stg[g][:]), [b_stg[g]], [D.buf("xh")]))
        fin = [f for f in fin if f is not None]
        if standalone:
            S.finalize(fin)
        else:
            S.flush()
    return nc if standalone else fin


def roundrobin(gens):
    gens = list(gens)
    while gens:
        nxt = []
        for g in gens:
            try:
                next(g)
                nxt.append(g)
            except StopIteration:
                pass
        gens = nxt


def build_gdn(ntiles=32, P=None, io=None, tag=""):
    standalone = P is None
    nc = bass.Bass("TRN2", target_bir_lowering=False) if standalone else P.nc
    D = Dram(nc, io)
    xh = D("xh", [4, 8, 128, 2048], BF16, "ExternalInput")
    wqkv = D("wqkv", [1024, 768], F32, "ExternalInput")
    wzab = D("wzab", [1024, 260], F32, "ExternalInput")
    gn = D("gn", [128, 8], F32, "ExternalInput")
    cw = D("cw", [128, 24], F32, "ExternalInput")
    hp = D("hp", [128, 4], F32, "ExternalInput")
    ong = D("ong", [128, 256], F32, "ExternalInput")
    ident_d = D("ident", [128, 128], BF16, "ExternalInput")
    cf_d = D("cf", [128, 4, 128], F32, "ExternalInput")
    cm_d = D("cm", [128, 10, 128], BF16, "ExternalInput")
    og = D("og", [2, 4, 128, 2048], BF16, "ExternalOutput")
    with ExitStack() as es:
        C = Ctx(nc, es, parent=P, tag=tag)
        S = C.S
        fin = []
        identb = C.sb("identb", [128, 128], BF16)
        cf = C.sb("cfs", [128, 4, 128], F32)
        gnt = C.sb("gnt", [128, 8], F32)
        cwt = C.sb("cwt", [128, 24], F32)
        hpt = C.sb("hpt", [128, 4], F32)
        ongt = C.sb("ongt", [128, 256], F32)
        b_const = Buf()
        for dst, src in ((identb, ident_d), (gnt, gn), (cwt, cw), (hpt, hp), (ongt, ong)):
            S.dma("sp", lambda e, dst=dst, src=src: e.dma_start(out=dst[:], in_=src[:, :]), [], [b_const])
        S.dma("sp", lambda e: e.dma_start(out=cf[:], in_=cf_d[:, :, :]), [], [b_const])
        cmt = C.sb("cmt", [128, 10, 128], BF16)
        S.dma("sp", lambda e: e.dma_start(out=cmt[:], in_=cm_d[:, :, :]), [], [b_const])
        Uf = cf[:, 0, :]
        NMA = cf[:, 1, :]
        NMQ = cf[:, 2, :]
        identf = cf[:, 3, :]
        onesf = C.sb("onesf", [128, 128], F32)
        onesb = C.sb("onesb", [128, 128], BF16)
        misc = C.sb("misc", [128, 8], F32)
        S.op("pool", lambda e: e.memset(onesf[:], 1.0), [], [b_const])
        S.op("pool", lambda e: e.memset(onesb[:], 1.0), [], [b_const])
        S.op("pool", lambda e: e.memset(misc[:, 0:1], EPS), [], [b_const])
        S.op("pool", lambda e: e.memset(misc[:, 1:2], 128 * EPS), [], [b_const])
        S.op("pool", lambda e: e.memset(misc[:, 2:3], 1.0), [], [b_const])
        S.op("act", lambda e: e.activation(out=misc[:, 5:7], in_=hpt[:, 0:2], func=AF.Exp), [b_const], [b_const])
        S.op("dve", lambda e: e.tensor_scalar(out=misc[:, 3:5], in0=misc[:, 5:7], scalar1=-1.0, scalar2=None,
                                              op0=ALU.mult), [b_const], [b_const])
        dg = C.sb("dg", [128, 24, 128], BF16)
        for k in range(24):
            S.op("dve" if k % 2 else "pool", lambda e, k=k: e.tensor_scalar(
                out=dg[:, k, :], in0=identf, scalar1=cwt[:, k:k + 1], scalar2=None, op0=ALU.mult),
                [b_const], [b_const])
        wq = C.sb("wq", [128, 8, 768], BF16)
        wz = C.sb("wz", [128, 8, 260], BF16)
        wst = [C.sb("wst%d" % i, [128, 1028], F32) for i in range(2)]
        b_wst = [Buf(), Buf()]
        b_w = Buf()
        for dc in range(8):
            i = dc % 2
            S.dma("sp", lambda e, i=i, dc=dc: e.dma_start(out=wst[i][:, 0:768], in_=wqkv[dc * 128:(dc + 1) * 128, :]),
                  [], [b_wst[i]])
            S.dma("pool", lambda e, i=i, dc=dc: e.dma_start(out=wst[i][:, 768:1028], in_=wzab[dc * 128:(dc + 1) * 128, :]),
                  [], [b_wst[i]])
            S.op("dve", lambda e, i=i, dc=dc: e.tensor_scalar(out=wq[:, dc, :], in0=wst[i][:, 0:768],
                                                              scalar1=gnt[:, dc:dc + 1], scalar2=None, op0=ALU.mult),
                 [b_wst[i], b_const], [b_w])
            S.op("dve", lambda e, i=i, dc=dc: e.tensor_scalar(out=wz[:, dc, :], in0=wst[i][:, 768:1028],
                                                              scalar1=gnt[:, dc:dc + 1], scalar2=None, op0=ALU.mult),
                 [b_wst[i], b_const], [b_w])
        TW = 256
        xt = [C.sb("xt%d" % i, [128, 8, TW], BF16) for i in range(2)]
        b_xt = [Buf(), Buf()]
        Pc = [[C.sb("Pc%d_%d" % (g, i), [128, TW + 3], BF16) for i in range(2)] for g in range(6)]
        b_Pc = [[Buf(), Buf()] for g in range(6)]
        for g in range(6):
            S.op("pool", lambda e, g=g: e.memset(Pc[g][0][:, 0:3], 0.0), [], [b_Pc[g][0]])
        s1 = [[C.sb("s1_%d_%d" % (g, i), [128, TW], BF16) for i in range(2)] for g in range(6)]
        b_s1 = [[Buf(), Buf()] for g in range(6)]
        qn = [[C.sb("qn_%d_%d" % (g, i), [128, TW], BF16) for i in range(2)] for g in range(4)]
        b_qn = [[Buf(), Buf()] for g in range(4)]
        sqb = [C.sb("sqb%d" % i, [128, TW], BF16) for i in range(2)]
        b_sqb = [Buf(), Buf()]
        lnb = [C.sb("lnb%d" % i, [128, TW], F32) for i in range(2)]
        b_lnb = [Buf(), Buf()]
        banks = C.banks

        def reg(b, lo, hi, bf=False):
            ap = banks[b][:, lo:hi]
            return ap.bitcast(BF16) if bf else ap

        bankb = C.bankb
        pj = [reg(0, 0, 256), reg(1, 0, 256)]
        b_pj = [bankb[0], bankb[1]]
        pc = [reg(0, 256, 512), reg(1, 256, 512)]
        b_pc = [bankb[0], bankb[1]]
        pz = reg(2, 0, 260)
        b_pz = bankb[2]
        gz = [C.sb("gz%d" % i, [128, 256], F32) for i in range(2)]
        b_gz = [Buf(), Buf()]
        sz = C.sb("sz", [128, 256], F32)
        b_sz = Buf()
        gt = [C.sb("gt%d" % i, [128, 16], F32) for i in range(2)]
        b_gt = [Buf(), Buf()]
        X1 = [reg(3 + 2 * h, 0, 256) for h in range(2)]
        X5 = [reg(3 + 2 * h, 256, 512) for h in range(2)]
        X2 = [reg(4 + 2 * h, 0, 256) for h in range(2)]
        X3 = [reg(4 + 2 * h, 256, 384) for h in range(2)]
        X4 = [reg(4 + 2 * h, 384, 512, True) for h in range(2)]
        X3b = [reg(7, 64 * h, 64 * h + 64, True) for h in range(2)]
        XT = [reg(7, 128 + 64 * h, 192 + 64 * h, True) for h in range(2)]
        b_X1 = [bankb[3], bankb[5]]
        b_X5a = b_X1
        b_X5b = b_X1
        b_X2 = [bankb[4], bankb[6]]
        b_X3 = b_X2
        b_X4 = b_X2
        b_X3b = [bankb[7], bankb[7]]
        b_XT = [bankb[7], bankb[7]]

        def hs(name, shape, dt):
            return [C.sb("%s_%d" % (name, h), shape, dt) for h in range(2)]

        gb = hs("gb", [128, 128], F32); b_gb = [Buf(), Buf()]
        Rsb = hs("Rsb", [128, 128], F32); b_Rsb = [Buf(), Buf()]
        egb = hs("egb", [128, 128], F32); b_egb = [Buf(), Buf()]
        gc2 = hs("gc2", [128, 8], F32); b_gc2 = [Buf(), Buf()]
        RA = hs("RA", [128, 128], F32); b_RA = [Buf(), Buf()]
        RQ = hs("RQ", [128, 128], F32); b_RQ = [Buf(), Buf()]
        DA = hs("DA", [128, 128], F32); b_DA = [Buf(), Buf()]
        DQ = hs("DQ", [128, 128], F32); b_DQ = [Buf(), Buf()]
        QKT = hs("QKT", [128, 128], BF16); b_QKT = [Buf(), Buf()]
        YZ = [[C.sb("YZ_%d_%d" % (h, i), [128, 2, 128], BF16) for i in range(2)] for h in range(2)]
        b_YZ = [[Buf(), Buf()] for h in range(2)]
        PQ = [[C.sb("PQ_%d_%d" % (h, i), [128, 2, 128], BF16) for i in range(2)] for h in range(2)]
        b_PQ = [[Buf(), Buf()] for h in range(2)]
        AB = hs("AB", [128, 2, 128], BF16); b_AB = [Buf(), Buf()]
        Wm = hs("Wm", [128, 2, 128], BF16); b_Wm = [Buf(), Buf()]
        kbg = hs("kbg", [128, 128], BF16); b_kbg = [Buf(), Buf()]
        kdec = hs("kdec", [128, 128], BF16); b_kdec = [Buf(), Buf()]
        vb = hs("vb", [128, 128], BF16); b_vb = [Buf(), Buf()]
        Usb = hs("Usb", [128, 128], F32); b_Usb = [Buf(), Buf()]
        wT = hs("wT", [128, 128], BF16); b_wT = [Buf(), Buf()]
        qdT = hs("qdT", [128, 128], BF16); b_qdT = [Buf(), Buf()]
        vnew = hs("vnew", [128, 128], BF16); b_vnew = [Buf(), Buf()]
        Sf = hs("Sf", [128, 128], F32); b_Sf = [Buf(), Buf()]
        Sb = hs("Sb", [128, 128], BF16); b_Sb = [Buf(), Buf()]
        junk = hs("junk", [128, 128], F32); b_junk = [Buf(), Buf()]
        ost = hs("ost", [128, 8], F32); b_ost = [Buf(), Buf()]
        ogt = hs("ogt", [128, 128], BF16); b_ogt = [Buf(), Buf()]
        ostg = [[C.sb("ostg_%d_%d" % (h, i), [128, 512], BF16) for i in range(2)] for h in range(2)]
        b_ostg = [[Buf(), Buf()] for h in range(2)]
        for h in range(2):
            S.op("pool", lambda e, h=h: e.memset(Sf[h][:], 0.0), [], [b_Sf[h]])
            S.op("pool", lambda e, h=h: e.memset(Sb[h][:], 0.0), [], [b_Sb[h]])

        def chunk_head(h, ti, ci, cg):
            cs = slice(ci * 128, (ci + 1) * 128)
            kT = qn[2 + h][ti][:, cs]
            qT = qn[h][ti][:, cs]
            vT = s1[4 + h][ti][:, cs]
            r_kT = [b_qn[2 + h][ti]]
            r_qT = [b_qn[h][ti]]
            r_vT = [b_s1[4 + h][ti]]
            g_ = gt[cg % 2]
            bg = b_gt[cg % 2]
            gcol = g_[:, 6 + h:7 + h]
            beta = g_[:, 12 + h:13 + h]
            S.op("dve", lambda e: e.tensor_scalar(out=gb[h][:], in0=onesf[:], scalar1=gcol, scalar2=None, op0=ALU.mult),
                 [bg, b_const], [b_gb[h]])
            S.op("pe", lambda e: e.matmul(X1[h][:, 0:128], lhsT=gb[h][:], rhs=Uf, start=True, stop=True),
                 [b_gb[h], b_const], [b_X1[h]])
            S.op("pe", lambda e: e.matmul(X1[h][:, 128:129], lhsT=Uf, rhs=gcol, start=True, stop=True),
                 [bg, b_const], [b_X1[h]])
            yield
            S.op("act", lambda e: e.activation(out=Rsb[h][:], in_=X1[h][:, 0:128], func=AF.Copy), [b_X1[h]], [b_Rsb[h]])
            S.op("act", lambda e: e.activation(out=egb[h][:], in_=X1[h][:, 0:128], func=AF.Exp), [b_X1[h]], [b_egb[h]])
            S.op("dve", lambda e: e.tensor_copy(out=gc2[h][:, 0:1], in_=X1[h][:, 128:129]), [b_X1[h]], [b_gc2[h]])
            S.op("dve", lambda e: e.tensor_scalar(out=gc2[h][:, 1:2], in0=X1[h][:, 128:129], scalar1=-1.0, scalar2=None,
                                                  op0=ALU.mult), [b_X1[h]], [b_gc2[h]])
            S.op("act", lambda e: e.activation(out=gc2[h][:, 2:3], in_=X1[h][:, 128:129], func=AF.Exp),
                 [b_X1[h]], [b_gc2[h]])
            S.op("dve", lambda e: e.tensor_tensor(out=gc2[h][:, 3:4], in0=gc2[h][:, 2:3], in1=beta, op=ALU.mult),
                 [b_gc2[h], bg], [b_gc2[h]])
            yield
            S.op("pool", lambda e: e.tensor_tensor(out=RA[h][:], in0=Rsb[h][:], in1=NMA, op=ALU.add),
                 [b_Rsb[h], b_const], [b_RA[h]])
            S.op("pool", lambda e: e.tensor_tensor(out=RQ[h][:], in0=Rsb[h][:], in1=NMQ, op=ALU.add),
                 [b_Rsb[h], b_const], [b_RQ[h]])
            S.op("pe", lambda e: e.matmul(X2[h][:, 0:128], lhsT=kT, rhs=kT, start=True, stop=True), r_kT, [b_X2[h]])
            S.op("pe", lambda e: e.matmul(X2[h][:, 128:256], lhsT=kT, rhs=qT, start=True, stop=True),
                 r_kT + r_qT, [b_X2[h]])
            yield
            S.op("act", lambda e: e.activation(out=DA[h][:], in_=RA[h][:], func=AF.Exp, scale=-1.0, bias=gc2[h][:, 0:1]),
                 [b_RA[h], b_gc2[h]], [b_DA[h]])
            S.op("act", lambda e: e.activation(out=DQ[h][:], in_=RQ[h][:], func=AF.Exp, scale=1.0, bias=gc2[h][:, 1:2]),
                 [b_RQ[h], b_gc2[h]], [b_DQ[h]])
            yield
            Abf = AB[h][:, 0, :]
            Bbf = AB[h][:, 1, :]
            S.op("dve", lambda e: e.scalar_tensor_tensor(out=Abf, in0=X2[h][:, 0:128], scalar=beta,
                                                         in1=DA[h][:], op0=ALU.mult, op1=ALU.mult),
                 [b_X2[h], bg, b_DA[h]], [b_AB[h]])
            S.op("dve", lambda e: e.tensor_tensor(out=QKT[h][:], in0=X2[h][:, 128:256], in1=DQ[h][:], op=ALU.mult),
                 [b_X2[h], b_DQ[h]], [b_QKT[h]])
            S.op("pe", lambda e: e.transpose(X3b[h][:], Abf, identb[:]), [b_AB[h], b_const], [b_X3b[h]])
            yield
            S.op("act", lambda e: e.activation(out=Bbf, in_=X3b[h][:], func=AF.Copy), [b_X3b[h]], [b_AB[h]])
            S.op("pe", lambda e: e.transpose(X4[h][:, 0:128], kT, identb[:]), r_kT + [b_const], [b_X4[h]])
            S.op("pe", lambda e: e.transpose(X4[h][:, 128:256], vT, identb[:]), r_vT + [b_const], [b_X4[h]])
            yield
            yz0 = YZ[h][0]
            pq0 = PQ[h][0]
            S.op("pool", lambda e: e.tensor_tensor(out=yz0[:, 1, :], in0=Abf, in1=cmt[:, 0, :], op=ALU.mult),
                 [b_AB[h], b_const], [b_YZ[h][0]])
            S.op("pool", lambda e: e.tensor_tensor(out=yz0[:, 0, :], in0=Bbf, in1=cmt[:, 1, :], op=ALU.mult),
                 [b_AB[h], b_const], [b_YZ[h][0]])
            S.op("pool", lambda e: e.tensor_tensor(out=pq0[:, 0, :], in0=identb[:], in1=yz0[:, 0, :], op=ALU.subtract),
                 [b_YZ[h][0], b_const], [b_PQ[h][0]])
            S.op("pool", lambda e: e.tensor_tensor(out=pq0[:, 1, :], in0=identb[:], in1=yz0[:, 1, :], op=ALU.subtract),
                 [b_YZ[h][0], b_const], [b_PQ[h][0]])
            S.op("act", lambda e: e.activation(out=kbg[h][:], in_=X4[h][:, 0:128], func=AF.Copy, scale=gc2[h][:, 3:4]),
                 [b_X4[h], b_gc2[h]], [b_kbg[h]])
            S.op("dve", lambda e: e.tensor_scalar(out=kdec[h][:], in0=X4[h][:, 0:128], scalar1=DQ[h][:, 127:128],
                                                  scalar2=None, op0=ALU.mult), [b_X4[h], b_DQ[h]], [b_kdec[h]])
            S.op("act", lambda e: e.activation(out=vb[h][:], in_=X4[h][:, 128:256], func=AF.Copy, scale=beta),
                 [b_X4[h], bg], [b_vb[h]])
            S.op("pool", lambda e: e.tensor_tensor(out=qdT[h][:], in0=qT, in1=egb[h][:], op=ALU.mult),
                 r_qT + [b_egb[h]], [b_qdT[h]])
            yield
            yz = 0
            pm = 0
            for lev in range(2):
                cur = YZ[h][yz]
                S.op("pe", lambda e, cur=cur: e.matmul(X1[h][:, 0:128], lhsT=cur[:, 1, :], rhs=cur[:, 0, :],
                                                       start=True, stop=True), [b_YZ[h][yz]], [b_X1[h]])
                S.op("pe", lambda e, cur=cur: e.matmul(X1[h][:, 128:256], lhsT=cur[:, 0, :], rhs=cur[:, 1, :],
                                                       start=True, stop=True), [b_YZ[h][yz]], [b_X1[h]])
                yield
                nyz = 1 - yz
                nxt = YZ[h][nyz]
                S.op("act", lambda e, nxt=nxt: e.activation(out=nxt[:, 0, :], in_=X1[h][:, 0:128], func=AF.Copy),
                     [b_X1[h]], [b_YZ[h][nyz]])
                S.op("dve", lambda e, nxt=nxt: e.tensor_copy(out=nxt[:, 1, :], in_=X1[h][:, 128:256]),
                     [b_X1[h]], [b_YZ[h][nyz]])
                yz = nyz
                yield
                pq = PQ[h][pm]
                S.op("pe", lambda e, nxt=nxt, pq=pq: e.matmul(X3[h][:], lhsT=nxt[:, 1, :], rhs=pq[:, 0, :],
                                                              start=True, stop=True),
                     [b_YZ[h][yz], b_PQ[h][pm]], [b_X3[h]])
                S.op("pe", lambda e, nxt=nxt, pq=pq: e.matmul(X2[h][:, 0:128], lhsT=nxt[:, 0, :], rhs=pq[:, 1, :],
                                                              start=True, stop=True),
                     [b_YZ[h][yz], b_PQ[h][pm]], [b_X2[h]])
                yield
                npm = 1 - pm
                npq = PQ[h][npm]
                S.op("dve", lambda e, pq=pq, npq=npq: e.tensor_tensor(out=npq[:, 0, :], in0=pq[:, 0, :], in1=X3[h][:],
                                                                      op=ALU.add),
                     [b_PQ[h][pm], b_X3[h]], [b_PQ[h][npm]])
                S.op("dve", lambda e, pq=pq, npq=npq: e.tensor_tensor(out=npq[:, 1, :], in0=pq[:, 1, :],
                                                                      in1=X2[h][:, 0:128], op=ALU.add),
                     [b_PQ[h][pm], b_X2[h]], [b_PQ[h][npm]])
                pm = npm
                yield
            for m in range(4):
                lastm = m == 3
                pq = PQ[h][pm]
                S.op("pe", lambda e, pq=pq: e.matmul(X1[h][:, 0:128], lhsT=Abf, rhs=pq[:, 0, :], start=True, stop=True),
                     [b_AB[h], b_PQ[h][pm]], [b_X1[h]])
                if not lastm:
                    S.op("pe", lambda e, pq=pq: e.matmul(X1[h][:, 128:256], lhsT=Bbf, rhs=pq[:, 1, :],
                                                         start=True, stop=True), [b_AB[h], b_PQ[h][pm]], [b_X1[h]])
                yield
                S.op("dve", lambda e, m=m: e.tensor_tensor(out=Wm[h][:, 0, :], in0=X1[h][:, 0:128],
                                                           in1=cmt[:, 3 + 2 * m, :], op=ALU.mult),
                     [b_X1[h], b_const], [b_Wm[h]])
                if not lastm:
                    S.op("dve", lambda e, m=m: e.tensor_tensor(out=Wm[h][:, 1, :], in0=X1[h][:, 128:256],
                                                               in1=cmt[:, 2 + 2 * m, :], op=ALU.mult),
                         [b_X1[h], b_const], [b_Wm[h]])
                yield
                S.op("pe", lambda e, pq=pq: e.matmul(X3[h][:], lhsT=pq[:, 1, :], rhs=Wm[h][:, 0, :], start=True, stop=True),
                     [b_PQ[h][pm], b_Wm[h]], [b_X3[h]])
                if not lastm:
                    S.op("pe", lambda e, pq=pq: e.matmul(X2[h][:, 0:128], lhsT=pq[:, 0, :], rhs=Wm[h][:, 1, :],
                                                         start=True, stop=True), [b_PQ[h][pm], b_Wm[h]], [b_X2[h]])
                yield
                npm = 1 - pm
                npq = PQ[h][npm]
                S.op("dve", lambda e, pq=pq, npq=npq: e.tensor_tensor(out=npq[:, 0, :], in0=pq[:, 0, :], in1=X3[h][:],
                                                                      op=ALU.subtract),
                     [b_PQ[h][pm], b_X3[h]], [b_PQ[h][npm]])
                if not lastm:
                    S.op("dve", lambda e, pq=pq, npq=npq: e.tensor_tensor(out=npq[:, 1, :], in0=pq[:, 1, :],
                                                                          in1=X2[h][:, 0:128], op=ALU.subtract),
                         [b_PQ[h][pm], b_X2[h]], [b_PQ[h][npm]])
                pm = npm
                yield
            TT = PQ[h][pm][:, 0, :]
            bTT = b_PQ[h][pm]
            S.op("pe", lambda e: e.matmul(X2[h][:, 0:128], lhsT=TT, rhs=vb[h][:], start=True, stop=True),
                 [bTT, b_vb[h]], [b_X2[h]])
            S.op("pe", lambda e: e.matmul(X2[h][:, 128:256], lhsT=kbg[h][:], rhs=TT, start=True, stop=True),
                 [bTT, b_kbg[h]], [b_X2[h]])
            yield
            S.op("act", lambda e: e.activation(out=Usb[h][:], in_=X2[h][:, 0:128], func=AF.Copy), [b_X2[h]], [b_Usb[h]])
            S.op("dve", lambda e: e.tensor_copy(out=wT[h][:], in_=X2[h][:, 128:256]), [b_X2[h]], [b_wT[h]])
            yield
            S.op("pe", lambda e: e.matmul(X5[h][:, 0:128], lhsT=wT[h][:], rhs=Sb[h][:], start=True, stop=True),
                 [b_wT[h], b_Sb[h]], [b_X5a[h]])
            yield
            S.op("dve", lambda e: e.tensor_tensor(out=vnew[h][:], in0=Usb[h][:], in1=X5[h][:, 0:128], op=ALU.subtract),
                 [b_Usb[h], b_X5a[h]], [b_vnew[h]])
            S.op("pe", lambda e: e.matmul(X5[h][:, 128:256], lhsT=qdT[h][:], rhs=Sb[h][:], start=True, stop=False),
                 [b_qdT[h], b_Sb[h]], [b_X5b[h]])
            yield
            S.op("pe", lambda e: e.matmul(X5[h][:, 128:256], lhsT=QKT[h][:], rhs=vnew[h][:], start=False, stop=True),
                 [b_QKT[h], b_vnew[h]], [b_X5b[h]])
            S.op("pe", lambda e: e.matmul(X5[h][:, 0:128], lhsT=kdec[h][:], rhs=vnew[h][:], start=True, stop=True),
                 [b_kdec[h], b_vnew[h]], [b_X5a[h]])
            yield
            S.op("dve", lambda e: e.scalar_tensor_tensor(out=Sf[h][:], in0=Sf[h][:], scalar=egb[h][:, 127:128],
                                                         in1=X5[h][:, 0:128], op0=ALU.mult, op1=ALU.add),
                 [b_Sf[h], b_egb[h], b_X5a[h]], [b_Sf[h]])
            S.op("act", lambda e: e.activation(out=junk[h][:], in_=X5[h][:, 128:256], func=AF.Square,
                                               accum_out=ost[h][:, 0:1]), [b_X5b[h]], [b_junk[h], b_ost[h]])
            yield
            S.op("act", lambda e: e.activation(out=Sb[h][:], in_=Sf[h][:], func=AF.Copy), [b_Sf[h]], [b_Sb[h]])
            S.op("act", lambda e: e.activation(out=ost[h][:, 1:2], in_=ost[h][:, 0:1], func=AF.Ln, scale=1.0 / 128,
                                               bias=misc[:, 0:1]), [b_ost[h], b_const], [b_ost[h]])
            S.op("act", lambda e: e.activation(out=ost[h][:, 2:3], in_=ost[h][:, 1:2], func=AF.Exp, scale=-0.5),
                 [b_ost[h]], [b_ost[h]])
            yield
            S.op("dve", lambda e: e.scalar_tensor_tensor(out=ogt[h][:], in0=X5[h][:, 128:256], scalar=ost[h][:, 2:3],
                                                         in1=gz[cg % 2][:, h * 128:(h + 1) * 128], op0=ALU.mult,
                                                         op1=ALU.mult),
                 [b_X5b[h], b_ost[h], b_gz[cg % 2]], [b_ogt[h]])
            S.op("pe", lambda e: e.transpose(XT[h][:], ogt[h][:], identb[:]), [b_ogt[h], b_const], [b_XT[h]])
            yield
            sg = (cg // 4) % 2
            sp_ = cg % 4
            S.op("act", lambda e: e.activation(out=ostg[h][sg][:, sp_ * 128:(sp_ + 1) * 128], in_=XT[h][:], func=AF.Copy),
                 [b_XT[h]], [b_ostg[h][sg]])
            if sp_ == 3:
                tb = (cg // 4) * 512
                fin.append(S.dma("sp", lambda e: e.dma_start(out=og[h, tb // 2048, :, tb % 2048:tb % 2048 + 512],
                                                             in_=ostg[h][sg][:]), [b_ostg[h][sg]], [D.buf("og")]))

        import os
        lvl = int(os.environ.get("GDN_S1", "9"))

        def do_tile(t):
            if lvl < 2:
                return
            ti = t % 2
            rank = (t * TW) // 2048
            toff = (t * TW) % 2048
            S.dma("sp" if t % 2 == 0 else "pool", lambda e, ti=ti, rank=rank, toff=toff: e.dma_start(
                out=xt[ti][:], in_=xh[rank, :, :, toff:toff + TW].rearrange("c p t -> p c t")), [D.buf("xh")], [b_xt[ti]])
            def grp(g):
                pi = g % 2
                for dc in range(8):
                    S.op("pe", lambda e, g=g, dc=dc, pi=pi: e.matmul(pj[pi][:], lhsT=wq[:, dc, g * 128:(g + 1) * 128],
                                                                      rhs=xt[ti][:, dc, :], start=(dc == 0), stop=(dc == 7)),
                         [b_w, b_xt[ti]], [b_pj[pi]])
                if t > 0 and lvl >= 3:
                    S.op("pool", lambda e, g=g: e.tensor_copy(out=Pc[g][ti][:, 0:3], in_=Pc[g][1 - ti][:, TW:TW + 3]),
                         [b_Pc[g][1 - ti]], [b_Pc[g][ti]])
                S.op("act", lambda e, g=g, pi=pi: e.activation(out=Pc[g][ti][:, 3:TW + 3], in_=pj[pi][:], func=AF.Copy),
                     [b_pj[pi]], [b_Pc[g][ti]])
                if lvl < 3:
                    return
                for j in range(4):
                    S.op("pe", lambda e, g=g, j=j, pi=pi: e.matmul(pc[pi][:], lhsT=dg[:, g * 4 + j, :],
                                                                    rhs=Pc[g][ti][:, j:j + TW], start=(j == 0), stop=(j == 3)),
                         [b_const, b_Pc[g][ti]], [b_pc[pi]])
                S.op("act", lambda e, g=g, pi=pi: e.activation(out=s1[g][ti][:], in_=pc[pi][:], func=AF.Silu),
                     [b_pc[pi]], [b_s1[g][ti]])
            for g in range(6):
                grp(g)

            def zab(ci):
                if lvl < 4:
                    return
                cg = t * 2 + ci
                gi = cg % 2
                for dc in range(8):
                    S.op("pe", lambda e, dc=dc, ci=ci: e.matmul(pz[:], lhsT=xt[ti][:, dc, ci * 128:(ci + 1) * 128],
                                                                rhs=wz[:, dc, :], start=(dc == 0), stop=(dc == 7)),
                         [b_w, b_xt[ti]], [b_pz])
                S.op("act", lambda e: e.activation(out=sz[:], in_=pz[:, 0:256], func=AF.Silu), [b_pz], [b_sz])
                S.op("dve", lambda e, gi=gi: e.tensor_tensor(out=gt[gi][:, 0:2], in0=pz[:, 256:258], in1=hpt[:, 2:4],
                                                             op=ALU.add), [b_pz, b_const], [b_gt[gi]])
                S.op("dve", lambda e, gi=gi: e.tensor_copy(out=gt[gi][:, 8:10], in_=pz[:, 258:260]), [b_pz], [b_gt[gi]])
                S.op("pool", lambda e, gi=gi: e.tensor_tensor(out=gz[gi][:], in0=sz[:], in1=ongt[:], op=ALU.mult),
                     [b_sz, b_const], [b_gz[gi]])
                S.op("act", lambda e, gi=gi: e.activation(out=gt[gi][:, 2:4], in_=gt[gi][:, 0:2], func=AF.Exp),
                     [b_gt[gi]], [b_gt[gi]])
                S.op("act", lambda e, gi=gi: e.activation(out=gt[gi][:, 4:6], in_=gt[gi][:, 2:4], func=AF.Ln,
                                                          bias=misc[:, 2:3]), [b_gt[gi], b_const], [b_gt[gi]])
                S.op("act", lambda e, gi=gi: e.activation(out=gt[gi][:, 10:12], in_=gt[gi][:, 8:10], func=AF.Exp,
                                                          scale=-1.0), [b_gt[gi]], [b_gt[gi]])
                S.op("dve", lambda e, gi=gi: e.tensor_tensor(out=gt[gi][:, 6:8], in0=gt[gi][:, 4:6], in1=misc[:, 3:5],
                                                             op=ALU.mult), [b_gt[gi], b_const], [b_gt[gi]])
                S.op("dve", lambda e, gi=gi: e.tensor_scalar(out=gt[gi][:, 10:12], in0=gt[gi][:, 10:12], scalar1=1.0,
                                                             scalar2=None, op0=ALU.add), [b_gt[gi]], [b_gt[gi]])
                S.op("dve", lambda e, gi=gi: e.reciprocal(out=gt[gi][:, 12:14], in_=gt[gi][:, 10:12]),
                     [b_gt[gi]], [b_gt[gi]])
            for ci in range(2):
                zab(ci)

            def l2n(g):
                if lvl < 5:
                    return
                pi = g % 2
                isq = g < 2
                S.op("act", lambda e, g=g, pi=pi, isq=isq: e.activation(
                    out=sqb[pi][:], in_=s1[g][ti][:], func=AF.Square, scale=(float(np.sqrt(128.0)) if isq else 1.0)),
                    [b_s1[g][ti]], [b_sqb[pi]])
                S.op("pe", lambda e, pi=pi: e.matmul(pc[pi][:], lhsT=onesb[:], rhs=sqb[pi][:], start=True, stop=True),
                     [b_const, b_sqb[pi]], [b_pc[pi]])
                S.op("act", lambda e, pi=pi, isq=isq: e.activation(out=lnb[pi][:], in_=pc[pi][:], func=AF.Ln,
                                                                   bias=(misc[:, 1:2] if isq else misc[:, 0:1])),
                     [b_pc[pi], b_const], [b_lnb[pi]])
                S.op("act", lambda e, pi=pi: e.activation(out=lnb[pi][:], in_=lnb[pi][:], func=AF.Exp, scale=-0.5),
                     [b_lnb[pi]], [b_lnb[pi]])
                S.op("dve", lambda e, g=g, pi=pi: e.tensor_tensor(out=qn[g][ti][:], in0=s1[g][ti][:], in1=lnb[pi][:],
                                                                  op=ALU.mult), [b_s1[g][ti], b_lnb[pi]], [b_qn[g][ti]])
            for g in range(4):
                l2n(g)
            for ci in range(2):
                cg = t * 2 + ci
                import os, itertools
                dbg = int(os.environ.get("GDN_DBG", "99"))
                if dbg >= 99:
                    roundrobin([chunk_head(0, ti, ci, cg), chunk_head(1, ti, ci, cg)])
                elif dbg >= 2:
                    roundrobin([itertools.islice(chunk_head(0, ti, ci, cg), dbg - 2),
                                itertools.islice(chunk_head(1, ti, ci, cg), dbg - 2)])
        for t in range(ntiles):
            do_tile(t)
        if standalone:
            S.finalize(fin)
        else:
            S.flush()
    return nc if standalone else fin


import ml_dtypes
NPBF = ml_dtypes.bfloat16


def _consts():
    i = np.arange(128)
    U = (i[:, None] <= i[None, :]).astype(np.float32)
    NMA = np.where(i[None, :] >= i[:, None], 30000.0, 0.0).astype(np.float32)
    NMQ = np.where(i[None, :] < i[:, None], -30000.0, 0.0).astype(np.float32)
    I = np.eye(128, dtype=np.float32)
    cf = np.ascontiguousarray(np.stack([U, NMA, NMQ, I], axis=1))
    ms = []
    bi = i[:, None]
    bj = i[None, :]
    m8 = ((bi // 8) == (bj // 8)).astype(np.float32)
    ms += [m8, m8.T]
    for m in range(4):
        sz = 8 << m
        ml = (((bi // sz) == (bj // sz) + 1) & ((bi // (2 * sz)) == (bj // (2 * sz)))).astype(np.float32)
        ms += [ml, ml.T]
    cm = np.ascontiguousarray(np.stack(ms, axis=1)).astype(NPBF)
    return {"ident": I.astype(NPBF), "cf": cf, "cm": cm}


def gdn_inputs(inp, layer, r):
    h0, h1 = 2 * r, 2 * r + 1
    w = inp["a_w_in"][layer]
    cols = []
    for base in (0, 1024, 2048):
        for h in (h0, h1):
            cols.append(np.arange(base + h * 128, base + (h + 1) * 128))
    qkv_cols = np.concatenate(cols)
    zcols = np.concatenate([np.arange(3072 + h * 128, 3072 + (h + 1) * 128) for h in (h0, h1)])
    ab_cols = np.array([4096 + h0, 4096 + h1, 4104 + h0, 4104 + h1])
    wqkv = np.ascontiguousarray(w[:, qkv_cols])
    wzab = np.ascontiguousarray(w[:, np.concatenate([zcols, ab_cols])])
    gn = np.ascontiguousarray(inp["a_norm_g"][layer].reshape(8, 128).T)
    cwf = inp["a_conv_w"][layer][:, qkv_cols]
    cw = np.ascontiguousarray(cwf.reshape(4, 6, 128).transpose(2, 1, 0).reshape(128, 24))
    hp = np.broadcast_to(np.array([inp["a_log"][layer][h0], inp["a_log"][layer][h1],
                                   inp["a_dt_bias"][layer][h0], inp["a_dt_bias"][layer][h1]], np.float32), (128, 4))
    ong = np.broadcast_to(np.tile(inp["a_out_norm_g"][layer], 2), (128, 256))
    d = {"wqkv": wqkv, "wzab": wzab, "gn": gn, "cw": cw, "hp": np.ascontiguousarray(hp),
         "ong": np.ascontiguousarray(ong)}
    d.update(_consts())
    return d


TZL = 2432


def build_moba(ngroups=16, P=None, io=None, tag="", kv_mode="compute"):
    standalone = P is None
    nc = bass.Bass("TRN2", target_bir_lowering=False) if standalone else P.nc
    D = Dram(nc, io)
    xh = D("xh", [4, 8, 128, 2048], BF16, "ExternalInput")
    xh2 = D("xh2", [4, 8, 128, 2048], BF16, "ExternalInput")
    wqz = D("wqz", [1024, 512], F32, "ExternalInput")
    wkv = D("wkv", [1024, 512], F32, "ExternalInput")
    gn = D("gn", [128, 16], F32, "ExternalInput")
    tz = D("tz", [2, 128, TZL], F32, "ExternalInput")
    b31 = D("b31", [128, 2], F32, "ExternalInput")
    lsel_d = D("lsel", [32, 32, 128], BF16, "ExternalInput")
    ident_d = D("ident", [128, 128], BF16, "ExternalInput")
    og = D("og", [2, 4, 128, 2048], BF16, "ExternalOutput")
    SCALE = float(128 ** -0.5)
    BIG = 30000.0
    with ExitStack() as es:
        C = Ctx(nc, es, parent=P, tag=tag)
        S = C.S
        fin = []
        b_const = Buf()
        identb = C.sb("identb", [128, 128], BF16)
        gnt = C.sb("gnt", [128, 16], F32)
        b31t = C.sb("b31t", [128, 4], F32)
        lsel = C.sb("lselt", [32, 32, 128], BF16)
        S.dma("sp", lambda e: e.dma_start(out=identb[:], in_=ident_d[:, :]), [], [b_const])
        S.dma("sp", lambda e: e.dma_start(out=gnt[:], in_=gn[:, :]), [], [b_const])
        S.dma("sp", lambda e: e.dma_start(out=b31t[:, 0:2], in_=b31[:, :]), [], [b_const])
        S.dma("sp", lambda e: e.dma_start(out=lsel[:], in_=lsel_d[:, :, :]), [], [b_const])
        S.op("pool", lambda e: e.memset(b31t[:, 2:3], 0.0), [], [b_const])
        ebT = [C.sb("ebT%d" % h, [128, TZL], BF16) for h in range(2)]
        tzs = C.sb("tzs", [128, TZL], F32)
        b_tzs = Buf()
        for h in range(2):
            S.dma("sp", lambda e, h=h: e.dma_start(out=tzs[:], in_=tz[h, :, :]), [], [b_tzs])
            S.op("act", lambda e, h=h: e.activation(out=ebT[h][:], in_=tzs[:], func=AF.Exp), [b_tzs], [b_const])
        wq = C.sb("wq", [128, 8, 512], BF16)
        wk = C.sb("wk", [128, 8, 512], BF16)
        wst = [C.sb("wst%d" % i, [128, 1024], F32) for i in range(2)]
        b_wst = [Buf(), Buf()]
        b_w = Buf()
        for dc in range(8):
            i = dc % 2
            S.dma("sp", lambda e, i=i, dc=dc: e.dma_start(out=wst[i][:, 0:512], in_=wqz[dc * 128:(dc + 1) * 128, :]),
                  [], [b_wst[i]])
            S.dma("pool", lambda e, i=i, dc=dc: e.dma_start(out=wst[i][:, 512:1024], in_=wkv[dc * 128:(dc + 1) * 128, :]),
                  [], [b_wst[i]])
            S.op("dve", lambda e, i=i, dc=dc: e.tensor_scalar(out=wq[:, dc, :], in0=wst[i][:, 0:512],
                                                              scalar1=gnt[:, dc:dc + 1], scalar2=None, op0=ALU.mult),
                 [b_wst[i], b_const], [b_w])
            S.op("dve", lambda e, i=i, dc=dc: e.tensor_scalar(out=wk[:, dc, :], in0=wst[i][:, 512:1024],
                                                              scalar1=gnt[:, 8 + dc:9 + dc], scalar2=None, op0=ALU.mult),
                 [b_wst[i], b_const], [b_w])
        banks = C.banks
        bankb = C.bankb
        KT = C.sb("KT", [128, 2, 8192], BF16)
        b_KT = [Buf() for _ in range(16)]
        Vaug = C.sb("Vaug", [128, 2, 64, 130], BF16)
        b_V = [Buf() for _ in range(16)]
        kmT = C.sb("kmT", [128, 2, 32], F32)
        b_km = Buf()
        S.op("pool", lambda e: e.memset(Vaug[:, :, :, 128:130], 1.0), [], [b_V[0]])
        xt = [C.sb("xt%d" % i, [128, 8, 512], BF16) for i in range(2)]
        b_xt = [Buf(), Buf()]

        def load_x(src, T, i, bname="xh"):
            rank = (T * 512) // 2048
            toff = (T * 512) % 2048
            S.dma("sp" if T % 2 == 0 else "pool", lambda e: e.dma_start(
                out=xt[i][:], in_=src[rank, :, :, toff:toff + 512].rearrange("c p t -> p c t")), [D.buf(bname)], [b_xt[i]])

        def phase_a(T):
            i = T % 2
            load_x(xh2, T, i, "xh2")

            def kproj(h):
                for dc in range(8):
                    S.op("pe", lambda e, dc=dc: e.matmul(banks[4][:], lhsT=wk[:, dc, h * 128:(h + 1) * 128],
                                                         rhs=xt[i][:, dc, :], start=(dc == 0), stop=(dc == 7)),
                         [b_w, b_xt[i]], [bankb[4]])
                S.op("act", lambda e: e.activation(out=KT[:, h, T * 512:(T + 1) * 512], in_=banks[4][:], func=AF.Copy),
                     [bankb[4]], [b_KT[T]])
                S.op("dve", lambda e: e.tensor_reduce(out=kmT[:, h, 2 * T:2 * T + 2],
                                                      in_=banks[4][:].rearrange("p (b t) -> p b t", b=2),
                                                      axis=AX.X, op=ALU.add), [bankb[4]], [b_km])

            def vproj(s):
                for dc in range(8):
                    S.op("pe", lambda e, dc=dc: e.matmul(banks[5][:, 0:256], lhsT=xt[i][:, dc, s * 128:(s + 1) * 128],
                                                         rhs=wk[:, dc, 256:512], start=(dc == 0), stop=(dc == 7)),
                         [b_w, b_xt[i]], [bankb[5]])
                S.op("dve", lambda e: e.tensor_copy(out=Vaug[:, :, 4 * T + s, 0:128],
                                                    in_=banks[5][:, 0:256].rearrange("p (h d) -> p h d", h=2)),
                     [bankb[5]], [b_V[T]])
            for h in range(2):
                kproj(h)
            for s in range(4):
                vproj(s)
        for T in range(16):
            phase_a(T)

        q_bf = [C.sb("q_bf%d" % h, [128, 512], BF16) for h in range(2)]
        q_f = [C.sb("q_f%d" % h, [128, 512], F32) for h in range(2)]
        b_q = [Buf(), Buf()]
        gzt = C.sb("gzt", [128, 4, 256], F32)
        b_gz = Buf()
        MT = [C.sb("MT%d" % h, [32, 512], BF16) for h in range(2)]
        b_MT = [Buf(), Buf()]
        gsb = [C.sb("gsb%d" % i, [128, 32], F32) for i in range(2)]
        m8 = [C.sb("m8_%d" % i, [128, 8], F32) for i in range(2)]
        mbf = [C.sb("mbf%d" % i, [128, 32], F32) for i in range(2)]
        mbb = [C.sb("mbb%d" % i, [128, 32], BF16) for i in range(2)]
        b_gs = [Buf(), Buf()]
        pT = [C.sb("pT%d" % i, [128, 512], BF16) for i in range(2)]
        b_pT = [Buf(), Buf()]
        rinv = C.sb("rinv", [128, 4], F32)
        b_rinv = Buf()
        ogt = C.sb("ogt", [128, 4, 128], BF16)
        b_ogt = Buf()
        ostg = [C.sb("ostg%d" % i, [128, 512], BF16) for i in range(2)]
        b_ostg = [Buf(), Buf()]
        mtp = banks[6][:, 64:128].bitcast(BF16)
        otp = banks[7][:, 0:256].bitcast(BF16)

        def oacc(s):
            return banks[2 + s // 2][:, (s % 2) * 130:(s % 2) * 130 + 129]

        def group(G):
            i = G % 2
            load_x(xh, G, i)

            def qproj(h):
                for dc in range(8):
                    S.op("pe", lambda e, dc=dc: e.matmul(banks[4][:], lhsT=wq[:, dc, h * 128:(h + 1) * 128],
                                                         rhs=xt[i][:, dc, :], start=(dc == 0), stop=(dc == 7)),
                         [b_w, b_xt[i]], [bankb[4]])
                S.op("act", lambda e: e.activation(out=q_bf[h][:], in_=banks[4][:], func=AF.Copy), [bankb[4]], [b_q[h]])
                S.op("dve", lambda e: e.tensor_copy(out=q_f[h][:], in_=banks[4][:]), [bankb[4]], [b_q[h]])

            def zproj(s):
                for dc in range(8):
                    S.op("pe", lambda e, dc=dc: e.matmul(banks[5][:, 0:256], lhsT=xt[i][:, dc, s * 128:(s + 1) * 128],
                                                         rhs=wq[:, dc, 256:512], start=(dc == 0), stop=(dc == 7)),
                         [b_w, b_xt[i]], [bankb[5]])
                S.op("act", lambda e: e.activation(out=gzt[:, s, :], in_=banks[5][:, 0:256], func=AF.Silu),
                     [bankb[5]], [b_gz])

            def gate(h, s, k):
                cur = 2 * G + s // 2
                S.op("pool", lambda e: e.memset(gsb[k][:], -3.0e38), [], [b_gs[k]])
                if cur > 0:
                    S.op("pe", lambda e: e.matmul(banks[6][:, 0:32], lhsT=q_f[h][:, s * 128:(s + 1) * 128],
                                                  rhs=kmT[:, h, :], start=True, stop=True), [b_q[h], b_km], [bankb[6]])
                    S.op("dve", lambda e: e.tensor_copy(out=gsb[k][:, 0:cur], in_=banks[6][:, 0:cur]),
                         [bankb[6]], [b_gs[k]])
                S.op("dve", lambda e: e.max(out=m8[k][:], in_=gsb[k][:]), [b_gs[k]], [b_gs[k]])
                S.op("dve", lambda e: e.tensor_scalar(out=mbf[k][:], in0=gsb[k][:], scalar1=m8[k][:, 2:3], scalar2=None,
                                                      op0=ALU.is_ge), [b_gs[k]], [b_gs[k]])
                S.op("dve", lambda e: e.tensor_scalar(out=mbb[k][:], in0=mbf[k][:], scalar1=-1.0, scalar2=BIG,
                                                      op0=ALU.add, op1=ALU.mult), [b_gs[k]], [b_gs[k]])
                S.op("pool", lambda e: e.memset(mbb[k][:, cur:cur + 1], 0.0), [b_gs[k]], [b_gs[k]])
                if cur < 31:
                    S.op("pool", lambda e: e.memset(mbb[k][:, cur + 1:32], -BIG), [b_gs[k]], [b_gs[k]])
                S.op("pe", lambda e: e.transpose(mtp[0:32, :], mbb[k][:], identb[:]), [b_gs[k], b_const], [bankb[6]])
                S.op("act", lambda e: e.activation(out=MT[h][:, s * 128:(s + 1) * 128], in_=mtp[0:32, :], func=AF.Copy),
                     [bankb[6]], [b_MT[h]])

            def attend(h):
                NJ = 4 * G + 4

                def qk(j):
                    n = j // 2
                    S.op("pe", lambda e: e.matmul(banks[j % 2][:], lhsT=KT[:, h, j * 128:(j + 1) * 128], rhs=q_bf[h][:],
                                                  start=True, stop=False), [b_KT[j // 4], b_q[h]], [bankb[j % 2]])
                    S.op("pe", lambda e: e.matmul(banks[j % 2][:], lhsT=lsel[:, n, :], rhs=MT[h][:],
                                                  start=False, stop=True), [b_const, b_MT[h]], [bankb[j % 2]])

                def ex(j):
                    d0 = 512 * G - 128 * j
                    far = d0 >= 1664
                    bias = b31t[:, h:h + 1] if far else b31t[:, 2:3]
                    S.op("act", lambda e: e.activation(out=pT[j % 2][:], in_=banks[j % 2][:], func=AF.Exp, scale=SCALE,
                                                       bias=bias), [bankb[j % 2], b_const], [b_pT[j % 2]])
                    if not far:
                        off = d0 + 384
                        S.op("pool", lambda e: e.tensor_tensor(out=pT[j % 2][:], in0=pT[j % 2][:],
                                                               in1=ebT[h][:, off:off + 512], op=ALU.mult),
                             [b_pT[j % 2], b_const], [b_pT[j % 2]])

                def pv(j):
                    for s in range(4):
                        S.op("pe", lambda e, s=s: e.matmul(oacc(s), lhsT=pT[j % 2][:, s * 128:(s + 1) * 128],
                                                           rhs=Vaug[:, h, j, 0:129], start=(j == 0 and s % 2 == 0),
                                                           stop=(j == NJ - 1), skip_group_check=True),
                             [b_pT[j % 2], b_V[j // 4]], [bankb[2 + s // 2]])
                qk(0)
                for j in range(NJ):
                    if j + 1 < NJ:
                        qk(j + 1)
                    ex(j)
                    pv(j)

                def fin_s(s):
                    S.op("dve", lambda e: e.reciprocal(out=rinv[:, s:s + 1], in_=oacc(s)[:, 128:129]),
                         [bankb[2 + s // 2]], [b_rinv])
                    S.op("dve", lambda e: e.scalar_tensor_tensor(out=ogt[:, s, :], in0=oacc(s)[:, 0:128],
                                                                 scalar=rinv[:, s:s + 1],
                                                                 in1=gzt[:, s, h * 128:(h + 1) * 128],
                                                                 op0=ALU.mult, op1=ALU.mult),
                         [bankb[2 + s // 2], b_rinv, b_gz], [b_ogt])
                    S.op("pe", lambda e: e.transpose(otp[:, s * 128:(s + 1) * 128], ogt[:, s, :], identb[:]),
                         [b_ogt, b_const], [bankb[7]])
                for s in range(4):
                    fin_s(s)
                k = (2 * G + h) % 2
                S.op("act", lambda e: e.activation(out=ostg[k][:], in_=otp[:, 0:512], func=AF.Copy),
                     [bankb[7]], [b_ostg[k]])
                fin.append(S.dma("sp", lambda e: e.dma_start(
                    out=og[h, (G * 512) // 2048, :, (G * 512) % 2048:(G * 512) % 2048 + 512], in_=ostg[k][:]),
                    [b_ostg[k]], [D.buf("og")]))

            for h in range(2):
                qproj(h)
            for s in range(4):
                zproj(s)
            kk = 0
            for h in range(2):
                for s in range(4):
                    gate(h, s, kk % 2)
                    kk += 1
            for h in range(2):
                attend(h)
        for G in range(ngroups):
            group(G)
        if standalone:
            S.finalize(fin)
        else:
            S.flush()
    return nc if standalone else fin


def t5_bucket_np(rel):
    import math
    n = np.maximum(rel, 0)
    nf = np.maximum(n, 1).astype(np.float32)
    large = 16 + (np.log(nf / np.float32(16)) / np.float32(math.log(2048 / 16)) * np.float32(16)).astype(np.int32)
    large = np.minimum(large, 31)
    return np.where(n < 16, n, large)


def moba_inputs(inp, j, r):
    h0, h1 = 2 * r, 2 * r + 1
    w = inp["b_w_in"][j]
    qc = np.concatenate([np.arange(h * 128, (h + 1) * 128) for h in (h0, h1)])
    wqz = np.ascontiguousarray(np.concatenate([w[:, qc], w[:, 1024 + qc]], axis=1))
    wkv = np.ascontiguousarray(np.concatenate([inp["w_kv"][:, qc], inp["w_kv"][:, 1024 + qc]], axis=1))
    gn = np.ascontiguousarray(np.concatenate([inp["b_norm_g"][j].reshape(8, 128).T,
                                              inp["kv_norm_g"].reshape(8, 128).T], axis=1))
    p = np.arange(128)[:, None]
    m = np.arange(TZL)[None, :]
    dist = m - p - 384
    idx = np.where(dist < 0, 32, t5_bucket_np(dist))
    tz = []
    for h in (h0, h1):
        ext = np.concatenate([inp["rel_bias"][:, h], np.array([-30000.0], np.float32)])
        tz.append(ext[idx])
    tz = np.ascontiguousarray(np.stack(tz).astype(np.float32))
    b31 = np.ascontiguousarray(np.broadcast_to(inp["rel_bias"][31, [h0, h1]], (128, 2)))
    lsel = np.zeros((32, 32, 128), np.float32)
    for n in range(32):
        lsel[n, n, :] = 1.0
    d = {"wqz": wqz, "wkv": wkv, "gn": gn, "tz": tz, "b31": b31, "lsel": lsel.astype(NPBF),
         "ident": np.eye(128, dtype=np.float32).astype(NPBF)}
    return d


GDN_W = (("wqkv", [1024, 768], F32), ("wzab", [1024, 260], F32), ("gn", [128, 8], F32), ("cw", [128, 24], F32),
         ("hp", [128, 4], F32), ("ong", [128, 256], F32))
MOBA_W = (("wqz", [1024, 512], F32), ("wkv", [1024, 512], F32), ("gn", [128, 16], F32), ("tz", [2, 128, TZL], F32),
          ("b31", [128, 2], F32))


def build_single(nlayers=4, npairs=4, nchunks=4, mini=False):
    nc = bass.Bass("TRN2", target_bir_lowering=False)

    def ext(name, shape, dt, kind="ExternalInput"):
        return nc.dram_tensor(name, shape, dt, kind=kind).ap()

    def internal(name, shape, dt):
        return nc.dram_tensor(name, shape, dt, kind="Internal").ap()

    with ExitStack() as es:
        P = Ctx(nc, es)
        S = P.S
        x_in = ext("x", [8192, 1024], F32)
        ident = ext("ident", [128, 128], BF16)
        cf = ext("cf", [128, 4, 128], F32)
        cm = ext("cm", [128, 10, 128], BF16)
        lsel = ext("lsel", [32, 32, 128], BF16)
        gf = ext("gf", [1024], F32)
        wo = [ext("wo%d" % l, [1024, 1024], F32) for l in range(4)]
        lw = {}
        for l in range(nlayers):
            spec = GDN_W if l < 2 else MOBA_W
            for p in range(npairs):
                lw[(l, p)] = {n: ext("%s_l%d_p%d" % (n, l, p), sh, dt) for n, sh, dt in spec}
        xs = internal("xs", [8192, 1024], F32)
        XH = [internal("XH%d" % i, [4, 8, 128, 2048], BF16) for i in range(2)]
        if mini:
            OG = ext("ogout", [8, 128, 8192], BF16, "ExternalOutput")
        else:
            OG = internal("OG", [8, 128, 8192], BF16)
            y = ext("y", [8192, 1024], F32, "ExternalOutput")
        b_xs = [Buf() for _ in range(4)]
        b_XH = [Buf(), Buf()]
        b_OG = Buf()
        fin = []
        for r in range(nchunks):
            rows = slice(2048 * r, 2048 * (r + 1))
            build_tok(False, False, P=P, tag="t0c%d_" % r,
                      io={"x": x_in[rows, :], "ident": ident, "xo": xs[rows, :], "xh": XH[0][r],
                          "_bufs": {"x": Buf(), "xo": b_xs[r], "xh": b_XH[0]}})
        for layer in range(nlayers):
            cur = layer % 2
            for p in range(npairs):
                ogv = OG[2 * p:2 * p + 2].rearrange("h p (j t) -> h j p t", j=4)
                io = dict(lw[(layer, p)])
                if layer < 2:
                    io.update({"xh": XH[cur], "ident": ident, "cf": cf, "cm": cm, "og": ogv,
                               "_bufs": {"xh": b_XH[cur], "og": b_OG}})
                    fin = build_gdn(ntiles=(4 if mini else 32), P=P, io=io, tag="g%dp%d_" % (layer, p))
                else:
                    io.update({"xh": XH[cur], "xh2": XH[0], "ident": ident, "lsel": lsel, "og": ogv,
                               "_bufs": {"xh": b_XH[cur], "xh2": b_XH[0], "og": b_OG}})
                    fin = build_moba(P=P, io=io, tag="m%dp%d_" % (layer, p))
            if mini:
                break
            final = layer == 3
            for r in range(nchunks):
                rows = slice(2048 * r, 2048 * (r + 1))
                io = {"x": xs[rows, :], "ident": ident, "og": OG[:, :, 2048 * r:2048 * (r + 1)], "wo": wo[layer],
                      "_bufs": {"x": b_xs[r], "xo": b_xs[r], "og": b_OG, "xh": b_XH[1 - cur], "y": Buf()}}
                if final:
                    io["gf"] = gf
                    io["y"] = y[rows, :]
                else:
                    io["xo"] = xs[rows, :]
                    io["xh"] = XH[1 - cur][r]
                f = build_tok(True, final, P=P, io=io, tag="t%dc%d_" % (layer + 1, r))
                if final:
                    fin = (fin if r else []) + f
        S.finalize(fin)
    return nc, S.nops


def single_maps(inp, mini=False, nlayers=4, npairs=4):
    consts = _consts()
    lsel = np.zeros((32, 32, 128), np.float32)
    for n in range(32):
        lsel[n, n, :] = 1.0
    shared = {"ident": consts["ident"], "cf": consts["cf"], "cm": consts["cm"], "lsel": lsel.astype(NPBF),
              "gf": np.ascontiguousarray(inp["final_norm_g"])}
    for l in range(4):
        shared["wo%d" % l] = np.ascontiguousarray(inp["a_w_out"][l] if l < 2 else inp["b_w_out"][l - 2])
    for l in range(nlayers):
        spec = GDN_W if l < 2 else MOBA_W
        for p in range(npairs):
            src = gdn_inputs(inp, l, p) if l < 2 else moba_inputs(inp, l - 2, p)
            for n, _, _ in spec:
                shared["%s_l%d_p%d" % (n, l, p)] = src[n]
    maps = []
    for b in range(2):
        d = dict(shared)
        d["x"] = np.ascontiguousarray(inp["x"][b].astype(np.float32))
        maps.append(d)
    return maps


def kernel(**inp):
    inp = {k: np.asarray(v) for k, v in inp.items()}
    nc, _ = build_single()
    res = run_bass_kernel_spmd(nc, single_maps(inp), core_ids=[0, 1])
    return np.stack([res.results[b]["y"] for b in range(2)]).astype(np.float32)
```

```python
import numpy as np
import concourse.bass as bass
import concourse.mybir as mybir
from concourse.bass_utils import run_bass_kernel_spmd
from contextlib import ExitStack

F32 = mybir.dt.float32
BF16 = mybir.dt.bfloat16
AF = mybir.ActivationFunctionType
ALU = mybir.AluOpType
AX = mybir.AxisListType


class Buf:
    __slots__ = ("name", "last_w", "readers", "excl")

    def __init__(self, name="", excl=False):
        self.name = name
        self.last_w = None
        self.readers = []
        self.excl = excl


class Q:
    def __init__(self, name, is_pe=False):
        self.name = name
        self.is_pe = is_pe
        self.sem = None
        self.cnt = 0
        self.known = {}
        self.prog = []
        self.dsems = []
        self.dcnt = []
        self.dnext = 0


class Sched:
    NDMA = 8

    def __init__(self, nc, es):
        self.nc = nc
        self.es = es
        self.q = {}
        for n in ("pe", "act", "dve", "pool", "sp"):
            q = Q(n, is_pe=(n == "pe"))
            q.sem = es.enter_context(nc.semaphore("s_" + n))
            self.q[n] = q
        for n in ("act", "pool", "sp"):
            q = self.q[n]
            for i in range(self.NDMA):
                q.dsems.append(es.enter_context(nc.semaphore("d_%s%d" % (n, i))))
                q.dcnt.append(0)
        self.nops = 0

    def _waits(self, q, deps):
        waits = []
        for tok, skip_same in deps:
            sem, val, src = tok
            if src is q and (q.is_pe or skip_same):
                continue
            key = id(sem)
            if q.known.get(key, 0) >= val:
                continue
            q.known[key] = val
            waits.append((sem, val))
        return waits

    def _deps(self, reads, writes):
        deps = []
        for b in reads:
            if b.last_w is not None:
                deps.append((b.last_w, b.excl))
        for b in writes:
            if b.last_w is not None:
                deps.append((b.last_w, b.excl))
            deps.extend((r, False) for r in b.readers)
        return deps

    def _commit(self, tok, reads, writes):
        for b in reads:
            if b.excl:
                b.last_w = tok
            else:
                b.readers.append(tok)
        for b in writes:
            b.last_w = tok
            b.readers = []

    def op(self, qn, fn, reads=(), writes=()):
        q = self.q[qn]
        waits = self._waits(q, self._deps(reads, writes))
        q.cnt += 1
        tok = (q.sem, q.cnt, q)
        q.prog.append((waits, fn, (q.sem, 1)))
        self._commit(tok, reads, writes)
        self.nops += 1
        return tok

    def dma(self, qn, fn, reads=(), writes=()):
        q = self.q[qn]
        deps = self._deps(reads, writes)
        k = q.dnext
        q.dnext = (k + 1) % len(q.dsems)
        sem = q.dsems[k]
        if q.dcnt[k] > 0:
            deps.append(((sem, q.dcnt[k], None), False))
        waits = self._waits(q, deps)
        q.dcnt[k] += 16
        tok = (sem, q.dcnt[k], None)
        q.prog.append((waits, fn, (sem, 16)))
        self._commit(tok, reads, writes)
        self.nops += 1
        return tok

    def coll(self, fn, reads=(), writes=()):
        return self.dma("pool", fn, reads, writes)

    def flush(self):
        nc = self.nc
        with nc.Block() as block:
            def replay(qn, eng):
                for waits, fn, inc in self.q[qn].prog:
                    for sem, val in waits:
                        eng.wait_ge(sem, val)
                    if fn is not None:
                        inst = fn(eng)
                        inst.then_inc(inc[0], inc[1])
                self.q[qn].prog = []

            @block.tensor
            def _(e):
                replay("pe", e)

            @block.scalar
            def _(e):
                replay("act", e)

            @block.vector
            def _(e):
                replay("dve", e)

            @block.gpsimd
            def _(e):
                replay("pool", e)

            @block.sync
            def _(e):
                replay("sp", e)

    def finalize(self, final_toks):
        q = self.q["sp"]
        waits = self._waits(q, [(t, False) for t in final_toks])
        q.prog.append((waits, None, None))
        self.flush()


EPS = 1e-6


class Ctx:
    def __init__(self, nc, es, parent=None, tag=""):
        self.nc = nc
        self.es = es
        self.tag = tag
        if parent is None:
            self.S = Sched(nc, es)
            self.psum_all = es.enter_context(nc.psum_tensor("psum_all", [128, 4096], F32))
            self.bankb = [Buf("bank%d" % i, excl=True) for i in range(8)]
        else:
            self.S = parent.S
            self.psum_all = parent.psum_all
            self.bankb = parent.bankb
        self.banks = [self.psum_all[:, 512 * i:512 * (i + 1)] for i in range(8)]

    def sb(self, name, shape, dt):
        return self.es.enter_context(self.nc.sbuf_tensor(self.tag + name, shape, dt))


class Dram:
    def __init__(self, nc, io=None):
        self.nc = nc
        self.io = io
        self.bufs = {} if io is None else io.get("_bufs", {})

    def __call__(self, name, shape, dt, kind):
        if self.io is None:
            return self.nc.dram_tensor(name, shape, dt, kind=kind).ap()
        return self.io[name]

    def buf(self, name):
        if name not in self.bufs:
            self.bufs[name] = Buf(name)
        return self.bufs[name]


def build_tok(has_proj, final, P=None, io=None, tag=""):
    standalone = P is None
    nc = bass.Bass("TRN2", target_bir_lowering=False) if standalone else P.nc
    D = Dram(nc, io)
    NT = 16
    x = D("x", [2048, 1024], F32, "ExternalInput")
    ident_d = D("ident", [128, 128], BF16, "ExternalInput")
    direct_og = standalone or (io is not None and "og" in io)
    if has_proj:
        if direct_og:
            og = D("og", [8, 128, 2048], BF16, "ExternalInput")
        wo = D("wo", [1024, 1024], F32, "ExternalInput")
    if final:
        gf = D("gf", [1024], F32, "ExternalInput")
        y = D("y", [2048, 1024], F32, "ExternalOutput")
    else:
        xo = D("xo", [2048, 1024], F32, "ExternalOutput")
        xh = D("xh", [8, 128, 2048], BF16, "ExternalOutput")
    with ExitStack() as es:
        C = Ctx(nc, es, parent=P, tag=tag)
        S = C.S
        fin = []
        ident = C.sb("identb", [128, 128], BF16)
        b_ident = Buf()
        S.dma("sp", lambda e: e.dma_start(out=ident[:], in_=ident_d[:, :]), [], [b_ident])
        if has_proj:
            if direct_og:
                def og_dma(e, dst, t0):
                    return e.dma_start(out=dst[:], in_=og[:, :, t0:t0 + 128].rearrange("h p t -> p h t"))
                og_deps = [D.buf("og")]
            else:
                ogv = io["ogfull"].rearrange("(c j p) t -> c j p t", c=8, j=4)
                oht = C.sb("oht", [128, 4], F32)
                b_oh = Buf()
                S.dma("sp", lambda e: e.dma_start(out=oht[:], in_=io["oh"][:, :]), [], [b_oh])
                cand = [[C.sb("cand%d_%d" % (k, j), [128, 8, 128], BF16) for j in range(4)] for k in range(2)]
                b_cand = [[Buf() for j in range(4)] for k in range(2)]
                og_deps = [D.buf("og")]
            wof = C.sb("wof", [128, 8, 1024], F32)
            wob = C.sb("wob", [128, 8, 1024], BF16)
            b_wof, b_wob = Buf(), Buf()
            wo_v = wo.rearrange("(h p) n -> p h n", p=128)
            for h in range(8):
                S.dma("sp" if h % 2 == 0 else "pool",
                      lambda e, h=h: e.dma_start(out=wof[:, h, :], in_=wo_v[:, h, :]), [], [b_wof])
            for h in range(8):
                S.op("pool" if h % 2 == 0 else "dve",
                     lambda e, h=h: e.tensor_copy(out=wob[:, h, :], in_=wof[:, h, :]), [b_wof], [b_wob])
        if final:
            gfb = C.sb("gfb", [128, 1024], F32)
            b_gfb = Buf()
            S.dma("pool", lambda e: e.dma_start(out=gfb[:], in_=gf.partition_broadcast(128)), [], [b_gfb])
        NB = 2
        epsb = C.sb("epsb", [128, 1], F32)
        b_eps = Buf()
        S.op("pool", lambda e: e.memset(epsb[:], EPS), [], [b_eps])
        xt = [C.sb("xt%d" % i, [128, 1024], F32) for i in range(NB)]
        b_xt = [Buf() for _ in range(NB)]
        sq = [C.sb("sq%d" % i, [128, 1024], F32) for i in range(NB)]
        b_sq = [Buf() for _ in range(NB)]
        st = [C.sb("st%d" % i, [128, 4], F32) for i in range(NB)]
        b_st = [Buf() for _ in range(NB)]
        if has_proj:
            ot = [C.sb("ot%d" % i, [128, 8, 128], BF16) for i in range(NB)]
            b_ot = [Buf() for _ in range(NB)]
            pp = [C.psum_all[:, 1024 * i:1024 * (i + 1)] for i in range(NB)]
            b_pp = [[C.bankb[2 * i], C.bankb[2 * i + 1]] for i in range(NB)]
        if final:
            yt = [C.sb("yt%d" % i, [128, 1024], F32) for i in range(NB)]
            b_yt = [Buf() for _ in range(NB)]
        else:
            xb = [C.sb("xb%d" % i, [128, 1024], BF16) for i in range(NB)]
            b_xb = [Buf() for _ in range(NB)]
            pt = [C.banks[4 + i].bitcast(BF16).rearrange("p (c t) -> p c t", c=8) for i in range(NB)]
            b_pt = [C.bankb[4 + i] for i in range(NB)]
            stg = [C.sb("stg%d" % i, [128, 8, 512], BF16) for i in range(2)]
            b_stg = [Buf() for _ in range(2)]
        for t in range(NT):
            i = t % NB
            t0 = t * 128
            S.dma("sp", lambda e, i=i, t0=t0: e.dma_start(out=xt[i][:], in_=x[t0:t0 + 128, :]), [D.buf("x")], [b_xt[i]])
            if has_proj:
                if direct_og:
                    S.dma("pool", lambda e, i=i, t0=t0: og_dma(e, ot[i], t0), og_deps, [b_ot[i]])
                else:
                    for j in range(4):
                        S.dma("pool" if j % 2 else "sp", lambda e, i=i, t0=t0, j=j: e.dma_start(
                            out=cand[i][j][:], in_=ogv[:, j, :, t0:t0 + 128].rearrange("c p t -> p c t")),
                            og_deps, [b_cand[i][j]])
                    S.op("dve", lambda e, i=i: e.tensor_scalar(out=ot[i][:], in0=cand[i][0][:], scalar1=oht[:, 0:1],
                                                               scalar2=None, op0=ALU.mult),
                         [b_cand[i][0], b_oh], [b_ot[i]])
                    for j in range(1, 4):
                        S.op("dve", lambda e, i=i, j=j: e.scalar_tensor_tensor(
                            out=ot[i][:], in0=cand[i][j][:], scalar=oht[:, j:j + 1], in1=ot[i][:],
                            op0=ALU.mult, op1=ALU.add), [b_cand[i][j], b_oh, b_ot[i]], [b_ot[i]])
                for nch in range(2):
                    for h in range(8):
                        S.op("pe", lambda e, i=i, h=h, nch=nch: e.matmul(
                            pp[i][:, nch * 512:(nch + 1) * 512], lhsT=ot[i][:, h, :],
                            rhs=wob[:, h, nch * 512:(nch + 1) * 512], start=(h == 0), stop=(h == 7)),
                            [b_ot[i], b_wob], b_pp[i])
                S.op("dve", lambda e, i=i: e.tensor_tensor(out=xt[i][:], in0=xt[i][:], in1=pp[i][:], op=ALU.add),
                     [b_xt[i]] + b_pp[i], [b_xt[i]])
            if not final:
                S.dma("pool", lambda e, i=i, t0=t0: e.dma_start(out=xo[t0:t0 + 128, :], in_=xt[i][:]), [b_xt[i]], [D.buf("xo")])
                fin.append(None)
            S.op("act", lambda e, i=i: e.activation(out=sq[i][:], in_=xt[i][:], func=AF.Square,
                                                    accum_out=st[i][:, 0:1]), [b_xt[i]], [b_sq[i], b_st[i]])
            S.op("act", lambda e, i=i: e.activation(out=st[i][:, 1:2], in_=st[i][:, 0:1], func=AF.Ln,
                                                    scale=1.0 / 1024, bias=epsb[:, 0:1]), [b_st[i], b_eps], [b_st[i]])
            S.op("act", lambda e, i=i: e.activation(out=st[i][:, 2:3], in_=st[i][:, 1:2], func=AF.Exp,
                                                    scale=-0.5), [b_st[i]], [b_st[i]])
            if final:
                S.op("dve", lambda e, i=i: e.scalar_tensor_tensor(out=yt[i][:], in0=xt[i][:], scalar=st[i][:, 2:3],
                                                                   in1=gfb[:], op0=ALU.mult, op1=ALU.mult),
                     [b_xt[i], b_st[i], b_gfb], [b_yt[i]])
                fin.append(S.dma("sp", lambda e, i=i, t0=t0: e.dma_start(out=y[t0:t0 + 128, :], in_=yt[i][:]),
                                 [b_yt[i]], [D.buf("y")]))
            else:
                S.op("act", lambda e, i=i: e.activation(out=xb[i][:], in_=xt[i][:], func=AF.Copy,
                                                        scale=st[i][:, 2:3]), [b_xt[i], b_st[i]], [b_xb[i]])
                for dc in range(8):
                    S.op("pe", lambda e, i=i, dc=dc: e.transpose(pt[i][:, dc, :], xb[i][:, dc * 128:(dc + 1) * 128],
                                                                 ident[:]), [b_xb[i], b_ident], [b_pt[i]])
                g = (t // 4) % 2
                tq = t % 4
                S.op("dve", lambda e, i=i, g=g, tq=tq: e.tensor_copy(out=stg[g][:, :, tq * 128:(tq + 1) * 128],
                                                                     in_=pt[i][:]), [b_pt[i]], [b_stg[g]])
                if tq == 3:
                    tb = (t // 4) * 512
                    fin.append(S.dma("sp", lambda e, g=g, tb=tb: e.dma_start(
                        out=xh[:, :, tb:tb + 512].rearrange("c p t -> p c t"), in_=stg[g][:]), [b_stg[g]], [D.buf("xh")]))
        fin = [f for f in fin if f is not None]
        if standalone:
            S.finalize(fin)
        else:
            S.flush()
    return nc if standalone else fin


def roundrobin(gens):
    gens = list(gens)
    while gens:
        nxt = []
        for g in gens:
            try:
                next(g)
                nxt.append(g)
            except StopIteration:
                pass
        gens = nxt


def build_gdn(ntiles=32, P=None, io=None, tag=""):
    standalone = P is None
    nc = bass.Bass("TRN2", target_bir_lowering=False) if standalone else P.nc
    D = Dram(nc, io)
    xh = D("xh", [4, 8, 128, 2048], BF16, "ExternalInput")
    wqkv = D("wqkv", [1024, 768], F32, "ExternalInput")
    wzab = D("wzab", [1024, 260], F32, "ExternalInput")
    gn = D("gn", [128, 8], F32, "ExternalInput")
    cw = D("cw", [128, 24], F32, "ExternalInput")
    hp = D("hp", [128, 4], F32, "ExternalInput")
    ong = D("ong", [128, 256], F32, "ExternalInput")
    ident_d = D("ident", [128, 128], BF16, "ExternalInput")
    cf_d = D("cf", [128, 4, 128], F32, "ExternalInput")
    cm_d = D("cm", [128, 10, 128], BF16, "ExternalInput")
    og = D("og", [2, 4, 128, 2048], BF16, "ExternalOutput")
    with ExitStack() as es:
        C = Ctx(nc, es, parent=P, tag=tag)
        S = C.S
        fin = []
        identb = C.sb("identb", [128, 128], BF16)
        cf = C.sb("cfs", [128, 4, 128], F32)
        gnt = C.sb("gnt", [128, 8], F32)
        cwt = C.sb("cwt", [128, 24], F32)
        hpt = C.sb("hpt", [128, 4], F32)
        ongt = C.sb("ongt", [128, 256], F32)
        b_const = Buf()
        for dst, src in ((identb, ident_d), (gnt, gn), (cwt, cw), (hpt, hp), (ongt, ong)):
            S.dma("sp", lambda e, dst=dst, src=src: e.dma_start(out=dst[:], in_=src[:, :]), [], [b_const])
        S.dma("sp", lambda e: e.dma_start(out=cf[:], in_=cf_d[:, :, :]), [], [b_const])
        cmt = C.sb("cmt", [128, 10, 128], BF16)
        S.dma("sp", lambda e: e.dma_start(out=cmt[:], in_=cm_d[:, :, :]), [], [b_const])
        Uf = cf[:, 0, :]
        NMA = cf[:, 1, :]
        NMQ = cf[:, 2, :]
        identf = cf[:, 3, :]
        onesf = C.sb("onesf", [128, 128], F32)
        onesb = C.sb("onesb", [128, 128], BF16)
        misc = C.sb("misc", [128, 8], F32)
        S.op("pool", lambda e: e.memset(onesf[:], 1.0), [], [b_const])
        S.op("pool", lambda e: e.memset(onesb[:], 1.0), [], [b_const])
        S.op("pool", lambda e: e.memset(misc[:, 0:1], EPS), [], [b_const])
        S.op("pool", lambda e: e.memset(misc[:, 1:2], 128 * EPS), [], [b_const])
        S.op("pool", lambda e: e.memset(misc[:, 2:3], 1.0), [], [b_const])
        S.op("act", lambda e: e.activation(out=misc[:, 5:7], in_=hpt[:, 0:2], func=AF.Exp), [b_const], [b_const])
        S.op("dve", lambda e: e.tensor_scalar(out=misc[:, 3:5], in0=misc[:, 5:7], scalar1=-1.0, scalar2=None,
                                              op0=ALU.mult), [b_const], [b_const])
        dg = C.sb("dg", [128, 24, 128], BF16)
        for k in range(24):
            S.op("dve" if k % 2 else "pool", lambda e, k=k: e.tensor_scalar(
                out=dg[:, k, :], in0=identf, scalar1=cwt[:, k:k + 1], scalar2=None, op0=ALU.mult),
                [b_const], [b_const])
        wq = C.sb("wq", [128, 8, 768], BF16)
        wz = C.sb("wz", [128, 8, 260], BF16)
        wst = [C.sb("wst%d" % i, [128, 1028], F32) for i in range(2)]
        b_wst = [Buf(), Buf()]
        b_w = Buf()
        for dc in range(8):
            i = dc % 2
            S.dma("sp", lambda e, i=i, dc=dc: e.dma_start(out=wst[i][:, 0:768], in_=wqkv[dc * 128:(dc + 1) * 128, :]),
                  [], [b_wst[i]])
            S.dma("pool", lambda e, i=i, dc=dc: e.dma_start(out=wst[i][:, 768:1028], in_=wzab[dc * 128:(dc + 1) * 128, :]),
                  [], [b_wst[i]])
            S.op("dve", lambda e, i=i, dc=dc: e.tensor_scalar(out=wq[:, dc, :], in0=wst[i][:, 0:768],
                                                              scalar1=gnt[:, dc:dc + 1], scalar2=None, op0=ALU.mult),
                 [b_wst[i], b_const], [b_w])
            S.op("dve", lambda e, i=i, dc=dc: e.tensor_scalar(out=wz[:, dc, :], in0=wst[i][:, 768:1028],
                                                              scalar1=gnt[:, dc:dc + 1], scalar2=None, op0=ALU.mult),
                 [b_wst[i], b_const], [b_w])
        TW = 256
        xt = [C.sb("xt%d" % i, [128, 8, TW], BF16) for i in range(2)]
        b_xt = [Buf(), Buf()]
        Pc = [[C.sb("Pc%d_%d" % (g, i), [128, TW + 3], BF16) for i in range(2)] for g in range(6)]
        b_Pc = [[Buf(), Buf()] for g in range(6)]
        for g in range(6):
            S.op("pool", lambda e, g=g: e.memset(Pc[g][0][:, 0:3], 0.0), [], [b_Pc[g][0]])
        s1 = [[C.sb("s1_%d_%d" % (g, i), [128, TW], BF16) for i in range(2)] for g in range(6)]
        b_s1 = [[Buf(), Buf()] for g in range(6)]
        qn = [[C.sb("qn_%d_%d" % (g, i), [128, TW], BF16) for i in range(2)] for g in range(4)]
        b_qn = [[Buf(), Buf()] for g in range(4)]
        sqb = [C.sb("sqb%d" % i, [128, TW], BF16) for i in range(2)]
        b_sqb = [Buf(), Buf()]
        lnb = [C.sb("lnb%d" % i, [128, TW], F32) for i in range(2)]
        b_lnb = [Buf(), Buf()]
        banks = C.banks

        def reg(b, lo, hi, bf=False):
            ap = banks[b][:, lo:hi]
            return ap.bitcast(BF16) if bf else ap

        bankb = C.bankb
        pj = [reg(0, 0, 256), reg(1, 0, 256)]
        b_pj = [bankb[0], bankb[1]]
        pc = [reg(0, 256, 512), reg(1, 256, 512)]
        b_pc = [bankb[0], bankb[1]]
        pz = reg(2, 0, 260)
        b_pz = bankb[2]
        gz = [C.sb("gz%d" % i, [128, 256], F32) for i in range(4)]
        b_gz = [Buf() for _ in range(4)]
        sz = [C.sb("sz%d" % i, [128, 256], F32) for i in range(2)]
        b_sz = [Buf(), Buf()]
        gt = [C.sb("gt%d" % i, [128, 16], F32) for i in range(4)]
        b_gt = [Buf() for _ in range(4)]
        X1 = [reg(3 + 2 * h, 0, 256) for h in range(2)]
        X5 = [reg(3 + 2 * h, 256, 512) for h in range(2)]
        X2 = [reg(4 + 2 * h, 0, 256) for h in range(2)]
        X3 = [reg(4 + 2 * h, 256, 384) for h in range(2)]
        X4 = [reg(4 + 2 * h, 384, 512, True) for h in range(2)]
        X3b = [reg(7, 64 * h, 64 * h + 64, True) for h in range(2)]
        XT = [reg(7, 128 + 64 * h, 192 + 64 * h, True) for h in range(2)]
        b_X1 = [bankb[3], bankb[5]]
        b_X5a = b_X1
        b_X5b = b_X1
        b_X2 = [bankb[4], bankb[6]]
        b_X3 = b_X2
        b_X4 = b_X2
        b_X3b = [bankb[7], bankb[7]]
        b_XT = [bankb[7], bankb[7]]

        def hs(name, shape, dt):
            return [C.sb("%s_%d" % (name, h), shape, dt) for h in range(2)]

        gb = hs("gb", [128, 128], F32); b_gb = [Buf(), Buf()]
        Rsb = hs("Rsb", [128, 128], F32); b_Rsb = [Buf(), Buf()]
        egb = hs("egb", [128, 128], F32); b_egb = [Buf(), Buf()]
        gc2 = hs("gc2", [128, 8], F32); b_gc2 = [Buf(), Buf()]
        RA = hs("RA", [128, 128], F32); b_RA = [Buf(), Buf()]
        RQ = hs("RQ", [128, 128], F32); b_RQ = [Buf(), Buf()]
        DA = hs("DA", [128, 128], F32); b_DA = [Buf(), Buf()]
        DQ = hs("DQ", [128, 128], F32); b_DQ = [Buf(), Buf()]
        QKT = hs("QKT", [128, 128], BF16); b_QKT = [Buf(), Buf()]
        YZ = [[C.sb("YZ_%d_%d" % (h, i), [128, 2, 128], BF16) for i in range(2)] for h in range(2)]
        b_YZ = [[Buf(), Buf()] for h in range(2)]
        PQ = [[C.sb("PQ_%d_%d" % (h, i), [128, 2, 128], BF16) for i in range(2)] for h in range(2)]
        b_PQ = [[Buf(), Buf()] for h in range(2)]
        AB = hs("AB", [128, 2, 128], BF16); b_AB = [Buf(), Buf()]
        Wm = hs("Wm", [128, 2, 128], BF16); b_Wm = [Buf(), Buf()]
        kbg = hs("kbg", [128, 128], BF16); b_kbg = [Buf(), Buf()]
        kdec = hs("kdec", [128, 128], BF16); b_kdec = [Buf(), Buf()]
        vb = hs("vb", [128, 128], BF16); b_vb = [Buf(), Buf()]
        Usb = hs("Usb", [128, 128], F32); b_Usb = [Buf(), Buf()]
        wT = hs("wT", [128, 128], BF16); b_wT = [Buf(), Buf()]
        qdT = hs("qdT", [128, 128], BF16); b_qdT = [Buf(), Buf()]
        vnew = hs("vnew", [128, 128], BF16); b_vnew = [Buf(), Buf()]
        Sf = hs("Sf", [128, 128], F32); b_Sf = [Buf(), Buf()]
        Sb = hs("Sb", [128, 128], BF16); b_Sb = [Buf(), Buf()]
        junk = hs("junk", [128, 128], F32); b_junk = [Buf(), Buf()]
        ost = hs("ost", [128, 8], F32); b_ost = [Buf(), Buf()]
        ogt = hs("ogt", [128, 128], BF16); b_ogt = [Buf(), Buf()]
        ostg = [[C.sb("ostg_%d_%d" % (h, i), [128, 512], BF16) for i in range(2)] for h in range(2)]
        b_ostg = [[Buf(), Buf()] for h in range(2)]
        for h in range(2):
            S.op("pool", lambda e, h=h: e.memset(Sf[h][:], 0.0), [], [b_Sf[h]])
            S.op("pool", lambda e, h=h: e.memset(Sb[h][:], 0.0), [], [b_Sb[h]])

        def chunk_head(h, ti, ci, cg):
            cs = slice(ci * 128, (ci + 1) * 128)
            kT = qn[2 + h][ti][:, cs]
            qT = qn[h][ti][:, cs]
            vT = s1[4 + h][ti][:, cs]
            r_kT = [b_qn[2 + h][ti]]
            r_qT = [b_qn[h][ti]]
            r_vT = [b_s1[4 + h][ti]]
            g_ = gt[cg % 4]
            bg = b_gt[cg % 4]
            gcol = g_[:, 6 + h:7 + h]
            beta = g_[:, 12 + h:13 + h]
            S.op("dve", lambda e: e.tensor_scalar(out=gb[h][:], in0=onesf[:], scalar1=gcol, scalar2=None, op0=ALU.mult),
                 [bg, b_const], [b_gb[h]])
            S.op("pe", lambda e: e.matmul(X1[h][:, 0:128], lhsT=gb[h][:], rhs=Uf, start=True, stop=True),
                 [b_gb[h], b_const], [b_X1[h]])
            S.op("pe", lambda e: e.matmul(X1[h][:, 128:129], lhsT=Uf, rhs=gcol, start=True, stop=True),
                 [bg, b_const], [b_X1[h]])
            yield
            S.op("act", lambda e: e.activation(out=Rsb[h][:], in_=X1[h][:, 0:128], func=AF.Copy), [b_X1[h]], [b_Rsb[h]])
            S.op("act", lambda e: e.activation(out=egb[h][:], in_=X1[h][:, 0:128], func=AF.Exp), [b_X1[h]], [b_egb[h]])
            S.op("dve", lambda e: e.tensor_copy(out=gc2[h][:, 0:1], in_=X1[h][:, 128:129]), [b_X1[h]], [b_gc2[h]])
            S.op("dve", lambda e: e.tensor_scalar(out=gc2[h][:, 1:2], in0=X1[h][:, 128:129], scalar1=-1.0, scalar2=None,
                                                  op0=ALU.mult), [b_X1[h]], [b_gc2[h]])
            S.op("act", lambda e: e.activation(out=gc2[h][:, 2:3], in_=X1[h][:, 128:129], func=AF.Exp),
                 [b_X1[h]], [b_gc2[h]])
            S.op("dve", lambda e: e.tensor_tensor(out=gc2[h][:, 3:4], in0=gc2[h][:, 2:3], in1=beta, op=ALU.mult),
                 [b_gc2[h], bg], [b_gc2[h]])
            yield
            S.op("pool", lambda e: e.tensor_tensor(out=RA[h][:], in0=Rsb[h][:], in1=NMA, op=ALU.add),
                 [b_Rsb[h], b_const], [b_RA[h]])
            S.op("pool", lambda e: e.tensor_tensor(out=RQ[h][:], in0=Rsb[h][:], in1=NMQ, op=ALU.add),
                 [b_Rsb[h], b_const], [b_RQ[h]])
            S.op("pe", lambda e: e.matmul(X2[h][:, 0:128], lhsT=kT, rhs=kT, start=True, stop=True), r_kT, [b_X2[h]])
            S.op("pe", lambda e: e.matmul(X2[h][:, 128:256], lhsT=kT, rhs=qT, start=True, stop=True),
                 r_kT + r_qT, [b_X2[h]])
            yield
            S.op("act", lambda e: e.activation(out=DA[h][:], in_=RA[h][:], func=AF.Exp, scale=-1.0, bias=gc2[h][:, 0:1]),
                 [b_RA[h], b_gc2[h]], [b_DA[h]])
            S.op("act", lambda e: e.activation(out=DQ[h][:], in_=RQ[h][:], func=AF.Exp, scale=1.0, bias=gc2[h][:, 1:2]),
                 [b_RQ[h], b_gc2[h]], [b_DQ[h]])
            yield
            Abf = AB[h][:, 0, :]
            Bbf = AB[h][:, 1, :]
            S.op("dve", lambda e: e.scalar_tensor_tensor(out=Abf, in0=X2[h][:, 0:128], scalar=beta,
                                                         in1=DA[h][:], op0=ALU.mult, op1=ALU.mult),
                 [b_X2[h], bg, b_DA[h]], [b_AB[h]])
            S.op("dve", lambda e: e.tensor_tensor(out=QKT[h][:], in0=X2[h][:, 128:256], in1=DQ[h][:], op=ALU.mult),
                 [b_X2[h], b_DQ[h]], [b_QKT[h]])
            S.op("pe", lambda e: e.transpose(X3b[h][:], Abf, identb[:]), [b_AB[h], b_const], [b_X3b[h]])
            yield
            S.op("act", lambda e: e.activation(out=Bbf, in_=X3b[h][:], func=AF.Copy), [b_X3b[h]], [b_AB[h]])
            S.op("pe", lambda e: e.transpose(X4[h][:, 0:128], kT, identb[:]), r_kT + [b_const], [b_X4[h]])
            S.op("pe", lambda e: e.transpose(X4[h][:, 128:256], vT, identb[:]), r_vT + [b_const], [b_X4[h]])
            yield
            yz0 = YZ[h][0]
            pq0 = PQ[h][0]
            S.op("pool", lambda e: e.tensor_tensor(out=yz0[:, 1, :], in0=Abf, in1=cmt[:, 0, :], op=ALU.mult),
                 [b_AB[h], b_const], [b_YZ[h][0]])
            S.op("pool", lambda e: e.tensor_tensor(out=yz0[:, 0, :], in0=Bbf, in1=cmt[:, 1, :], op=ALU.mult),
                 [b_AB[h], b_const], [b_YZ[h][0]])
            S.op("pool", lambda e: e.tensor_tensor(out=pq0[:, 0, :], in0=identb[:], in1=yz0[:, 0, :], op=ALU.subtract),
                 [b_YZ[h][0], b_const], [b_PQ[h][0]])
            S.op("pool", lambda e: e.tensor_tensor(out=pq0[:, 1, :], in0=identb[:], in1=yz0[:, 1, :], op=ALU.subtract),
                 [b_YZ[h][0], b_const], [b_PQ[h][0]])
            S.op("act", lambda e: e.activation(out=kbg[h][:], in_=X4[h][:, 0:128], func=AF.Copy, scale=gc2[h][:, 3:4]),
                 [b_X4[h], b_gc2[h]], [b_kbg[h]])
            S.op("dve", lambda e: e.tensor_scalar(out=kdec[h][:], in0=X4[h][:, 0:128], scalar1=DQ[h][:, 127:128],
                                                  scalar2=None, op0=ALU.mult), [b_X4[h], b_DQ[h]], [b_kdec[h]])
            S.op("act", lambda e: e.activation(out=vb[h][:], in_=X4[h][:, 128:256], func=AF.Copy, scale=beta),
                 [b_X4[h], bg], [b_vb[h]])
            S.op("pool", lambda e: e.tensor_tensor(out=qdT[h][:], in0=qT, in1=egb[h][:], op=ALU.mult),
                 r_qT + [b_egb[h]], [b_qdT[h]])
            yield
            yz = 0
            pm = 0
            for lev in range(2):
                cur = YZ[h][yz]
                S.op("pe", lambda e, cur=cur: e.matmul(X1[h][:, 0:128], lhsT=cur[:, 1, :], rhs=cur[:, 0, :],
                                                       start=True, stop=True), [b_YZ[h][yz]], [b_X1[h]])
                S.op("pe", lambda e, cur=cur: e.matmul(X1[h][:, 128:256], lhsT=cur[:, 0, :], rhs=cur[:, 1, :],
                                                       start=True, stop=True), [b_YZ[h][yz]], [b_X1[h]])
                yield
                nyz = 1 - yz
                nxt = YZ[h][nyz]
                S.op("act", lambda e, nxt=nxt: e.activation(out=nxt[:, 0, :], in_=X1[h][:, 0:128], func=AF.Copy),
                     [b_X1[h]], [b_YZ[h][nyz]])
                S.op("dve", lambda e, nxt=nxt: e.tensor_copy(out=nxt[:, 1, :], in_=X1[h][:, 128:256]),
                     [b_X1[h]], [b_YZ[h][nyz]])
                yz = nyz
                yield
                pq = PQ[h][pm]
                S.op("pe", lambda e, nxt=nxt, pq=pq: e.matmul(X3[h][:], lhsT=nxt[:, 1, :], rhs=pq[:, 0, :],
                                                              start=True, stop=True),
                     [b_YZ[h][yz], b_PQ[h][pm]], [b_X3[h]])
                S.op("pe", lambda e, nxt=nxt, pq=pq: e.matmul(X2[h][:, 0:128], lhsT=nxt[:, 0, :], rhs=pq[:, 1, :],
                                                              start=True, stop=True),
                     [b_YZ[h][yz], b_PQ[h][pm]], [b_X2[h]])
                yield
                npm = 1 - pm
                npq = PQ[h][npm]
                S.op("dve", lambda e, pq=pq, npq=npq: e.tensor_tensor(out=npq[:, 0, :], in0=pq[:, 0, :], in1=X3[h][:],
                                                                      op=ALU.add),
                     [b_PQ[h][pm], b_X3[h]], [b_PQ[h][npm]])
                S.op("dve", lambda e, pq=pq, npq=npq: e.tensor_tensor(out=npq[:, 1, :], in0=pq[:, 1, :],
                                                                      in1=X2[h][:, 0:128], op=ALU.add),
                     [b_PQ[h][pm], b_X2[h]], [b_PQ[h][npm]])
                pm = npm
                yield
            for m in range(4):
                lastm = m == 3
                pq = PQ[h][pm]
                S.op("pe", lambda e, pq=pq: e.matmul(X1[h][:, 0:128], lhsT=Abf, rhs=pq[:, 0, :], start=True, stop=True),
                     [b_AB[h], b_PQ[h][pm]], [b_X1[h]])
                if not lastm:
                    S.op("pe", lambda e, pq=pq: e.matmul(X1[h][:, 128:256], lhsT=Bbf, rhs=pq[:, 1, :],
                                                         start=True, stop=True), [b_AB[h], b_PQ[h][pm]], [b_X1[h]])
                yield
                S.op("dve", lambda e, m=m: e.tensor_tensor(out=Wm[h][:, 0, :], in0=X1[h][:, 0:128],
                                                           in1=cmt[:, 3 + 2 * m, :], op=ALU.mult),
                     [b_X1[h], b_const], [b_Wm[h]])
                if not lastm:
                    S.op("dve", lambda e, m=m: e.tensor_tensor(out=Wm[h][:, 1, :], in0=X1[h][:, 128:256],
                                                               in1=cmt[:, 2 + 2 * m, :], op=ALU.mult),
                         [b_X1[h], b_const], [b_Wm[h]])
                yield
                S.op("pe", lambda e, pq=pq: e.matmul(X3[h][:], lhsT=pq[:, 1, :], rhs=Wm[h][:, 0, :], start=True, stop=True),
                     [b_PQ[h][pm], b_Wm[h]], [b_X3[h]])
                if not lastm:
                    S.op("pe", lambda e, pq=pq: e.matmul(X2[h][:, 0:128], lhsT=pq[:, 0, :], rhs=Wm[h][:, 1, :],
                                                         start=True, stop=True), [b_PQ[h][pm], b_Wm[h]], [b_X2[h]])
                yield
                npm = 1 - pm
                npq = PQ[h][npm]
                S.op("dve", lambda e, pq=pq, npq=npq: e.tensor_tensor(out=npq[:, 0, :], in0=pq[:, 0, :], in1=X3[h][:],
                                                                      op=ALU.subtract),
                     [b_PQ[h][pm], b_X3[h]], [b_PQ[h][npm]])
                if not lastm:
                    S.op("dve", lambda e, pq=pq, npq=npq: e.tensor_tensor(out=npq[:, 1, :], in0=pq[:, 1, :],
                                                                          in1=X2[h][:, 0:128], op=ALU.subtract),
                         [b_PQ[h][pm], b_X2[h]], [b_PQ[h][npm]])
                pm = npm
                yield
            TT = PQ[h][pm][:, 0, :]
            bTT = b_PQ[h][pm]
            S.op("pe", lambda e: e.matmul(X2[h][:, 0:128], lhsT=TT, rhs=vb[h][:], start=True, stop=True),
                 [bTT, b_vb[h]], [b_X2[h]])
            S.op("pe", lambda e: e.matmul(X2[h][:, 128:256], lhsT=kbg[h][:], rhs=TT, start=True, stop=True),
                 [bTT, b_kbg[h]], [b_X2[h]])
            yield
            S.op("act", lambda e: e.activation(out=Usb[h][:], in_=X2[h][:, 0:128], func=AF.Copy), [b_X2[h]], [b_Usb[h]])
            S.op("dve", lambda e: e.tensor_copy(out=wT[h][:], in_=X2[h][:, 128:256]), [b_X2[h]], [b_wT[h]])
            yield
            S.op("pe", lambda e: e.matmul(X5[h][:, 0:128], lhsT=wT[h][:], rhs=Sb[h][:], start=True, stop=True),
                 [b_wT[h], b_Sb[h]], [b_X5a[h]])
            yield
            S.op("dve", lambda e: e.tensor_tensor(out=vnew[h][:], in0=Usb[h][:], in1=X5[h][:, 0:128], op=ALU.subtract),
                 [b_Usb[h], b_X5a[h]], [b_vnew[h]])
            S.op("pe", lambda e: e.matmul(X5[h][:, 128:256], lhsT=qdT[h][:], rhs=Sb[h][:], start=True, stop=False),
                 [b_qdT[h], b_Sb[h]], [b_X5b[h]])
            yield
            S.op("pe", lambda e: e.matmul(X5[h][:, 128:256], lhsT=QKT[h][:], rhs=vnew[h][:], start=False, stop=True),
                 [b_QKT[h], b_vnew[h]], [b_X5b[h]])
            S.op("pe", lambda e: e.matmul(X5[h][:, 0:128], lhsT=kdec[h][:], rhs=vnew[h][:], start=True, stop=True),
                 [b_kdec[h], b_vnew[h]], [b_X5a[h]])
            yield
            S.op("dve", lambda e: e.scalar_tensor_tensor(out=Sf[h][:], in0=Sf[h][:], scalar=egb[h][:, 127:128],
                                                         in1=X5[h][:, 0:128], op0=ALU.mult, op1=ALU.add),
                 [b_Sf[h], b_egb[h], b_X5a[h]], [b_Sf[h]])
            S.op("act", lambda e: e.activation(out=junk[h][:], in_=X5[h][:, 128:256], func=AF.Square,
                                               accum_out=ost[h][:, 0:1]), [b_X5b[h]], [b_junk[h], b_ost[h]])
            yield
            S.op("act", lambda e: e.activation(out=Sb[h][:], in_=Sf[h][:], func=AF.Copy), [b_Sf[h]], [b_Sb[h]])
            S.op("act", lambda e: e.activation(out=ost[h][:, 1:2], in_=ost[h][:, 0:1], func=AF.Ln, scale=1.0 / 128,
                                               bias=misc[:, 0:1]), [b_ost[h], b_const], [b_ost[h]])
            S.op("act", lambda e: e.activation(out=ost[h][:, 2:3], in_=ost[h][:, 1:2], func=AF.Exp, scale=-0.5),
                 [b_ost[h]], [b_ost[h]])
            yield
            S.op("dve", lambda e: e.scalar_tensor_tensor(out=ogt[h][:], in0=X5[h][:, 128:256], scalar=ost[h][:, 2:3],
                                                         in1=gz[cg % 4][:, h * 128:(h + 1) * 128], op0=ALU.mult,
                                                         op1=ALU.mult),
                 [b_X5b[h], b_ost[h], b_gz[cg % 4]], [b_ogt[h]])
            S.op("pe", lambda e: e.transpose(XT[h][:], ogt[h][:], identb[:]), [b_ogt[h], b_const], [b_XT[h]])
            yield
            sg = (cg // 4) % 2
            sp_ = cg % 4
            S.op("act", lambda e: e.activation(out=ostg[h][sg][:, sp_ * 128:(sp_ + 1) * 128], in_=XT[h][:], func=AF.Copy),
                 [b_XT[h]], [b_ostg[h][sg]])
            if sp_ == 3:
                tb = (cg // 4) * 512
                fin.append(S.dma("sp", lambda e: e.dma_start(out=og[h, tb // 2048, :, tb % 2048:tb % 2048 + 512],
                                                             in_=ostg[h][sg][:]), [b_ostg[h][sg]], [D.buf("og")]))

        def stage1(t):
            ti = t % 2
            rank = (t * TW) // 2048
            toff = (t * TW) % 2048
            S.dma("sp" if t % 2 == 0 else "pool", lambda e: e.dma_start(
                out=xt[ti][:], in_=xh[rank, :, :, toff:toff + TW].rearrange("c p t -> p c t")), [D.buf("xh")], [b_xt[ti]])

            def proj(g):
                pi = g % 2
                for dc in range(8):
                    S.op("pe", lambda e, dc=dc: e.matmul(pj[pi][:], lhsT=wq[:, dc, g * 128:(g + 1) * 128],
                                                         rhs=xt[ti][:, dc, :], start=(dc == 0), stop=(dc == 7)),
                         [b_w, b_xt[ti]], [b_pj[pi]])
                if t > 0:
                    S.op("pool", lambda e: e.tensor_copy(out=Pc[g][ti][:, 0:3], in_=Pc[g][1 - ti][:, TW:TW + 3]),
                         [b_Pc[g][1 - ti]], [b_Pc[g][ti]])
                S.op("act", lambda e: e.activation(out=Pc[g][ti][:, 3:TW + 3], in_=pj[pi][:], func=AF.Copy),
                     [b_pj[pi]], [b_Pc[g][ti]])

            def conv(g):
                pi = g % 2
                for j in range(4):
                    S.op("pe", lambda e, j=j: e.matmul(pc[pi][:], lhsT=dg[:, g * 4 + j, :],
                                                       rhs=Pc[g][ti][:, j:j + TW], start=(j == 0), stop=(j == 3)),
                         [b_const, b_Pc[g][ti]], [b_pc[pi]])
                S.op("act", lambda e: e.activation(out=s1[g][ti][:], in_=pc[pi][:], func=AF.Silu),
                     [b_pc[pi]], [b_s1[g][ti]])

            def zab_a(ci):
                gi = (t * 2 + ci) % 4
                for dc in range(8):
                    S.op("pe", lambda e, dc=dc: e.matmul(pz[:], lhsT=xt[ti][:, dc, ci * 128:(ci + 1) * 128],
                                                         rhs=wz[:, dc, :], start=(dc == 0), stop=(dc == 7)),
                         [b_w, b_xt[ti]], [b_pz])
                S.op("act", lambda e: e.activation(out=sz[ci][:], in_=pz[:, 0:256], func=AF.Silu), [b_pz], [b_sz[ci]])
                S.op("dve", lambda e: e.tensor_tensor(out=gt[gi][:, 0:2], in0=pz[:, 256:258], in1=hpt[:, 2:4],
                                                      op=ALU.add), [b_pz, b_const], [b_gt[gi]])
                S.op("dve", lambda e: e.tensor_copy(out=gt[gi][:, 8:10], in_=pz[:, 258:260]), [b_pz], [b_gt[gi]])
                S.op("pool", lambda e: e.tensor_tensor(out=gz[gi][:], in0=sz[ci][:], in1=ongt[:], op=ALU.mult),
                     [b_sz[ci], b_const], [b_gz[gi]])

            def zab_b(ci):
                gi = (t * 2 + ci) % 4
                S.op("act", lambda e: e.activation(out=gt[gi][:, 2:4], in_=gt[gi][:, 0:2], func=AF.Exp),
                     [b_gt[gi]], [b_gt[gi]])
                S.op("act", lambda e: e.activation(out=gt[gi][:, 10:12], in_=gt[gi][:, 8:10], func=AF.Exp, scale=-1.0),
                     [b_gt[gi]], [b_gt[gi]])

            def zab_c(ci):
                gi = (t * 2 + ci) % 4
                S.op("act", lambda e: e.activation(out=gt[gi][:, 4:6], in_=gt[gi][:, 2:4], func=AF.Ln,
                                                   bias=misc[:, 2:3]), [b_gt[gi], b_const], [b_gt[gi]])
                S.op("dve", lambda e: e.tensor_scalar(out=gt[gi][:, 10:12], in0=gt[gi][:, 10:12], scalar1=1.0,
                                                      scalar2=None, op0=ALU.add), [b_gt[gi]], [b_gt[gi]])

            def zab_d(ci):
                gi = (t * 2 + ci) % 4
                S.op("dve", lambda e: e.tensor_tensor(out=gt[gi][:, 6:8], in0=gt[gi][:, 4:6], in1=misc[:, 3:5],
                                                      op=ALU.mult), [b_gt[gi], b_const], [b_gt[gi]])
                S.op("dve", lambda e: e.reciprocal(out=gt[gi][:, 12:14], in_=gt[gi][:, 10:12]), [b_gt[gi]], [b_gt[gi]])

            def l2n_a(g):
                pi = g % 2
                isq = g < 2
                S.op("act", lambda e: e.activation(out=sqb[pi][:], in_=s1[g][ti][:], func=AF.Square,
                                                   scale=(float(np.sqrt(128.0)) if isq else 1.0)),
                     [b_s1[g][ti]], [b_sqb[pi]])
                S.op("pe", lambda e: e.matmul(pc[pi][:], lhsT=onesb[:], rhs=sqb[pi][:], start=True, stop=True),
                     [b_const, b_sqb[pi]], [b_pc[pi]])

            def l2n_b(g):
                pi = g % 2
                isq = g < 2
                S.op("act", lambda e: e.activation(out=lnb[pi][:], in_=pc[pi][:], func=AF.Ln,
                                                   bias=(misc[:, 1:2] if isq else misc[:, 0:1])),
                     [b_pc[pi], b_const], [b_lnb[pi]])
                S.op("act", lambda e: e.activation(out=lnb[pi][:], in_=lnb[pi][:], func=AF.Exp, scale=-0.5),
                     [b_lnb[pi]], [b_lnb[pi]])

            def l2n_c(g):
                pi = g % 2
                S.op("dve", lambda e: e.tensor_tensor(out=qn[g][ti][:], in0=s1[g][ti][:], in1=lnb[pi][:], op=ALU.mult),
                     [b_s1[g][ti], b_lnb[pi]], [b_qn[g][ti]])

            for g in range(6):
                proj(g)
                yield
            for g in range(6):
                conv(g)
            for ci in range(2):
                zab_a(ci)
            yield
            for f in (zab_b, zab_c, zab_d):
                for ci in range(2):
                    f(ci)
                yield
            for g in range(4):
                l2n_a(g)
                yield
                l2n_b(g)
                yield
                l2n_c(g)
                yield

        def drive(chains, bg):
            chains = list(chains)
            while chains:
                nxt = []
                for g in chains:
                    try:
                        next(g)
                        nxt.append(g)
                    except StopIteration:
                        pass
                chains = nxt
                if bg is not None:
                    try:
                        next(bg)
                    except StopIteration:
                        bg = None
            return bg

        for _ in stage1(0):
            pass
        for t in range(ntiles):
            bg = stage1(t + 1) if t + 1 < ntiles else None
            for ci in range(2):
                cg = t * 2 + ci
                bg = drive([chunk_head(0, t % 2, ci, cg), chunk_head(1, t % 2, ci, cg)], bg)
            if bg is not None:
                for _ in bg:
                    pass
        if standalone:
            S.finalize(fin)
        else:
            S.flush()
    return nc if standalone else fin


import ml_dtypes
NPBF = ml_dtypes.bfloat16


def _consts():
    i = np.arange(128)
    U = (i[:, None] <= i[None, :]).astype(np.float32)
    NMA = np.where(i[None, :] >= i[:, None], 30000.0, 0.0).astype(np.float32)
    NMQ = np.where(i[None, :] < i[:, None], -30000.0, 0.0).astype(np.float32)
    I = np.eye(128, dtype=np.float32)
    cf = np.ascontiguousarray(np.stack([U, NMA, NMQ, I], axis=1))
    ms = []
    bi = i[:, None]
    bj = i[None, :]
    m8 = ((bi // 8) == (bj // 8)).astype(np.float32)
    ms += [m8, m8.T]
    for m in range(4):
        sz = 8 << m
        ml = (((bi // sz) == (bj // sz) + 1) & ((bi // (2 * sz)) == (bj // (2 * sz)))).astype(np.float32)
        ms += [ml, ml.T]
    cm = np.ascontiguousarray(np.stack(ms, axis=1)).astype(NPBF)
    return {"ident": I.astype(NPBF), "cf": cf, "cm": cm}


def gdn_inputs(inp, layer, r):
    h0, h1 = 2 * r, 2 * r + 1
    w = inp["a_w_in"][layer]
    cols = []
    for base in (0, 1024, 2048):
        for h in (h0, h1):
            cols.append(np.arange(base + h * 128, base + (h + 1) * 128))
    qkv_cols = np.concatenate(cols)
    zcols = np.concatenate([np.arange(3072 + h * 128, 3072 + (h + 1) * 128) for h in (h0, h1)])
    ab_cols = np.array([4096 + h0, 4096 + h1, 4104 + h0, 4104 + h1])
    wqkv = np.ascontiguousarray(w[:, qkv_cols])
    wzab = np.ascontiguousarray(w[:, np.concatenate([zcols, ab_cols])])
    gn = np.ascontiguousarray(inp["a_norm_g"][layer].reshape(8, 128).T)
    cwf = inp["a_conv_w"][layer][:, qkv_cols]
    cw = np.ascontiguousarray(cwf.reshape(4, 6, 128).transpose(2, 1, 0).reshape(128, 24))
    hp = np.broadcast_to(np.array([inp["a_log"][layer][h0], inp["a_log"][layer][h1],
                                   inp["a_dt_bias"][layer][h0], inp["a_dt_bias"][layer][h1]], np.float32), (128, 4))
    ong = np.broadcast_to(np.tile(inp["a_out_norm_g"][layer], 2), (128, 256))
    d = {"wqkv": wqkv, "wzab": wzab, "gn": gn, "cw": cw, "hp": np.ascontiguousarray(hp),
         "ong": np.ascontiguousarray(ong)}
    d.update(_consts())
    return d


TZL = 2432


def build_moba(ngroups=16, P=None, io=None, tag="", kv_mode="compute"):
    standalone = P is None
    nc = bass.Bass("TRN2", target_bir_lowering=False) if standalone else P.nc
    D = Dram(nc, io)
    xh = D("xh", [4, 8, 128, 2048], BF16, "ExternalInput")
    xh2 = D("xh2", [4, 8, 128, 2048], BF16, "ExternalInput")
    wqz = D("wqz", [1024, 512], F32, "ExternalInput")
    wkv = D("wkv", [1024, 512], F32, "ExternalInput")
    gn = D("gn", [128, 16], F32, "ExternalInput")
    tz = D("tz", [2, 128, TZL], F32, "ExternalInput")
    b31 = D("b31", [128, 2], F32, "ExternalInput")
    lsel_d = D("lsel", [32, 32, 128], BF16, "ExternalInput")
    ident_d = D("ident", [128, 128], BF16, "ExternalInput")
    og = D("og", [2, 4, 128, 2048], BF16, "ExternalOutput")
    SCALE = float(128 ** -0.5)
    BIG = 30000.0
    with ExitStack() as es:
        C = Ctx(nc, es, parent=P, tag=tag)
        S = C.S
        fin = []
        b_const = Buf()
        identb = C.sb("identb", [128, 128], BF16)
        gnt = C.sb("gnt", [128, 16], F32)
        b31t = C.sb("b31t", [128, 4], F32)
        lsel = C.sb("lselt", [32, 32, 128], BF16)
        S.dma("sp", lambda e: e.dma_start(out=identb[:], in_=ident_d[:, :]), [], [b_const])
        S.dma("sp", lambda e: e.dma_start(out=gnt[:], in_=gn[:, :]), [], [b_const])
        S.dma("sp", lambda e: e.dma_start(out=b31t[:, 0:2], in_=b31[:, :]), [], [b_const])
        S.dma("sp", lambda e: e.dma_start(out=lsel[:], in_=lsel_d[:, :, :]), [], [b_const])
        S.op("pool", lambda e: e.memset(b31t[:, 2:3], 0.0), [], [b_const])
        ebT = [C.sb("ebT%d" % h, [128, TZL], BF16) for h in range(2)]
        tzs = C.sb("tzs", [128, TZL], F32)
        b_tzs = Buf()
        for h in range(2):
            S.dma("sp", lambda e, h=h: e.dma_start(out=tzs[:], in_=tz[h, :, :]), [], [b_tzs])
            S.op("act", lambda e, h=h: e.activation(out=ebT[h][:], in_=tzs[:], func=AF.Exp), [b_tzs], [b_const])
        wq = C.sb("wq", [128, 8, 512], BF16)
        wk = C.sb("wk", [128, 8, 512], BF16)
        wst = [C.sb("wst%d" % i, [128, 1024], F32) for i in range(2)]
        b_wst = [Buf(), Buf()]
        b_w = Buf()
        for dc in range(8):
            i = dc % 2
            S.dma("sp", lambda e, i=i, dc=dc: e.dma_start(out=wst[i][:, 0:512], in_=wqz[dc * 128:(dc + 1) * 128, :]),
                  [], [b_wst[i]])
            S.dma("pool", lambda e, i=i, dc=dc: e.dma_start(out=wst[i][:, 512:1024], in_=wkv[dc * 128:(dc + 1) * 128, :]),
                  [], [b_wst[i]])
            S.op("dve", lambda e, i=i, dc=dc: e.tensor_scalar(out=wq[:, dc, :], in0=wst[i][:, 0:512],
                                                              scalar1=gnt[:, dc:dc + 1], scalar2=None, op0=ALU.mult),
                 [b_wst[i], b_const], [b_w])
            S.op("dve", lambda e, i=i, dc=dc: e.tensor_scalar(out=wk[:, dc, :], in0=wst[i][:, 512:1024],
                                                              scalar1=gnt[:, 8 + dc:9 + dc], scalar2=None, op0=ALU.mult),
                 [b_wst[i], b_const], [b_w])
        banks = C.banks
        bankb = C.bankb
        KT = C.sb("KT", [128, 2, 8192], BF16)
        b_KT = [Buf() for _ in range(16)]
        Vaug = C.sb("Vaug", [128, 2, 64, 130], BF16)
        b_V = [Buf() for _ in range(16)]
        kmT = C.sb("kmT", [128, 2, 32], F32)
        b_km = Buf()
        S.op("pool", lambda e: e.memset(Vaug[:, :, :, 128:130], 1.0), [], [b_V[0]])
        xt = [C.sb("xt%d" % i, [128, 8, 512], BF16) for i in range(2)]
        b_xt = [Buf(), Buf()]

        def load_x(src, T, i, bname="xh"):
            rank = (T * 512) // 2048
            toff = (T * 512) % 2048
            S.dma("sp" if T % 2 == 0 else "pool", lambda e: e.dma_start(
                out=xt[i][:], in_=src[rank, :, :, toff:toff + 512].rearrange("c p t -> p c t")), [D.buf(bname)], [b_xt[i]])

        def phase_a(T):
            i = T % 2
            load_x(xh2, T, i, "xh2")

            def kproj(h):
                for dc in range(8):
                    S.op("pe", lambda e, dc=dc: e.matmul(banks[4][:], lhsT=wk[:, dc, h * 128:(h + 1) * 128],
                                                         rhs=xt[i][:, dc, :], start=(dc == 0), stop=(dc == 7)),
                         [b_w, b_xt[i]], [bankb[4]])
                S.op("act", lambda e: e.activation(out=KT[:, h, T * 512:(T + 1) * 512], in_=banks[4][:], func=AF.Copy),
                     [bankb[4]], [b_KT[T]])
                S.op("dve", lambda e: e.tensor_reduce(out=kmT[:, h, 2 * T:2 * T + 2],
                                                      in_=banks[4][:].rearrange("p (b t) -> p b t", b=2),
                                                      axis=AX.X, op=ALU.add), [bankb[4]], [b_km])

            def vproj(s):
                for dc in range(8):
                    S.op("pe", lambda e, dc=dc: e.matmul(banks[5][:, 0:256], lhsT=xt[i][:, dc, s * 128:(s + 1) * 128],
                                                         rhs=wk[:, dc, 256:512], start=(dc == 0), stop=(dc == 7)),
                         [b_w, b_xt[i]], [bankb[5]])
                S.op("dve", lambda e: e.tensor_copy(out=Vaug[:, :, 4 * T + s, 0:128],
                                                    in_=banks[5][:, 0:256].rearrange("p (h d) -> p h d", h=2)),
                     [bankb[5]], [b_V[T]])
            for h in range(2):
                kproj(h)
            for s in range(4):
                vproj(s)
        for T in range(16):
            phase_a(T)

        q_bf = [C.sb("q_bf%d" % h, [128, 512], BF16) for h in range(2)]
        q_f = [C.sb("q_f%d" % h, [128, 512], F32) for h in range(2)]
        b_q = [Buf(), Buf()]
        gzt = C.sb("gzt", [128, 4, 256], F32)
        b_gz = Buf()
        MT = [C.sb("MT%d" % h, [32, 512], BF16) for h in range(2)]
        b_MT = [Buf(), Buf()]
        gsb = [C.sb("gsb%d" % i, [128, 32], F32) for i in range(2)]
        m8 = [C.sb("m8_%d" % i, [128, 8], F32) for i in range(2)]
        mbf = [C.sb("mbf%d" % i, [128, 32], F32) for i in range(2)]
        mbb = [C.sb("mbb%d" % i, [128, 32], BF16) for i in range(2)]
        b_gs = [Buf(), Buf()]
        pT = [C.sb("pT%d" % i, [128, 512], BF16) for i in range(2)]
        b_pT = [Buf(), Buf()]
        rinv = C.sb("rinv", [128, 4], F32)
        b_rinv = Buf()
        ogt = C.sb("ogt", [128, 4, 128], BF16)
        b_ogt = Buf()
        ostg = [C.sb("ostg%d" % i, [128, 512], BF16) for i in range(2)]
        b_ostg = [Buf(), Buf()]
        mtp = banks[6][:, 64:128].bitcast(BF16)
        otp = banks[7][:, 0:256].bitcast(BF16)

        def oacc(s):
            return banks[2 + s // 2][:, (s % 2) * 130:(s % 2) * 130 + 129]

        def group(G):
            i = G % 2
            load_x(xh, G, i)

            def qproj(h):
                for dc in range(8):
                    S.op("pe", lambda e, dc=dc: e.matmul(banks[4][:], lhsT=wq[:, dc, h * 128:(h + 1) * 128],
                                                         rhs=xt[i][:, dc, :], start=(dc == 0), stop=(dc == 7)),
                         [b_w, b_xt[i]], [bankb[4]])
                S.op("act", lambda e: e.activation(out=q_bf[h][:], in_=banks[4][:], func=AF.Copy), [bankb[4]], [b_q[h]])
                S.op("dve", lambda e: e.tensor_copy(out=q_f[h][:], in_=banks[4][:]), [bankb[4]], [b_q[h]])

            def zproj(s):
                for dc in range(8):
                    S.op("pe", lambda e, dc=dc: e.matmul(banks[5][:, 0:256], lhsT=xt[i][:, dc, s * 128:(s + 1) * 128],
                                                         rhs=wq[:, dc, 256:512], start=(dc == 0), stop=(dc == 7)),
                         [b_w, b_xt[i]], [bankb[5]])
                S.op("act", lambda e: e.activation(out=gzt[:, s, :], in_=banks[5][:, 0:256], func=AF.Silu),
                     [bankb[5]], [b_gz])

            def gate(h, s, k):
                cur = 2 * G + s // 2
                S.op("pool", lambda e: e.memset(gsb[k][:], -3.0e38), [], [b_gs[k]])
                if cur > 0:
                    S.op("pe", lambda e: e.matmul(banks[6][:, 0:32], lhsT=q_f[h][:, s * 128:(s + 1) * 128],
                                                  rhs=kmT[:, h, :], start=True, stop=True), [b_q[h], b_km], [bankb[6]])
                    S.op("dve", lambda e: e.tensor_copy(out=gsb[k][:, 0:cur], in_=banks[6][:, 0:cur]),
                         [bankb[6]], [b_gs[k]])
                S.op("dve", lambda e: e.max(out=m8[k][:], in_=gsb[k][:]), [b_gs[k]], [b_gs[k]])
                S.op("dve", lambda e: e.tensor_scalar(out=mbf[k][:], in0=gsb[k][:], scalar1=m8[k][:, 2:3], scalar2=None,
                                                      op0=ALU.is_ge), [b_gs[k]], [b_gs[k]])
                S.op("dve", lambda e: e.tensor_scalar(out=mbb[k][:], in0=mbf[k][:], scalar1=-1.0, scalar2=BIG,
                                                      op0=ALU.add, op1=ALU.mult), [b_gs[k]], [b_gs[k]])
                S.op("pool", lambda e: e.memset(mbb[k][:, cur:cur + 1], 0.0), [b_gs[k]], [b_gs[k]])
                if cur < 31:
                    S.op("pool", lambda e: e.memset(mbb[k][:, cur + 1:32], -BIG), [b_gs[k]], [b_gs[k]])
                S.op("pe", lambda e: e.transpose(mtp[0:32, :], mbb[k][:], identb[:]), [b_gs[k], b_const], [bankb[6]])
                S.op("act", lambda e: e.activation(out=MT[h][:, s * 128:(s + 1) * 128], in_=mtp[0:32, :], func=AF.Copy),
                     [bankb[6]], [b_MT[h]])

            def attend(h):
                NJ = 4 * G + 4

                def qk(j):
                    n = j // 2
                    S.op("pe", lambda e: e.matmul(banks[j % 2][:], lhsT=KT[:, h, j * 128:(j + 1) * 128], rhs=q_bf[h][:],
                                                  start=True, stop=False), [b_KT[j // 4], b_q[h]], [bankb[j % 2]])
                    S.op("pe", lambda e: e.matmul(banks[j % 2][:], lhsT=lsel[:, n, :], rhs=MT[h][:],
                                                  start=False, stop=True), [b_const, b_MT[h]], [bankb[j % 2]])

                def ex(j):
                    d0 = 512 * G - 128 * j
                    far = d0 >= 1664
                    bias = b31t[:, h:h + 1] if far else b31t[:, 2:3]
                    S.op("act", lambda e: e.activation(out=pT[j % 2][:], in_=banks[j % 2][:], func=AF.Exp, scale=SCALE,
                                                       bias=bias), [bankb[j % 2], b_const], [b_pT[j % 2]])
                    if not far:
                        off = d0 + 384
                        S.op("pool", lambda e: e.tensor_tensor(out=pT[j % 2][:], in0=pT[j % 2][:],
                                                               in1=ebT[h][:, off:off + 512], op=ALU.mult),
                             [b_pT[j % 2], b_const], [b_pT[j % 2]])

                def pv(j):
                    for s in range(4):
                        S.op("pe", lambda e, s=s: e.matmul(oacc(s), lhsT=pT[j % 2][:, s * 128:(s + 1) * 128],
                                                           rhs=Vaug[:, h, j, 0:129], start=(j == 0 and s % 2 == 0),
                                                           stop=(j == NJ - 1), skip_group_check=True),
                             [b_pT[j % 2], b_V[j // 4]], [bankb[2 + s // 2]])
                qk(0)
                for j in range(NJ):
                    if j + 1 < NJ:
                        qk(j + 1)
                    ex(j)
                    pv(j)

                def fin_s(s):
                    S.op("dve", lambda e: e.reciprocal(out=rinv[:, s:s + 1], in_=oacc(s)[:, 128:129]),
                         [bankb[2 + s // 2]], [b_rinv])
                    S.op("dve", lambda e: e.scalar_tensor_tensor(out=ogt[:, s, :], in0=oacc(s)[:, 0:128],
                                                                 scalar=rinv[:, s:s + 1],
                                                                 in1=gzt[:, s, h * 128:(h + 1) * 128],
                                                                 op0=ALU.mult, op1=ALU.mult),
                         [bankb[2 + s // 2], b_rinv, b_gz], [b_ogt])
                    S.op("pe", lambda e: e.transpose(otp[:, s * 128:(s + 1) * 128], ogt[:, s, :], identb[:]),
                         [b_ogt, b_const], [bankb[7]])
                for s in range(4):
                    fin_s(s)
                k = (2 * G + h) % 2
                S.op("act", lambda e: e.activation(out=ostg[k][:], in_=otp[:, 0:512], func=AF.Copy),
                     [bankb[7]], [b_ostg[k]])
                fin.append(S.dma("sp", lambda e: e.dma_start(
                    out=og[h, (G * 512) // 2048, :, (G * 512) % 2048:(G * 512) % 2048 + 512], in_=ostg[k][:]),
                    [b_ostg[k]], [D.buf("og")]))

            for h in range(2):
                qproj(h)
            for s in range(4):
                zproj(s)
            kk = 0
            for h in range(2):
                for s in range(4):
                    gate(h, s, kk % 2)
                    kk += 1
            for h in range(2):
                attend(h)
        for G in range(ngroups):
            group(G)
        if standalone:
            S.finalize(fin)
        else:
            S.flush()
    return nc if standalone else fin


def t5_bucket_np(rel):
    import math
    n = np.maximum(rel, 0)
    nf = np.maximum(n, 1).astype(np.float32)
    large = 16 + (np.log(nf / np.float32(16)) / np.float32(math.log(2048 / 16)) * np.float32(16)).astype(np.int32)
    large = np.minimum(large, 31)
    return np.where(n < 16, n, large)


def moba_inputs(inp, j, r):
    h0, h1 = 2 * r, 2 * r + 1
    w = inp["b_w_in"][j]
    qc = np.concatenate([np.arange(h * 128, (h + 1) * 128) for h in (h0, h1)])
    wqz = np.ascontiguousarray(np.concatenate([w[:, qc], w[:, 1024 + qc]], axis=1))
    wkv = np.ascontiguousarray(np.concatenate([inp["w_kv"][:, qc], inp["w_kv"][:, 1024 + qc]], axis=1))
    gn = np.ascontiguousarray(np.concatenate([inp["b_norm_g"][j].reshape(8, 128).T,
                                              inp["kv_norm_g"].reshape(8, 128).T], axis=1))
    p = np.arange(128)[:, None]
    m = np.arange(TZL)[None, :]
    dist = m - p - 384
    idx = np.where(dist < 0, 32, t5_bucket_np(dist))
    tz = []
    for h in (h0, h1):
        ext = np.concatenate([inp["rel_bias"][:, h], np.array([-30000.0], np.float32)])
        tz.append(ext[idx])
    tz = np.ascontiguousarray(np.stack(tz).astype(np.float32))
    b31 = np.ascontiguousarray(np.broadcast_to(inp["rel_bias"][31, [h0, h1]], (128, 2)))
    lsel = np.zeros((32, 32, 128), np.float32)
    for n in range(32):
        lsel[n, n, :] = 1.0
    d = {"wqz": wqz, "wkv": wkv, "gn": gn, "tz": tz, "b31": b31, "lsel": lsel.astype(NPBF),
         "ident": np.eye(128, dtype=np.float32).astype(NPBF)}
    return d


GDN_W = (("wqkv", [1024, 768], F32), ("wzab", [1024, 260], F32), ("gn", [128, 8], F32), ("cw", [128, 24], F32),
         ("hp", [128, 4], F32), ("ong", [128, 256], F32))
MOBA_W = (("wqz", [1024, 512], F32), ("wkv", [1024, 512], F32), ("gn", [128, 16], F32), ("tz", [2, 128, TZL], F32),
          ("b31", [128, 2], F32))


def build_single(nlayers=4, npairs=4, nchunks=4, mini=False):
    nc = bass.Bass("TRN2", target_bir_lowering=False)

    def ext(name, shape, dt, kind="ExternalInput"):
        return nc.dram_tensor(name, shape, dt, kind=kind).ap()

    def internal(name, shape, dt):
        return nc.dram_tensor(name, shape, dt, kind="Internal").ap()

    with ExitStack() as es:
        P = Ctx(nc, es)
        S = P.S
        x_in = ext("x", [8192, 1024], F32)
        ident = ext("ident", [128, 128], BF16)
        cf = ext("cf", [128, 4, 128], F32)
        cm = ext("cm", [128, 10, 128], BF16)
        lsel = ext("lsel", [32, 32, 128], BF16)
        gf = ext("gf", [1024], F32)
        wo = [ext("wo%d" % l, [1024, 1024], F32) for l in range(4)]
        lw = {}
        for l in range(nlayers):
            spec = GDN_W if l < 2 else MOBA_W
            for p in range(npairs):
                lw[(l, p)] = {n: ext("%s_l%d_p%d" % (n, l, p), sh, dt) for n, sh, dt in spec}
        xs = internal("xs", [8192, 1024], F32)
        XH = [internal("XH%d" % i, [4, 8, 128, 2048], BF16) for i in range(2)]
        if mini:
            OG = ext("ogout", [8, 128, 8192], BF16, "ExternalOutput")
        else:
            OG = internal("OG", [8, 128, 8192], BF16)
            y = ext("y", [8192, 1024], F32, "ExternalOutput")
        b_xs = [Buf() for _ in range(4)]
        b_XH = [Buf(), Buf()]
        b_OG = Buf()
        fin = []
        for r in range(nchunks):
            rows = slice(2048 * r, 2048 * (r + 1))
            build_tok(False, False, P=P, tag="t0c%d_" % r,
                      io={"x": x_in[rows, :], "ident": ident, "xo": xs[rows, :], "xh": XH[0][r],
                          "_bufs": {"x": Buf(), "xo": b_xs[r], "xh": b_XH[0]}})
        for layer in range(nlayers):
            cur = layer % 2
            for p in range(npairs):
                ogv = OG[2 * p:2 * p + 2].rearrange("h p (j t) -> h j p t", j=4)
                io = dict(lw[(layer, p)])
                if layer < 2:
                    io.update({"xh": XH[cur], "ident": ident, "cf": cf, "cm": cm, "og": ogv,
                               "_bufs": {"xh": b_XH[cur], "og": b_OG}})
                    fin = build_gdn(ntiles=(4 if mini else 32), P=P, io=io, tag="g%dp%d_" % (layer, p))
                else:
                    io.update({"xh": XH[cur], "xh2": XH[0], "ident": ident, "lsel": lsel, "og": ogv,
                               "_bufs": {"xh": b_XH[cur], "xh2": b_XH[0], "og": b_OG}})
                    fin = build_moba(P=P, io=io, tag="m%dp%d_" % (layer, p))
            if mini:
                break
            final = layer == 3
            for r in range(nchunks):
                rows = slice(2048 * r, 2048 * (r + 1))
                io = {"x": xs[rows, :], "ident": ident, "og": OG[:, :, 2048 * r:2048 * (r + 1)], "wo": wo[layer],
                      "_bufs": {"x": b_xs[r], "xo": b_xs[r], "og": b_OG, "xh": b_XH[1 - cur], "y": Buf()}}
                if final:
                    io["gf"] = gf
                    io["y"] = y[rows, :]
                else:
                    io["xo"] = xs[rows, :]
                    io["xh"] = XH[1 - cur][r]
                f = build_tok(True, final, P=P, io=io, tag="t%dc%d_" % (layer + 1, r))
                if final:
                    fin = (fin if r else []) + f
        S.finalize(fin)
    return nc, S.nops


def single_maps(inp, mini=False, nlayers=4, npairs=4):
    consts = _consts()
    lsel = np.zeros((32, 32, 128), np.float32)
    for n in range(32):
        lsel[n, n, :] = 1.0
    shared = {"ident": consts["ident"], "cf": consts["cf"], "cm": consts["cm"], "lsel": lsel.astype(NPBF),
              "gf": np.ascontiguousarray(inp["final_norm_g"])}
    for l in range(4):
        shared["wo%d" % l] = np.ascontiguousarray(inp["a_w_out"][l] if l < 2 else inp["b_w_out"][l - 2])
    for l in range(nlayers):
        spec = GDN_W if l < 2 else MOBA_W
        for p in range(npairs):
            src = gdn_inputs(inp, l, p) if l < 2 else moba_inputs(inp, l - 2, p)
            for n, _, _ in spec:
                shared["%s_l%d_p%d" % (n, l, p)] = src[n]
    maps = []
    for b in range(2):
        d = dict(shared)
        d["x"] = np.ascontiguousarray(inp["x"][b].astype(np.float32))
        maps.append(d)
    return maps


def kernel(**inp):
    inp = {k: np.asarray(v) for k, v in inp.items()}
    nc, _ = build_single()
    res = run_bass_kernel_spmd(nc, single_maps(inp), core_ids=[0, 1])
    return np.stack([res.results[b]["y"] for b in range(2)]).astype(np.float32)
```

```python
import numpy as np
import concourse.bass as bass
import concourse.mybir as mybir
from concourse.bass_utils import run_bass_kernel_spmd
from contextlib import ExitStack

F32 = mybir.dt.float32
BF16 = mybir.dt.bfloat16
AF = mybir.ActivationFunctionType
ALU = mybir.AluOpType
AX = mybir.AxisListType


class Buf:
    __slots__ = ("name", "last_w", "readers", "excl")

    def __init__(self, name="", excl=False):
        self.name = name
        self.last_w = None
        self.readers = []
        self.excl = excl


class Q:
    def __init__(self, name, is_pe=False):
        self.name = name
        self.is_pe = is_pe
        self.sem = None
        self.cnt = 0
        self.known = {}
        self.prog = []
        self.dsems = []
        self.dcnt = []
        self.dnext = 0


class Sched:
    NDMA = 8

    def __init__(self, nc, es):
        self.nc = nc
        self.es = es
        self.q = {}
        for n in ("pe", "act", "dve", "pool", "sp"):
            q = Q(n, is_pe=(n == "pe"))
            q.sem = es.enter_context(nc.semaphore("s_" + n))
            self.q[n] = q
        for n in ("act", "pool", "sp"):
            q = self.q[n]
            for i in range(self.NDMA):
                q.dsems.append(es.enter_context(nc.semaphore("d_%s%d" % (n, i))))
                q.dcnt.append(0)
        self.nops = 0

    def _waits(self, q, deps):
        waits = []
        for tok, skip_same in deps:
            sem, val, src = tok
            if src is q and (q.is_pe or skip_same):
                continue
            key = id(sem)
            if q.known.get(key, 0) >= val:
                continue
            q.known[key] = val
            waits.append((sem, val))
        return waits

    def _deps(self, reads, writes):
        deps = []
        for b in reads:
            if b.last_w is not None:
                deps.append((b.last_w, b.excl))
        for b in writes:
            if b.last_w is not None:
                deps.append((b.last_w, b.excl))
            deps.extend((r, False) for r in b.readers)
        return deps

    def _commit(self, tok, reads, writes):
        for b in reads:
            if b.excl:
                b.last_w = tok
            else:
                b.readers.append(tok)
        for b in writes:
            b.last_w = tok
            b.readers = []

    def op(self, qn, fn, reads=(), writes=()):
        q = self.q[qn]
        waits = self._waits(q, self._deps(reads, writes))
        q.cnt += 1
        tok = (q.sem, q.cnt, q)
        q.prog.append((waits, fn, (q.sem, 1)))
        self._commit(tok, reads, writes)
        self.nops += 1
        return tok

    def dma(self, qn, fn, reads=(), writes=()):
        q = self.q[qn]
        deps = self._deps(reads, writes)
        k = q.dnext
        q.dnext = (k + 1) % len(q.dsems)
        sem = q.dsems[k]
        if q.dcnt[k] > 0:
            deps.append(((sem, q.dcnt[k], None), False))
        waits = self._waits(q, deps)
        q.dcnt[k] += 16
        tok = (sem, q.dcnt[k], None)
        q.prog.append((waits, fn, (sem, 16)))
        self._commit(tok, reads, writes)
        self.nops += 1
        return tok

    def coll(self, fn, reads=(), writes=()):
        return self.dma("pool", fn, reads, writes)

    def flush(self):
        nc = self.nc
        with nc.Block() as block:
            def replay(qn, eng):
                for waits, fn, inc in self.q[qn].prog:
                    for sem, val in waits:
                        eng.wait_ge(sem, val)
                    if fn is not None:
                        inst = fn(eng)
                        inst.then_inc(inc[0], inc[1])
                self.q[qn].prog = []

            @block.tensor
            def _(e):
                replay("pe", e)

            @block.scalar
            def _(e):
                replay("act", e)

            @block.vector
            def _(e):
                replay("dve", e)

            @block.gpsimd
            def _(e):
                replay("pool", e)

            @block.sync
            def _(e):
                replay("sp", e)

    def finalize(self, final_toks):
        q = self.q["sp"]
        waits = self._waits(q, [(t, False) for t in final_toks])
        q.prog.append((waits, None, None))
        self.flush()


EPS = 1e-6


class Ctx:
    def __init__(self, nc, es, parent=None, tag=""):
        self.nc = nc
        self.es = es
        self.tag = tag
        if parent is None:
            self.S = Sched(nc, es)
            self.psum_all = es.enter_context(nc.psum_tensor("psum_all", [128, 4096], F32))
            self.bankb = [Buf("bank%d" % i, excl=True) for i in range(8)]
        else:
            self.S = parent.S
            self.psum_all = parent.psum_all
            self.bankb = parent.bankb
        self.banks = [self.psum_all[:, 512 * i:512 * (i + 1)] for i in range(8)]

    def sb(self, name, shape, dt):
        return self.es.enter_context(self.nc.sbuf_tensor(self.tag + name, shape, dt))


class Dram:
    def __init__(self, nc, io=None):
        self.nc = nc
        self.io = io
        self.bufs = {} if io is None else io.get("_bufs", {})

    def __call__(self, name, shape, dt, kind):
        if self.io is None:
            return self.nc.dram_tensor(name, shape, dt, kind=kind).ap()
        return self.io[name]

    def buf(self, name):
        if name not in self.bufs:
            self.bufs[name] = Buf(name)
        return self.bufs[name]


def build_tok(has_proj, final, P=None, io=None, tag=""):
    standalone = P is None
    nc = bass.Bass("TRN2", target_bir_lowering=False) if standalone else P.nc
    D = Dram(nc, io)
    NT = 16
    x = D("x", [2048, 1024], F32, "ExternalInput")
    ident_d = D("ident", [128, 128], BF16, "ExternalInput")
    direct_og = standalone or (io is not None and "og" in io)
    if has_proj:
        if direct_og:
            og = D("og", [8, 128, 2048], BF16, "ExternalInput")
        wo = D("wo", [1024, 1024], F32, "ExternalInput")
    if final:
        gf = D("gf", [1024], F32, "ExternalInput")
        y = D("y", [2048, 1024], F32, "ExternalOutput")
    else:
        xo = D("xo", [2048, 1024], F32, "ExternalOutput")
        xh = D("xh", [8, 128, 2048], BF16, "ExternalOutput")
    with ExitStack() as es:
        C = Ctx(nc, es, parent=P, tag=tag)
        S = C.S
        fin = []
        ident = C.sb("identb", [128, 128], BF16)
        b_ident = Buf()
        S.dma("sp", lambda e: e.dma_start(out=ident[:], in_=ident_d[:, :]), [], [b_ident])
        if has_proj:
            if direct_og:
                def og_dma(e, dst, t0):
                    return e.dma_start(out=dst[:], in_=og[:, :, t0:t0 + 128].rearrange("h p t -> p h t"))
                og_deps = [D.buf("og")]
            else:
                ogv = io["ogfull"].rearrange("(c j p) t -> c j p t", c=8, j=4)
                oht = C.sb("oht", [128, 4], F32)
                b_oh = Buf()
                S.dma("sp", lambda e: e.dma_start(out=oht[:], in_=io["oh"][:, :]), [], [b_oh])
                cand = [[C.sb("cand%d_%d" % (k, j), [128, 8, 128], BF16) for j in range(4)] for k in range(2)]
                b_cand = [[Buf() for j in range(4)] for k in range(2)]
                og_deps = [D.buf("og")]
            wof = C.sb("wof", [128, 8, 1024], F32)
            wob = C.sb("wob", [128, 8, 1024], BF16)
            b_wof, b_wob = Buf(), Buf()
            wo_v = wo.rearrange("(h p) n -> p h n", p=128)
            for h in range(8):
                S.dma("sp" if h % 2 == 0 else "pool",
                      lambda e, h=h: e.dma_start(out=wof[:, h, :], in_=wo_v[:, h, :]), [], [b_wof])
            for h in range(8):
                S.op("pool" if h % 2 == 0 else "dve",
                     lambda e, h=h: e.tensor_copy(out=wob[:, h, :], in_=wof[:, h, :]), [b_wof], [b_wob])
        if final:
            gfb = C.sb("gfb", [128, 1024], F32)
            b_gfb = Buf()
            S.dma("pool", lambda e: e.dma_start(out=gfb[:], in_=gf.partition_broadcast(128)), [], [b_gfb])
        NB = 2
        epsb = C.sb("epsb", [128, 1], F32)
        b_eps = Buf()
        S.op("pool", lambda e: e.memset(epsb[:], EPS), [], [b_eps])
        xt = [C.sb("xt%d" % i, [128, 1024], F32) for i in range(NB)]
        b_xt = [Buf() for _ in range(NB)]
        sq = [C.sb("sq%d" % i, [128, 1024], F32) for i in range(NB)]
        b_sq = [Buf() for _ in range(NB)]
        st = [C.sb("st%d" % i, [128, 4], F32) for i in range(NB)]
        b_st = [Buf() for _ in range(NB)]
        if has_proj:
            ot = [C.sb("ot%d" % i, [128, 8, 128], BF16) for i in range(NB)]
            b_ot = [Buf() for _ in range(NB)]
            pp = [C.psum_all[:, 1024 * i:1024 * (i + 1)] for i in range(NB)]
            b_pp = [[C.bankb[2 * i], C.bankb[2 * i + 1]] for i in range(NB)]
        if final:
            yt = [C.sb("yt%d" % i, [128, 1024], F32) for i in range(NB)]
            b_yt = [Buf() for _ in range(NB)]
        else:
            xb = [C.sb("xb%d" % i, [128, 1024], BF16) for i in range(NB)]
            b_xb = [Buf() for _ in range(NB)]
            pt = [C.banks[4 + i].bitcast(BF16).rearrange("p (c t) -> p c t", c=8) for i in range(NB)]
            b_pt = [C.bankb[4 + i] for i in range(NB)]
            stg = [C.sb("stg%d" % i, [128, 8, 512], BF16) for i in range(2)]
            b_stg = [Buf() for _ in range(2)]
        for t in range(NT):
            i = t % NB
            t0 = t * 128
            S.dma("sp", lambda e, i=i, t0=t0: e.dma_start(out=xt[i][:], in_=x[t0:t0 + 128, :]), [D.buf("x")], [b_xt[i]])
            if has_proj:
                if direct_og:
                    S.dma("pool", lambda e, i=i, t0=t0: og_dma(e, ot[i], t0), og_deps, [b_ot[i]])
                else:
                    for j in range(4):
                        S.dma("pool" if j % 2 else "sp", lambda e, i=i, t0=t0, j=j: e.dma_start(
                            out=cand[i][j][:], in_=ogv[:, j, :, t0:t0 + 128].rearrange("c p t -> p c t")),
                            og_deps, [b_cand[i][j]])
                    S.op("dve", lambda e, i=i: e.tensor_scalar(out=ot[i][:], in0=cand[i][0][:], scalar1=oht[:, 0:1],
                                                               scalar2=None, op0=ALU.mult),
                         [b_cand[i][0], b_oh], [b_ot[i]])
                    for j in range(1, 4):
                        S.op("dve", lambda e, i=i, j=j: e.scalar_tensor_tensor(
                            out=ot[i][:], in0=cand[i][j][:], scalar=oht[:, j:j + 1], in1=ot[i][:],
                            op0=ALU.mult, op1=ALU.add), [b_cand[i][j], b_oh, b_ot[i]], [b_ot[i]])
                for nch in range(2):
                    for h in range(8):
                        S.op("pe", lambda e, i=i, h=h, nch=nch: e.matmul(
                            pp[i][:, nch * 512:(nch + 1) * 512], lhsT=ot[i][:, h, :],
                            rhs=wob[:, h, nch * 512:(nch + 1) * 512], start=(h == 0), stop=(h == 7)),
                            [b_ot[i], b_wob], b_pp[i])
                S.op("dve", lambda e, i=i: e.tensor_tensor(out=xt[i][:], in0=xt[i][:], in1=pp[i][:], op=ALU.add),
                     [b_xt[i]] + b_pp[i], [b_xt[i]])
            if not final:
                S.dma("pool", lambda e, i=i, t0=t0: e.dma_start(out=xo[t0:t0 + 128, :], in_=xt[i][:]), [b_xt[i]], [D.buf("xo")])
                fin.append(None)
            S.op("act", lambda e, i=i: e.activation(out=sq[i][:], in_=xt[i][:], func=AF.Square,
                                                    accum_out=st[i][:, 0:1]), [b_xt[i]], [b_sq[i], b_st[i]])
            S.op("act", lambda e, i=i: e.activation(out=st[i][:, 1:2], in_=st[i][:, 0:1], func=AF.Ln,
                                                    scale=1.0 / 1024, bias=epsb[:, 0:1]), [b_st[i], b_eps], [b_st[i]])
            S.op("act", lambda e, i=i: e.activation(out=st[i][:, 2:3], in_=st[i][:, 1:2], func=AF.Exp,
                                                    scale=-0.5), [b_st[i]], [b_st[i]])
            if final:
                S.op("dve", lambda e, i=i: e.scalar_tensor_tensor(out=yt[i][:], in0=xt[i][:], scalar=st[i][:, 2:3],
                                                                   in1=gfb[:], op0=ALU.mult, op1=ALU.mult),
                     [b_xt[i], b_st[i], b_gfb], [b_yt[i]])
                fin.append(S.dma("sp", lambda e, i=i, t0=t0: e.dma_start(out=y[t0:t0 + 128, :], in_=yt[i][:]),
                                 [b_yt[i]], [D.buf("y")]))
            else:
                S.op("act", lambda e, i=i: e.activation(out=xb[i][:], in_=xt[i][:], func=AF.Copy,
                                                        scale=st[i][:, 2:3]), [b_xt[i], b_st[i]], [b_xb[i]])
                for dc in range(8):
                    S.op("pe", lambda e, i=i, dc=dc: e.transpose(pt[i][:, dc, :], xb[i][:, dc * 128:(dc + 1) * 128],
                                                                 ident[:]), [b_xb[i], b_ident], [b_pt[i]])
                g = (t // 4) % 2
                tq = t % 4
                S.op("dve", lambda e, i=i, g=g, tq=tq: e.tensor_copy(out=stg[g][:, :, tq * 128:(tq + 1) * 128],
                                                                     in_=pt[i][:]), [b_pt[i]], [b_stg[g]])
                if tq == 3:
                    tb = (t // 4) * 512
                    fin.append(S.dma("sp", lambda e, g=g, tb=tb: e.dma_start(
                        out=xh[:, :, tb:tb + 512].rearrange("c p t -> p c t"), in_=stg[g][:]), [b_stg[g]], [D.buf("xh")]))
        fin = [f for f in fin if f is not None]
        if standalone:
            S.finalize(fin)
        else:
            S.flush()
    return nc if standalone else fin


def roundrobin(gens):
    gens = list(gens)
    while gens:
        nxt = []
        for g in gens:
            try:
                next(g)
                nxt.append(g)
            except StopIteration:
                pass
        gens = nxt


def build_gdn(ntiles=32, P=None, io=None, tag=""):
    standalone = P is None
    nc = bass.Bass("TRN2", target_bir_lowering=False) if standalone else P.nc
    D = Dram(nc, io)
    xh = D("xh", [4, 8, 128, 2048], BF16, "ExternalInput")
    wqkv = D("wqkv", [1024, 768], F32, "ExternalInput")
    wzab = D("wzab", [1024, 260], F32, "ExternalInput")
    gn = D("gn", [128, 8], F32, "ExternalInput")
    cw = D("cw", [128, 24], F32, "ExternalInput")
    hp = D("hp", [128, 4], F32, "ExternalInput")
    ong = D("ong", [128, 256], F32, "ExternalInput")
    ident_d = D("ident", [128, 128], BF16, "ExternalInput")
    cf_d = D("cf", [128, 4, 128], F32, "ExternalInput")
    cm_d = D("cm", [128, 10, 128], BF16, "ExternalInput")
    og = D("og", [2, 4, 128, 2048], BF16, "ExternalOutput")
    with ExitStack() as es:
        C = Ctx(nc, es, parent=P, tag=tag)
        S = C.S
        fin = []
        identb = C.sb("identb", [128, 128], BF16)
        cf = C.sb("cfs", [128, 4, 128], F32)
        gnt = C.sb("gnt", [128, 8], F32)
        cwt = C.sb("cwt", [128, 24], F32)
        hpt = C.sb("hpt", [128, 4], F32)
        ongt = C.sb("ongt", [128, 256], F32)
        b_const = Buf()
        for dst, src in ((identb, ident_d), (gnt, gn), (cwt, cw), (hpt, hp), (ongt, ong)):
            S.dma("sp", lambda e, dst=dst, src=src: e.dma_start(out=dst[:], in_=src[:, :]), [], [b_const])
        S.dma("sp", lambda e: e.dma_start(out=cf[:], in_=cf_d[:, :, :]), [], [b_const])
        cmt = C.sb("cmt", [128, 10, 128], BF16)
        S.dma("sp", lambda e: e.dma_start(out=cmt[:], in_=cm_d[:, :, :]), [], [b_const])
        Uf = cf[:, 0, :]
        NMA = cf[:, 1, :]
        NMQ = cf[:, 2, :]
        identf = cf[:, 3, :]
        onesf = C.sb("onesf", [128, 128], F32)
        onesb = C.sb("onesb", [128, 128], BF16)
        misc = C.sb("misc", [128, 8], F32)
        S.op("pool", lambda e: e.memset(onesf[:], 1.0), [], [b_const])
        S.op("pool", lambda e: e.memset(onesb[:], 1.0), [], [b_const])
        S.op("pool", lambda e: e.memset(misc[:, 0:1], EPS), [], [b_const])
        S.op("pool", lambda e: e.memset(misc[:, 1:2], 128 * EPS), [], [b_const])
        S.op("pool", lambda e: e.memset(misc[:, 2:3], 1.0), [], [b_const])
        S.op("act", lambda e: e.activation(out=misc[:, 5:7], in_=hpt[:, 0:2], func=AF.Exp), [b_const], [b_const])
        S.op("dve", lambda e: e.tensor_scalar(out=misc[:, 3:5], in0=misc[:, 5:7], scalar1=-1.0, scalar2=None,
                                              op0=ALU.mult), [b_const], [b_const])
        dg = C.sb("dg", [128, 24, 128], BF16)
        for k in range(24):
            S.op("dve" if k % 2 else "pool", lambda e, k=k: e.tensor_scalar(
                out=dg[:, k, :], in0=identf, scalar1=cwt[:, k:k + 1], scalar2=None, op0=ALU.mult),
                [b_const], [b_const])
        wq = C.sb("wq", [128, 8, 768], BF16)
        wz = C.sb("wz", [128, 8, 260], BF16)
        wst = [C.sb("wst%d" % i, [128, 1028], F32) for i in range(2)]
        b_wst = [Buf(), Buf()]
        b_w = Buf()
        for dc in range(8):
            i = dc % 2
            S.dma("sp", lambda e, i=i, dc=dc: e.dma_start(out=wst[i][:, 0:768], in_=wqkv[dc * 128:(dc + 1) * 128, :]),
                  [], [b_wst[i]])
            S.dma("pool", lambda e, i=i, dc=dc: e.dma_start(out=wst[i][:, 768:1028], in_=wzab[dc * 128:(dc + 1) * 128, :]),
                  [], [b_wst[i]])
            S.op("dve", lambda e, i=i, dc=dc: e.tensor_scalar(out=wq[:, dc, :], in0=wst[i][:, 0:768],
                                                              scalar1=gnt[:, dc:dc + 1], scalar2=None, op0=ALU.mult),
                 [b_wst[i], b_const], [b_w])
            S.op("dve", lambda e, i=i, dc=dc: e.tensor_scalar(out=wz[:, dc, :], in0=wst[i][:, 768:1028],
                                                              scalar1=gnt[:, dc:dc + 1], scalar2=None, op0=ALU.mult),
                 [b_wst[i], b_const], [b_w])
        TW = 256
        xt = [C.sb("xt%d" % i, [128, 8, TW], BF16) for i in range(2)]
        b_xt = [Buf(), Buf()]
        Pc = [[C.sb("Pc%d_%d" % (g, i), [128, TW + 3], BF16) for i in range(2)] for g in range(6)]
        b_Pc = [[Buf(), Buf()] for g in range(6)]
        for g in range(6):
            S.op("pool", lambda e, g=g: e.memset(Pc[g][0][:, 0:3], 0.0), [], [b_Pc[g][0]])
        s1 = [[C.sb("s1_%d_%d" % (g, i), [128, TW], BF16) for i in range(2)] for g in range(6)]
        b_s1 = [[Buf(), Buf()] for g in range(6)]
        qn = [[C.sb("qn_%d_%d" % (g, i), [128, TW], BF16) for i in range(2)] for g in range(4)]
        b_qn = [[Buf(), Buf()] for g in range(4)]
        sqb = [C.sb("sqb%d" % i, [128, TW], BF16) for i in range(2)]
        b_sqb = [Buf(), Buf()]
        lnb = [C.sb("lnb%d" % i, [128, TW], F32) for i in range(2)]
        b_lnb = [Buf(), Buf()]
        banks = C.banks

        def reg(b, lo, hi, bf=False):
            ap = banks[b][:, lo:hi]
            return ap.bitcast(BF16) if bf else ap

        bankb = C.bankb
        pj = [reg(0, 0, 256), reg(1, 0, 256)]
        b_pj = [bankb[0], bankb[1]]
        pc = [reg(0, 256, 512), reg(1, 256, 512)]
        b_pc = [bankb[0], bankb[1]]
        pz = reg(2, 0, 260)
        b_pz = bankb[2]
        gz = [C.sb("gz%d" % i, [128, 256], F32) for i in range(4)]
        b_gz = [Buf() for _ in range(4)]
        sz = [C.sb("sz%d" % i, [128, 256], F32) for i in range(2)]
        b_sz = [Buf(), Buf()]
        gt = [C.sb("gt%d" % i, [128, 16], F32) for i in range(4)]
        b_gt = [Buf() for _ in range(4)]
        X1 = [reg(3 + 2 * h, 0, 256) for h in range(2)]
        X5 = [reg(3 + 2 * h, 256, 512) for h in range(2)]
        X2 = [reg(4 + 2 * h, 0, 256) for h in range(2)]
        X3 = [reg(4 + 2 * h, 256, 384) for h in range(2)]
        X4 = [reg(4 + 2 * h, 384, 512, True) for h in range(2)]
        X3b = [reg(7, 64 * h, 64 * h + 64, True) for h in range(2)]
        XT = [reg(7, 128 + 64 * h, 192 + 64 * h, True) for h in range(2)]
        b_X1 = [bankb[3], bankb[5]]
        b_X5a = b_X1
        b_X5b = b_X1
        b_X2 = [bankb[4], bankb[6]]
        b_X3 = b_X2
        b_X4 = b_X2
        b_X3b = [bankb[7], bankb[7]]
        b_XT = [bankb[7], bankb[7]]

        def hs(name, shape, dt):
            return [C.sb("%s_%d" % (name, h), shape, dt) for h in range(2)]

        gb = hs("gb", [128, 128], F32); b_gb = [Buf(), Buf()]
        Rsb = hs("Rsb", [128, 128], F32); b_Rsb = [Buf(), Buf()]
        egb = hs("egb", [128, 128], F32); b_egb = [Buf(), Buf()]
        gc2 = hs("gc2", [128, 8], F32); b_gc2 = [Buf(), Buf()]
        RA = hs("RA", [128, 128], F32); b_RA = [Buf(), Buf()]
        RQ = hs("RQ", [128, 128], F32); b_RQ = [Buf(), Buf()]
        DA = hs("DA", [128, 128], F32); b_DA = [Buf(), Buf()]
        DQ = hs("DQ", [128, 128], F32); b_DQ = [Buf(), Buf()]
        QKT = hs("QKT", [128, 128], BF16); b_QKT = [Buf(), Buf()]
        YZ = [[C.sb("YZ_%d_%d" % (h, i), [128, 2, 128], BF16) for i in range(2)] for h in range(2)]
        b_YZ = [[Buf(), Buf()] for h in range(2)]
        PQ = [[C.sb("PQ_%d_%d" % (h, i), [128, 2, 128], BF16) for i in range(2)] for h in range(2)]
        b_PQ = [[Buf(), Buf()] for h in range(2)]
        AB = hs("AB", [128, 2, 128], BF16); b_AB = [Buf(), Buf()]
        Wm = hs("Wm", [128, 2, 128], BF16); b_Wm = [Buf(), Buf()]
        kbg = hs("kbg", [128, 128], BF16); b_kbg = [Buf(), Buf()]
        kdec = hs("kdec", [128, 128], BF16); b_kdec = [Buf(), Buf()]
        vb = hs("vb", [128, 128], BF16); b_vb = [Buf(), Buf()]
        Usb = hs("Usb", [128, 128], F32); b_Usb = [Buf(), Buf()]
        wT = hs("wT", [128, 128], BF16); b_wT = [Buf(), Buf()]
        qdT = hs("qdT", [128, 128], BF16); b_qdT = [Buf(), Buf()]
        vnew = hs("vnew", [128, 128], BF16); b_vnew = [Buf(), Buf()]
        Sf = hs("Sf", [128, 128], F32); b_Sf = [Buf(), Buf()]
        Sb = hs("Sb", [128, 128], BF16); b_Sb = [Buf(), Buf()]
        junk = hs("junk", [128, 128], F32); b_junk = [Buf(), Buf()]
        ost = hs("ost", [128, 8], F32); b_ost = [Buf(), Buf()]
        ogt = hs("ogt", [128, 128], BF16); b_ogt = [Buf(), Buf()]
        ostg = [[C.sb("ostg_%d_%d" % (h, i), [128, 512], BF16) for i in range(2)] for h in range(2)]
        b_ostg = [[Buf(), Buf()] for h in range(2)]
        for h in range(2):
            S.op("pool", lambda e, h=h: e.memset(Sf[h][:], 0.0), [], [b_Sf[h]])
            S.op("pool", lambda e, h=h: e.memset(Sb[h][:], 0.0), [], [b_Sb[h]])

        def chunk_head(h, ti, ci, cg):
            cs = slice(ci * 128, (ci + 1) * 128)
            kT = qn[2 + h][ti][:, cs]
            qT = qn[h][ti][:, cs]
            vT = s1[4 + h][ti][:, cs]
            r_kT = [b_qn[2 + h][ti]]
            r_qT = [b_qn[h][ti]]
            r_vT = [b_s1[4 + h][ti]]
            g_ = gt[cg % 4]
            bg = b_gt[cg % 4]
            gcol = g_[:, 6 + h:7 + h]
            beta = g_[:, 12 + h:13 + h]
            S.op("dve", lambda e: e.tensor_scalar(out=gb[h][:], in0=onesf[:], scalar1=gcol, scalar2=None, op0=ALU.mult),
                 [bg, b_const], [b_gb[h]])
            S.op("pe", lambda e: e.matmul(X1[h][:, 0:128], lhsT=gb[h][:], rhs=Uf, start=True, stop=True),
                 [b_gb[h], b_const], [b_X1[h]])
            S.op("pe", lambda e: e.matmul(X1[h][:, 128:129], lhsT=Uf, rhs=gcol, start=True, stop=True),
                 [bg, b_const], [b_X1[h]])
            yield
            S.op("act", lambda e: e.activation(out=Rsb[h][:], in_=X1[h][:, 0:128], func=AF.Copy), [b_X1[h]], [b_Rsb[h]])
            S.op("act", lambda e: e.activation(out=egb[h][:], in_=X1[h][:, 0:128], func=AF.Exp), [b_X1[h]], [b_egb[h]])
            S.op("dve", lambda e: e.tensor_copy(out=gc2[h][:, 0:1], in_=X1[h][:, 128:129]), [b_X1[h]], [b_gc2[h]])
            S.op("dve", lambda e: e.tensor_scalar(out=gc2[h][:, 1:2], in0=X1[h][:, 128:129], scalar1=-1.0, scalar2=None,
                                                  op0=ALU.mult), [b_X1[h]], [b_gc2[h]])
            S.op("act", lambda e: e.activation(out=gc2[h][:, 2:3], in_=X1[h][:, 128:129], func=AF.Exp),
                 [b_X1[h]], [b_gc2[h]])
            S.op("dve", lambda e: e.tensor_tensor(out=gc2[h][:, 3:4], in0=gc2[h][:, 2:3], in1=beta, op=ALU.mult),
                 [b_gc2[h], bg], [b_gc2[h]])
            yield
            S.op("pool", lambda e: e.tensor_tensor(out=RA[h][:], in0=Rsb[h][:], in1=NMA, op=ALU.add),
                 [b_Rsb[h], b_const], [b_RA[h]])
            S.op("pool", lambda e: e.tensor_tensor(out=RQ[h][:], in0=Rsb[h][:], in1=NMQ, op=ALU.add),
                 [b_Rsb[h], b_const], [b_RQ[h]])
            S.op("pe", lambda e: e.matmul(X2[h][:, 0:128], lhsT=kT, rhs=kT, start=True, stop=True), r_kT, [b_X2[h]])
            S.op("pe", lambda e: e.matmul(X2[h][:, 128:256], lhsT=kT, rhs=qT, start=True, stop=True),
                 r_kT + r_qT, [b_X2[h]])
            yield
            S.op("act", lambda e: e.activation(out=DA[h][:], in_=RA[h][:], func=AF.Exp, scale=-1.0, bias=gc2[h][:, 0:1]),
                 [b_RA[h], b_gc2[h]], [b_DA[h]])
            S.op("act", lambda e: e.activation(out=DQ[h][:], in_=RQ[h][:], func=AF.Exp, scale=1.0, bias=gc2[h][:, 1:2]),
                 [b_RQ[h], b_gc2[h]], [b_DQ[h]])
            yield
            Abf = AB[h][:, 0, :]
            Bbf = AB[h][:, 1, :]
            S.op("dve", lambda e: e.scalar_tensor_tensor(out=Abf, in0=X2[h][:, 0:128], scalar=beta,
                                                         in1=DA[h][:], op0=ALU.mult, op1=ALU.mult),
                 [b_X2[h], bg, b_DA[h]], [b_AB[h]])
            S.op("dve", lambda e: e.tensor_tensor(out=QKT[h][:], in0=X2[h][:, 128:256], in1=DQ[h][:], op=ALU.mult),
                 [b_X2[h], b_DQ[h]], [b_QKT[h]])
            S.op("pe", lambda e: e.transpose(X3b[h][:], Abf, identb[:]), [b_AB[h], b_const], [b_X3b[h]])
            yield
            S.op("act", lambda e: e.activation(out=Bbf, in_=X3b[h][:], func=AF.Copy), [b_X3b[h]], [b_AB[h]])
            S.op("pe", lambda e: e.transpose(X4[h][:, 0:128], kT, identb[:]), r_kT + [b_const], [b_X4[h]])
            S.op("pe", lambda e: e.transpose(X4[h][:, 128:256], vT, identb[:]), r_vT + [b_const], [b_X4[h]])
            yield
            yz0 = YZ[h][0]
            pq0 = PQ[h][0]
            S.op("pool", lambda e: e.tensor_tensor(out=yz0[:, 1, :], in0=Abf, in1=cmt[:, 0, :], op=ALU.mult),
                 [b_AB[h], b_const], [b_YZ[h][0]])
            S.op("pool", lambda e: e.tensor_tensor(out=yz0[:, 0, :], in0=Bbf, in1=cmt[:, 1, :], op=ALU.mult),
                 [b_AB[h], b_const], [b_YZ[h][0]])
            S.op("pool", lambda e: e.tensor_tensor(out=pq0[:, 0, :], in0=identb[:], in1=yz0[:, 0, :], op=ALU.subtract),
                 [b_YZ[h][0], b_const], [b_PQ[h][0]])
            S.op("pool", lambda e: e.tensor_tensor(out=pq0[:, 1, :], in0=identb[:], in1=yz0[:, 1, :], op=ALU.subtract),
                 [b_YZ[h][0], b_const], [b_PQ[h][0]])
            S.op("act", lambda e: e.activation(out=kbg[h][:], in_=X4[h][:, 0:128], func=AF.Copy, scale=gc2[h][:, 3:4]),
                 [b_X4[h], b_gc2[h]], [b_kbg[h]])
            S.op("dve", lambda e: e.tensor_scalar(out=kdec[h][:], in0=X4[h][:, 0:128], scalar1=DQ[h][:, 127:128],
                                                  scalar2=None, op0=ALU.mult), [b_X4[h], b_DQ[h]], [b_kdec[h]])
            S.op("act", lambda e: e.activation(out=vb[h][:], in_=X4[h][:, 128:256], func=AF.Copy, scale=beta),
                 [b_X4[h], bg], [b_vb[h]])
            S.op("pool", lambda e: e.tensor_tensor(out=qdT[h][:], in0=qT, in1=egb[h][:], op=ALU.mult),
                 r_qT + [b_egb[h]], [b_qdT[h]])
            yield
            yz = 0
            pm = 0
            for lev in range(2):
                cur = YZ[h][yz]
                S.op("pe", lambda e, cur=cur: e.matmul(X1[h][:, 0:128], lhsT=cur[:, 1, :], rhs=cur[:, 0, :],
                                                       start=True, stop=True), [b_YZ[h][yz]], [b_X1[h]])
                S.op("pe", lambda e, cur=cur: e.matmul(X1[h][:, 128:256], lhsT=cur[:, 0, :], rhs=cur[:, 1, :],
                                                       start=True, stop=True), [b_YZ[h][yz]], [b_X1[h]])
                yield
                nyz = 1 - yz
                nxt = YZ[h][nyz]
                S.op("act", lambda e, nxt=nxt: e.activation(out=nxt[:, 0, :], in_=X1[h][:, 0:128], func=AF.Copy),
                     [b_X1[h]], [b_YZ[h][nyz]])
                S.op("dve", lambda e, nxt=nxt: e.tensor_copy(out=nxt[:, 1, :], in_=X1[h][:, 128:256]),
                     [b_X1[h]], [b_YZ[h][nyz]])
                yz = nyz
                yield
                pq = PQ[h][pm]
                S.op("pe", lambda e, nxt=nxt, pq=pq: e.matmul(X3[h][:], lhsT=nxt[:, 1, :], rhs=pq[:, 0, :],
                                                              start=True, stop=True),
                     [b_YZ[h][yz], b_PQ[h][pm]], [b_X3[h]])
                S.op("pe", lambda e, nxt=nxt, pq=pq: e.matmul(X2[h][:, 0:128], lhsT=nxt[:, 0, :], rhs=pq[:, 1, :],
                                                              start=True, stop=True),
                     [b_YZ[h][yz], b_PQ[h][pm]], [b_X2[h]])
                yield
                npm = 1 - pm
                npq = PQ[h][npm]
                S.op("dve", lambda e, pq=pq, npq=npq: e.tensor_tensor(out=npq[:, 0, :], in0=pq[:, 0, :], in1=X3[h][:],
                                                                      op=ALU.add),
                     [b_PQ[h][pm], b_X3[h]], [b_PQ[h][npm]])
                S.op("dve", lambda e, pq=pq, npq=npq: e.tensor_tensor(out=npq[:, 1, :], in0=pq[:, 1, :],
                                                                      in1=X2[h][:, 0:128], op=ALU.add),
                     [b_PQ[h][pm], b_X2[h]], [b_PQ[h][npm]])
                pm = npm
                yield
            for m in range(4):
                lastm = m == 3
                pq = PQ[h][pm]
                S.op("pe", lambda e, pq=pq: e.matmul(X1[h][:, 0:128], lhsT=Abf, rhs=pq[:, 0, :], start=True, stop=True),
                     [b_AB[h], b_PQ[h][pm]], [b_X1[h]])
                if not lastm:
                    S.op("pe", lambda e, pq=pq: e.matmul(X1[h][:, 128:256], lhsT=Bbf, rhs=pq[:, 1, :],
                                                         start=True, stop=True), [b_AB[h], b_PQ[h][pm]], [b_X1[h]])
                yield
                S.op("dve", lambda e, m=m: e.tensor_tensor(out=Wm[h][:, 0, :], in0=X1[h][:, 0:128],
                                                           in1=cmt[:, 3 + 2 * m, :], op=ALU.mult),
                     [b_X1[h], b_const], [b_Wm[h]])
                if not lastm:
                    S.op("dve", lambda e, m=m: e.tensor_tensor(out=Wm[h][:, 1, :], in0=X1[h][:, 128:256],
                                                               in1=cmt[:, 2 + 2 * m, :], op=ALU.mult),
                         [b_X1[h], b_const], [b_Wm[h]])
                yield
                S.op("pe", lambda e, pq=pq: e.matmul(X3[h][:], lhsT=pq[:, 1, :], rhs=Wm[h][:, 0, :], start=True, stop=True),
                     [b_PQ[h][pm], b_Wm[h]], [b_X3[h]])
                if not lastm:
                    S.op("pe", lambda e, pq=pq: e.matmul(X2[h][:, 0:128], lhsT=pq[:, 0, :], rhs=Wm[h][:, 1, :],
                                                         start=True, stop=True), [b_PQ[h][pm], b_Wm[h]], [b_X2[h]])
                yield
                npm = 1 - pm
                npq = PQ[h][npm]
                S.op("dve", lambda e, pq=pq, npq=npq: e.tensor_tensor(out=npq[:, 0, :], in0=pq[:, 0, :], in1=X3[h][:],
                                                                      op=ALU.subtract),
                     [b_PQ[h][pm], b_X3[h]], [b_PQ[h][npm]])
                if not lastm:
                    S.op("dve", lambda e, pq=pq, npq=npq: e.tensor_tensor(out=npq[:, 1, :], in0=pq[:, 1, :],
                                                                          in1=X2[h][:, 0:128], op=ALU.subtract),
                         [b_PQ[h][pm], b_X2[h]], [b_PQ[h][npm]])
                pm = npm
                yield
            TT = PQ[h][pm][:, 0, :]
            bTT = b_PQ[h][pm]
            S.op("pe", lambda e: e.matmul(X2[h][:, 0:128], lhsT=TT, rhs=vb[h][:], start=True, stop=True),
                 [bTT, b_vb[h]], [b_X2[h]])
            S.op("pe", lambda e: e.matmul(X2[h][:, 128:256], lhsT=kbg[h][:], rhs=TT, start=True, stop=True),
                 [bTT, b_kbg[h]], [b_X2[h]])
            yield
            S.op("act", lambda e: e.activation(out=Usb[h][:], in_=X2[h][:, 0:128], func=AF.Copy), [b_X2[h]], [b_Usb[h]])
            S.op("dve", lambda e: e.tensor_copy(out=wT[h][:], in_=X2[h][:, 128:256]), [b_X2[h]], [b_wT[h]])
            yield
            S.op("pe", lambda e: e.matmul(X5[h][:, 0:128], lhsT=wT[h][:], rhs=Sb[h][:], start=True, stop=True),
                 [b_wT[h], b_Sb[h]], [b_X5a[h]])
            yield
            S.op("dve", lambda e: e.tensor_tensor(out=vnew[h][:], in0=Usb[h][:], in1=X5[h][:, 0:128], op=ALU.subtract),
                 [b_Usb[h], b_X5a[h]], [b_vnew[h]])
            S.op("pe", lambda e: e.matmul(X5[h][:, 128:256], lhsT=qdT[h][:], rhs=Sb[h][:], start=True, stop=False),
                 [b_qdT[h], b_Sb[h]], [b_X5b[h]])
            yield
            S.op("pe", lambda e: e.matmul(X5[h][:, 128:256], lhsT=QKT[h][:], rhs=vnew[h][:], start=False, stop=True),
                 [b_QKT[h], b_vnew[h]], [b_X5b[h]])
            S.op("pe", lambda e: e.matmul(X5[h][:, 0:128], lhsT=kdec[h][:], rhs=vnew[h][:], start=True, stop=True),
                 [b_kdec[h], b_vnew[h]], [b_X5a[h]])
            yield
            S.op("dve", lambda e: e.scalar_tensor_tensor(out=Sf[h][:], in0=Sf[h][:], scalar=egb[h][:, 127:128],
                                                         in1=X5[h][:, 0:128], op0=ALU.mult, op1=ALU.add),
                 [b_Sf[h], b_egb[h], b_X5a[h]], [b_Sf[h]])
            S.op("act", lambda e: e.activation(out=junk[h][:], in_=X5[h][:, 128:256], func=AF.Square,
                                               accum_out=ost[h][:, 0:1]), [b_X5b[h]], [b_junk[h], b_ost[h]])
            yield
            S.op("act", lambda e: e.activation(out=Sb[h][:], in_=Sf[h][:], func=AF.Copy), [b_Sf[h]], [b_Sb[h]])
            S.op("act", lambda e: e.activation(out=ost[h][:, 1:2], in_=ost[h][:, 0:1], func=AF.Ln, scale=1.0 / 128,
                                               bias=misc[:, 0:1]), [b_ost[h], b_const], [b_ost[h]])
            S.op("act", lambda e: e.activation(out=ost[h][:, 2:3], in_=ost[h][:, 1:2], func=AF.Exp, scale=-0.5),
                 [b_ost[h]], [b_ost[h]])
            yield
            S.op("dve", lambda e: e.scalar_tensor_tensor(out=ogt[h][:], in0=X5[h][:, 128:256], scalar=ost[h][:, 2:3],
                                                         in1=gz[cg % 4][:, h * 128:(h + 1) * 128], op0=ALU.mult,
                                                         op1=ALU.mult),
                 [b_X5b[h], b_ost[h], b_gz[cg % 4]], [b_ogt[h]])
            S.op("pe", lambda e: e.transpose(XT[h][:], ogt[h][:], identb[:]), [b_ogt[h], b_const], [b_XT[h]])
            yield
            sg = (cg // 4) % 2
            sp_ = cg % 4
            S.op("act", lambda e: e.activation(out=ostg[h][sg][:, sp_ * 128:(sp_ + 1) * 128], in_=XT[h][:], func=AF.Copy),
                 [b_XT[h]], [b_ostg[h][sg]])
            if sp_ == 3:
                tb = (cg // 4) * 512
                fin.append(S.dma("sp", lambda e: e.dma_start(out=og[h, tb // 2048, :, tb % 2048:tb % 2048 + 512],
                                                             in_=ostg[h][sg][:]), [b_ostg[h][sg]], [D.buf("og")]))

        def stage1(t):
            ti = t % 2
            rank = (t * TW) // 2048
            toff = (t * TW) % 2048
            S.dma("sp" if t % 2 == 0 else "pool", lambda e: e.dma_start(
                out=xt[ti][:], in_=xh[rank, :, :, toff:toff + TW].rearrange("c p t -> p c t")), [D.buf("xh")], [b_xt[ti]])

            def proj(g):
                pi = g % 2
                for dc in range(8):
                    S.op("pe", lambda e, dc=dc: e.matmul(pj[pi][:], lhsT=wq[:, dc, g * 128:(g + 1) * 128],
                                                         rhs=xt[ti][:, dc, :], start=(dc == 0), stop=(dc == 7)),
                         [b_w, b_xt[ti]], [b_pj[pi]])
                if t > 0:
                    S.op("pool", lambda e: e.tensor_copy(out=Pc[g][ti][:, 0:3], in_=Pc[g][1 - ti][:, TW:TW + 3]),
                         [b_Pc[g][1 - ti]], [b_Pc[g][ti]])
                S.op("act", lambda e: e.activation(out=Pc[g][ti][:, 3:TW + 3], in_=pj[pi][:], func=AF.Copy),
                     [b_pj[pi]], [b_Pc[g][ti]])

            def conv(g):
                pi = g % 2
                for j in range(4):
                    S.op("pe", lambda e, j=j: e.matmul(pc[pi][:], lhsT=dg[:, g * 4 + j, :],
                                                       rhs=Pc[g][ti][:, j:j + TW], start=(j == 0), stop=(j == 3)),
                         [b_const, b_Pc[g][ti]], [b_pc[pi]])
                S.op("act", lambda e: e.activation(out=s1[g][ti][:], in_=pc[pi][:], func=AF.Silu),
                     [b_pc[pi]], [b_s1[g][ti]])

            def zab_a(ci):
                gi = (t * 2 + ci) % 4
                for dc in range(8):
                    S.op("pe", lambda e, dc=dc: e.matmul(pz[:], lhsT=xt[ti][:, dc, ci * 128:(ci + 1) * 128],
                                                         rhs=wz[:, dc, :], start=(dc == 0), stop=(dc == 7)),
                         [b_w, b_xt[ti]], [b_pz])
                S.op("act", lambda e: e.activation(out=sz[ci][:], in_=pz[:, 0:256], func=AF.Silu), [b_pz], [b_sz[ci]])
                S.op("dve", lambda e: e.tensor_tensor(out=gt[gi][:, 0:2], in0=pz[:, 256:258], in1=hpt[:, 2:4],
                                                      op=ALU.add), [b_pz, b_const], [b_gt[gi]])
                S.op("dve", lambda e: e.tensor_copy(out=gt[gi][:, 8:10], in_=pz[:, 258:260]), [b_pz], [b_gt[gi]])
                S.op("pool", lambda e: e.tensor_tensor(out=gz[gi][:], in0=sz[ci][:], in1=ongt[:], op=ALU.mult),
                     [b_sz[ci], b_const], [b_gz[gi]])

            def zab_b(ci):
                gi = (t * 2 + ci) % 4
                S.op("act", lambda e: e.activation(out=gt[gi][:, 2:4], in_=gt[gi][:, 0:2], func=AF.Exp),
                     [b_gt[gi]], [b_gt[gi]])
                S.op("act", lambda e: e.activation(out=gt[gi][:, 10:12], in_=gt[gi][:, 8:10], func=AF.Exp, scale=-1.0),
                     [b_gt[gi]], [b_gt[gi]])

            def zab_c(ci):
                gi = (t * 2 + ci) % 4
                S.op("act", lambda e: e.activation(out=gt[gi][:, 4:6], in_=gt[gi][:, 2:4], func=AF.Ln,
                                                   bias=misc[:, 2:3]), [b_gt[gi], b_const], [b_gt[gi]])
                S.op("dve", lambda e: e.tensor_scalar(out=gt[gi][:, 10:12], in0=gt[gi][:, 10:12], scalar1=1.0,
                                                      scalar2=None, op0=ALU.add), [b_gt[gi]], [b_gt[gi]])

            def zab_d(ci):
                gi = (t * 2 + ci) % 4
                S.op("dve", lambda e: e.tensor_tensor(out=gt[gi][:, 6:8], in0=gt[gi][:, 4:6], in1=misc[:, 3:5],
                                                      op=ALU.mult), [b_gt[gi], b_const], [b_gt[gi]])
                S.op("dve", lambda e: e.reciprocal(out=gt[gi][:, 12:14], in_=gt[gi][:, 10:12]), [b_gt[gi]], [b_gt[gi]])

            def l2n_a(g):
                pi = g % 2
                isq = g < 2
                S.op("act", lambda e: e.activation(out=sqb[pi][:], in_=s1[g][ti][:], func=AF.Square,
                                                   scale=(float(np.sqrt(128.0)) if isq else 1.0)),
                     [b_s1[g][ti]], [b_sqb[pi]])
                S.op("pe", lambda e: e.matmul(pc[pi][:], lhsT=onesb[:], rhs=sqb[pi][:], start=True, stop=True),
                     [b_const, b_sqb[pi]], [b_pc[pi]])

            def l2n_b(g):
                pi = g % 2
                isq = g < 2
                S.op("act", lambda e: e.activation(out=lnb[pi][:], in_=pc[pi][:], func=AF.Ln,
                                                   bias=(misc[:, 1:2] if isq else misc[:, 0:1])),
                     [b_pc[pi], b_const], [b_lnb[pi]])
                S.op("act", lambda e: e.activation(out=lnb[pi][:], in_=lnb[pi][:], func=AF.Exp, scale=-0.5),
                     [b_lnb[pi]], [b_lnb[pi]])

            def l2n_c(g):
                pi = g % 2
                S.op("dve", lambda e: e.tensor_tensor(out=qn[g][ti][:], in0=s1[g][ti][:], in1=lnb[pi][:], op=ALU.mult),
                     [b_s1[g][ti], b_lnb[pi]], [b_qn[g][ti]])

            for g in range(6):
                proj(g)
                yield
            for g in range(6):
                conv(g)
            for ci in range(2):
                zab_a(ci)
            yield
            for f in (zab_b, zab_c, zab_d):
                for ci in range(2):
                    f(ci)
                yield
            for g in range(4):
                l2n_a(g)
                yield
                l2n_b(g)
                yield
                l2n_c(g)
                yield

        def drive(chains, bg):
            chains = list(chains)
            while chains:
                nxt = []
                for g in chains:
                    try:
                        next(g)
                        nxt.append(g)
                    except StopIteration:
                        pass
                chains = nxt
                if bg is not None:
                    try:
                        next(bg)
                    except StopIteration:
                        bg = None
            return bg

        for _ in stage1(0):
            pass
        for t in range(ntiles):
            bg = stage1(t + 1) if t + 1 < ntiles else None
            for ci in range(2):
                cg = t * 2 + ci
                bg = drive([chunk_head(0, t % 2, ci, cg), chunk_head(1, t % 2, ci, cg)], bg)
            if bg is not None:
                for _ in bg:
                    pass
        if standalone:
            S.finalize(fin)
        else:
            S.flush()
    return nc if standalone else fin


import ml_dtypes
NPBF = ml_dtypes.bfloat16


def _consts():
    i = np.arange(128)
    U = (i[:, None] <= i[None, :]).astype(np.float32)
    NMA = np.where(i[None, :] >= i[:, None], 30000.0, 0.0).astype(np.float32)
    NMQ = np.where(i[None, :] < i[:, None], -30000.0, 0.0).astype(np.float32)
    I = np.eye(128, dtype=np.float32)
    cf = np.ascontiguousarray(np.stack([U, NMA, NMQ, I], axis=1))
    ms = []
    bi = i[:, None]
    bj = i[None, :]
    m8 = ((bi // 8) == (bj // 8)).astype(np.float32)
    ms += [m8, m8.T]
    for m in range(4):
        sz = 8 << m
        ml = (((bi // sz) == (bj // sz) + 1) & ((bi // (2 * sz)) == (bj // (2 * sz)))).astype(np.float32)
        ms += [ml, ml.T]
    cm = np.ascontiguousarray(np.stack(ms, axis=1)).astype(NPBF)
    return {"ident": I.astype(NPBF), "cf": cf, "cm": cm}


def gdn_inputs(inp, layer, r):
    h0, h1 = 2 * r, 2 * r + 1
    w = inp["a_w_in"][layer]
    cols = []
    for base in (0, 1024, 2048):
        for h in (h0, h1):
            cols.append(np.arange(base + h * 128, base + (h + 1) * 128))
    qkv_cols = np.concatenate(cols)
    zcols = np.concatenate([np.arange(3072 + h * 128, 3072 + (h + 1) * 128) for h in (h0, h1)])
    ab_cols = np.array([4096 + h0, 4096 + h1, 4104 + h0, 4104 + h1])
    wqkv = np.ascontiguousarray(w[:, qkv_cols])
    wzab = np.ascontiguousarray(w[:, np.concatenate([zcols, ab_cols])])
    gn = np.ascontiguousarray(inp["a_norm_g"][layer].reshape(8, 128).T)
    cwf = inp["a_conv_w"][layer][:, qkv_cols]
    cw = np.ascontiguousarray(cwf.reshape(4, 6, 128).transpose(2, 1, 0).reshape(128, 24))
    hp = np.broadcast_to(np.array([inp["a_log"][layer][h0], inp["a_log"][layer][h1],
                                   inp["a_dt_bias"][layer][h0], inp["a_dt_bias"][layer][h1]], np.float32), (128, 4))
    ong = np.broadcast_to(np.tile(inp["a_out_norm_g"][layer], 2), (128, 256))
    d = {"wqkv": wqkv, "wzab": wzab, "gn": gn, "cw": cw, "hp": np.ascontiguousarray(hp),
         "ong": np.ascontiguousarray(ong)}
    d.update(_consts())
    return d


TZL = 2432


def build_moba(ngroups=16, P=None, io=None, tag="", kv_mode="compute"):
    standalone = P is None
    nc = bass.Bass("TRN2", target_bir_lowering=False) if standalone else P.nc
    D = Dram(nc, io)
    xh = D("xh", [4, 8, 128, 2048], BF16, "ExternalInput")
    xh2 = D("xh2", [4, 8, 128, 2048], BF16, "ExternalInput")
    wqz = D("wqz", [1024, 512], F32, "ExternalInput")
    wkv = D("wkv", [1024, 512], F32, "ExternalInput")
    gn = D("gn", [128, 16], F32, "ExternalInput")
    tz = D("tz", [2, 128, TZL], F32, "ExternalInput")
    b31 = D("b31", [128, 2], F32, "ExternalInput")
    lsel_d = D("lsel", [32, 32, 128], BF16, "ExternalInput")
    ident_d = D("ident", [128, 128], BF16, "ExternalInput")
    og = D("og", [2, 4, 128, 2048], BF16, "ExternalOutput")
    SCALE = float(128 ** -0.5)
    BIG = 30000.0
    with ExitStack() as es:
        C = Ctx(nc, es, parent=P, tag=tag)
        S = C.S
        fin = []
        b_const = Buf()
        identb = C.sb("identb", [128, 128], BF16)
        gnt = C.sb("gnt", [128, 16], F32)
        b31t = C.sb("b31t", [128, 4], F32)
        lsel = C.sb("lselt", [32, 32, 128], BF16)
        S.dma("sp", lambda e: e.dma_start(out=identb[:], in_=ident_d[:, :]), [], [b_const])
        S.dma("sp", lambda e: e.dma_start(out=gnt[:], in_=gn[:, :]), [], [b_const])
        S.dma("sp", lambda e: e.dma_start(out=b31t[:, 0:2], in_=b31[:, :]), [], [b_const])
        S.dma("sp", lambda e: e.dma_start(out=lsel[:], in_=lsel_d[:, :, :]), [], [b_const])
        S.op("pool", lambda e: e.memset(b31t[:, 2:3], 0.0), [], [b_const])
        ebT = [C.sb("ebT%d" % h, [128, TZL], BF16) for h in range(2)]
        tzs = C.sb("tzs", [128, TZL], F32)
        b_tzs = Buf()
        for h in range(2):
            S.dma("sp", lambda e, h=h: e.dma_start(out=tzs[:], in_=tz[h, :, :]), [], [b_tzs])
            S.op("act", lambda e, h=h: e.activation(out=ebT[h][:], in_=tzs[:], func=AF.Exp), [b_tzs], [b_const])
        wq = C.sb("wq", [128, 8, 512], BF16)
        wk = C.sb("wk", [128, 8, 512], BF16)
        wst = [C.sb("wst%d" % i, [128, 1024], F32) for i in range(2)]
        b_wst = [Buf(), Buf()]
        b_w = Buf()
        for dc in range(8):
            i = dc % 2
            S.dma("sp", lambda e, i=i, dc=dc: e.dma_start(out=wst[i][:, 0:512], in_=wqz[dc * 128:(dc + 1) * 128, :]),
                  [], [b_wst[i]])
            S.dma("pool", lambda e, i=i, dc=dc: e.dma_start(out=wst[i][:, 512:1024], in_=wkv[dc * 128:(dc + 1) * 128, :]),
                  [], [b_wst[i]])
            S.op("dve", lambda e, i=i, dc=dc: e.tensor_scalar(out=wq[:, dc, :], in0=wst[i][:, 0:512],
                                                              scalar1=gnt[:, dc:dc + 1], scalar2=None, op0=ALU.mult),
                 [b_wst[i], b_const], [b_w])
            S.op("dve", lambda e, i=i, dc=dc: e.tensor_scalar(out=wk[:, dc, :], in0=wst[i][:, 512:1024],
                                                              scalar1=gnt[:, 8 + dc:9 + dc], scalar2=None, op0=ALU.mult),
                 [b_wst[i], b_const], [b_w])
        banks = C.banks
        bankb = C.bankb
        KT = C.sb("KT", [128, 2, 8192], BF16)
        b_KT = [Buf() for _ in range(16)]
        Vaug = C.sb("Vaug", [128, 2, 64, 130], BF16)
        b_V = [Buf() for _ in range(16)]
        kmT = C.sb("kmT", [128, 2, 32], F32)
        b_km = Buf()
        S.op("pool", lambda e: e.memset(Vaug[:, :, :, 128:130], 1.0), [], [b_V[0]])
        xt = [C.sb("xt%d" % i, [128, 8, 512], BF16) for i in range(2)]
        b_xt = [Buf(), Buf()]

        def load_x(src, T, i, bname="xh"):
            rank = (T * 512) // 2048
            toff = (T * 512) % 2048
            S.dma("sp" if T % 2 == 0 else "pool", lambda e: e.dma_start(
                out=xt[i][:], in_=src[rank, :, :, toff:toff + 512].rearrange("c p t -> p c t")), [D.buf(bname)], [b_xt[i]])

        def phase_a(T):
            i = T % 2
            load_x(xh2, T, i, "xh2")

            def kproj(h):
                for dc in range(8):
                    S.op("pe", lambda e, dc=dc: e.matmul(banks[4][:], lhsT=wk[:, dc, h * 128:(h + 1) * 128],
                                                         rhs=xt[i][:, dc, :], start=(dc == 0), stop=(dc == 7)),
                         [b_w, b_xt[i]], [bankb[4]])
                S.op("act", lambda e: e.activation(out=KT[:, h, T * 512:(T + 1) * 512], in_=banks[4][:], func=AF.Copy),
                     [bankb[4]], [b_KT[T]])
                S.op("dve", lambda e: e.tensor_reduce(out=kmT[:, h, 2 * T:2 * T + 2],
                                                      in_=banks[4][:].rearrange("p (b t) -> p b t", b=2),
                                                      axis=AX.X, op=ALU.add), [bankb[4]], [b_km])

            def vproj(s):
                for dc in range(8):
                    S.op("pe", lambda e, dc=dc: e.matmul(banks[5][:, 0:256], lhsT=xt[i][:, dc, s * 128:(s + 1) * 128],
                                                         rhs=wk[:, dc, 256:512], start=(dc == 0), stop=(dc == 7)),
                         [b_w, b_xt[i]], [bankb[5]])
                S.op("dve", lambda e: e.tensor_copy(out=Vaug[:, :, 4 * T + s, 0:128],
                                                    in_=banks[5][:, 0:256].rearrange("p (h d) -> p h d", h=2)),
                     [bankb[5]], [b_V[T]])
            for h in range(2):
                kproj(h)
            for s in range(4):
                vproj(s)
        for T in range(16):
            phase_a(T)

        q_bf = [[C.sb("q_bf%d_%d" % (p, h), [128, 512], BF16) for h in range(2)] for p in range(2)]
        q_f = [[C.sb("q_f%d_%d" % (p, h), [128, 512], F32) for h in range(2)] for p in range(2)]
        b_q = [[Buf(), Buf()] for p in range(2)]
        gzt = [C.sb("gzt%d" % p, [128, 4, 256], F32) for p in range(2)]
        b_gz = [Buf(), Buf()]
        MT = [[C.sb("MT%d_%d" % (p, h), [32, 512], BF16) for h in range(2)] for p in range(2)]
        b_MT = [[Buf(), Buf()] for p in range(2)]
        gsb = [C.sb("gsb%d" % i, [128, 32], F32) for i in range(2)]
        m8 = [C.sb("m8_%d" % i, [128, 8], F32) for i in range(2)]
        mbf = [C.sb("mbf%d" % i, [128, 32], F32) for i in range(2)]
        mbb = [C.sb("mbb%d" % i, [128, 32], BF16) for i in range(2)]
        b_gs = [Buf(), Buf()]
        pT = [C.sb("pT%d" % i, [128, 512], BF16) for i in range(2)]
        b_pT = [Buf(), Buf()]
        rinv = C.sb("rinv", [128, 4], F32)
        b_rinv = Buf()
        ogt = C.sb("ogt", [128, 4, 128], BF16)
        b_ogt = Buf()
        ostg = [C.sb("ostg%d" % i, [128, 512], BF16) for i in range(2)]
        b_ostg = [Buf(), Buf()]
        mtp = banks[6][:, 64:128].bitcast(BF16)
        otp = banks[7][:, 0:256].bitcast(BF16)

        def oacc(s):
            return banks[2 + s // 2][:, (s % 2) * 130:(s % 2) * 130 + 129]

        def prep(G):
            p = G % 2
            i = G % 2
            load_x(xh, G, i)

            def qproj(h):
                for dc in range(8):
                    S.op("pe", lambda e, dc=dc: e.matmul(banks[4][:], lhsT=wq[:, dc, h * 128:(h + 1) * 128],
                                                         rhs=xt[i][:, dc, :], start=(dc == 0), stop=(dc == 7)),
                         [b_w, b_xt[i]], [bankb[4]])
                S.op("act", lambda e: e.activation(out=q_bf[p][h][:], in_=banks[4][:], func=AF.Copy),
                     [bankb[4]], [b_q[p][h]])
                S.op("dve", lambda e: e.tensor_copy(out=q_f[p][h][:], in_=banks[4][:]), [bankb[4]], [b_q[p][h]])

            def zproj(s):
                for dc in range(8):
                    S.op("pe", lambda e, dc=dc: e.matmul(banks[5][:, 0:256], lhsT=xt[i][:, dc, s * 128:(s + 1) * 128],
                                                         rhs=wq[:, dc, 256:512], start=(dc == 0), stop=(dc == 7)),
                         [b_w, b_xt[i]], [bankb[5]])
                S.op("act", lambda e: e.activation(out=gzt[p][:, s, :], in_=banks[5][:, 0:256], func=AF.Silu),
                     [bankb[5]], [b_gz[p]])

            def gate_a(h, s, k):
                cur = 2 * G + s // 2
                S.op("pool", lambda e: e.memset(gsb[k][:], -3.0e38), [], [b_gs[k]])
                if cur > 0:
                    S.op("pe", lambda e: e.matmul(banks[6][:, 0:32], lhsT=q_f[p][h][:, s * 128:(s + 1) * 128],
                                                  rhs=kmT[:, h, :], start=True, stop=True),
                         [b_q[p][h], b_km], [bankb[6]])
                    S.op("dve", lambda e: e.tensor_copy(out=gsb[k][:, 0:cur], in_=banks[6][:, 0:cur]),
                         [bankb[6]], [b_gs[k]])

            def gate_b(h, s, k):
                cur = 2 * G + s // 2
                S.op("dve", lambda e: e.max(out=m8[k][:], in_=gsb[k][:]), [b_gs[k]], [b_gs[k]])
                S.op("dve", lambda e: e.tensor_scalar(out=mbf[k][:], in0=gsb[k][:], scalar1=m8[k][:, 2:3], scalar2=None,
                                                      op0=ALU.is_ge), [b_gs[k]], [b_gs[k]])
                S.op("dve", lambda e: e.tensor_scalar(out=mbb[k][:], in0=mbf[k][:], scalar1=-1.0, scalar2=BIG,
                                                      op0=ALU.add, op1=ALU.mult), [b_gs[k]], [b_gs[k]])
                S.op("pool", lambda e: e.memset(mbb[k][:, cur:cur + 1], 0.0), [b_gs[k]], [b_gs[k]])
                if cur < 31:
                    S.op("pool", lambda e: e.memset(mbb[k][:, cur + 1:32], -BIG), [b_gs[k]], [b_gs[k]])

            def gate_c(h, s, k):
                S.op("pe", lambda e: e.transpose(mtp[0:32, :], mbb[k][:], identb[:]), [b_gs[k], b_const], [bankb[6]])
                S.op("act", lambda e: e.activation(out=MT[p][h][:, s * 128:(s + 1) * 128], in_=mtp[0:32, :], func=AF.Copy),
                     [bankb[6]], [b_MT[p][h]])

            for h in range(2):
                qproj(h)
                yield
            for s in range(4):
                zproj(s)
            yield
            kk = 0
            for h in range(2):
                for s in range(4):
                    k = kk % 2
                    kk += 1
                    gate_a(h, s, k)
                    yield
                    gate_b(h, s, k)
                    yield
                    gate_c(h, s, k)
                    yield

        def step(bg):
            if bg[0] is not None:
                try:
                    next(bg[0])
                except StopIteration:
                    bg[0] = None

        def attend(G, h, bg):
            p = G % 2
            NJ = 4 * G + 4

            def qk(j):
                n = j // 2
                S.op("pe", lambda e: e.matmul(banks[j % 2][:], lhsT=KT[:, h, j * 128:(j + 1) * 128], rhs=q_bf[p][h][:],
                                              start=True, stop=False), [b_KT[j // 4], b_q[p][h]], [bankb[j % 2]])
                S.op("pe", lambda e: e.matmul(banks[j % 2][:], lhsT=lsel[:, n, :], rhs=MT[p][h][:],
                                              start=False, stop=True), [b_const, b_MT[p][h]], [bankb[j % 2]])

            def ex(j):
                d0 = 512 * G - 128 * j
                far = d0 >= 1664
                bias = b31t[:, h:h + 1] if far else b31t[:, 2:3]
                S.op("act", lambda e: e.activation(out=pT[j % 2][:], in_=banks[j % 2][:], func=AF.Exp, scale=SCALE,
                                                   bias=bias), [bankb[j % 2], b_const], [b_pT[j % 2]])
                if not far:
                    off = d0 + 384
                    S.op("pool", lambda e: e.tensor_tensor(out=pT[j % 2][:], in0=pT[j % 2][:],
                                                           in1=ebT[h][:, off:off + 512], op=ALU.mult),
                         [b_pT[j % 2], b_const], [b_pT[j % 2]])

            def pv(j):
                for s in range(4):
                    S.op("pe", lambda e, s=s: e.matmul(oacc(s), lhsT=pT[j % 2][:, s * 128:(s + 1) * 128],
                                                       rhs=Vaug[:, h, j, 0:129], start=(j == 0 and s % 2 == 0),
                                                       stop=(j == NJ - 1), skip_group_check=True),
                         [b_pT[j % 2], b_V[j // 4]], [bankb[2 + s // 2]])
            qk(0)
            for j in range(NJ):
                if j + 1 < NJ:
                    qk(j + 1)
                ex(j)
                pv(j)
                step(bg)

            def fin_s(s):
                S.op("dve", lambda e: e.reciprocal(out=rinv[:, s:s + 1], in_=oacc(s)[:, 128:129]),
                     [bankb[2 + s // 2]], [b_rinv])
                S.op("dve", lambda e: e.scalar_tensor_tensor(out=ogt[:, s, :], in0=oacc(s)[:, 0:128],
                                                             scalar=rinv[:, s:s + 1],
                                                             in1=gzt[p][:, s, h * 128:(h + 1) * 128],
                                                             op0=ALU.mult, op1=ALU.mult),
                     [bankb[2 + s // 2], b_rinv, b_gz[p]], [b_ogt])
                S.op("pe", lambda e: e.transpose(otp[:, s * 128:(s + 1) * 128], ogt[:, s, :], identb[:]),
                     [b_ogt, b_const], [bankb[7]])
            for s in range(4):
                fin_s(s)
            k = (2 * G + h) % 2
            S.op("act", lambda e: e.activation(out=ostg[k][:], in_=otp[:, 0:512], func=AF.Copy),
                 [bankb[7]], [b_ostg[k]])
            fin.append(S.dma("sp", lambda e: e.dma_start(
                out=og[h, (G * 512) // 2048, :, (G * 512) % 2048:(G * 512) % 2048 + 512], in_=ostg[k][:]),
                [b_ostg[k]], [D.buf("og")]))

        for _ in prep(0):
            pass
        for G in range(ngroups):
            bg = [prep(G + 1) if G + 1 < ngroups else None]
            for h in range(2):
                attend(G, h, bg)
            while bg[0] is not None:
                step(bg)
        if standalone:
            S.finalize(fin)
        else:
            S.flush()
    return nc if standalone else fin


def t5_bucket_np(rel):
    import math
    n = np.maximum(rel, 0)
    nf = np.maximum(n, 1).astype(np.float32)
    large = 16 + (np.log(nf / np.float32(16)) / np.float32(math.log(2048 / 16)) * np.float32(16)).astype(np.int32)
    large = np.minimum(large, 31)
    return np.where(n < 16, n, large)


def moba_inputs(inp, j, r):
    h0, h1 = 2 * r, 2 * r + 1
    w = inp["b_w_in"][j]
    qc = np.concatenate([np.arange(h * 128, (h + 1) * 128) for h in (h0, h1)])
    wqz = np.ascontiguousarray(np.concatenate([w[:, qc], w[:, 1024 + qc]], axis=1))
    wkv = np.ascontiguousarray(np.concatenate([inp["w_kv"][:, qc], inp["w_kv"][:, 1024 + qc]], axis=1))
    gn = np.ascontiguousarray(np.concatenate([inp["b_norm_g"][j].reshape(8, 128).T,
                                              inp["kv_norm_g"].reshape(8, 128).T], axis=1))
    p = np.arange(128)[:, None]
    m = np.arange(TZL)[None, :]
    dist = m - p - 384
    idx = np.where(dist < 0, 32, t5_bucket_np(dist))
    tz = []
    for h in (h0, h1):
        ext = np.concatenate([inp["rel_bias"][:, h], np.array([-30000.0], np.float32)])
        tz.append(ext[idx])
    tz = np.ascontiguousarray(np.stack(tz).astype(np.float32))
    b31 = np.ascontiguousarray(np.broadcast_to(inp["rel_bias"][31, [h0, h1]], (128, 2)))
    lsel = np.zeros((32, 32, 128), np.float32)
    for n in range(32):
        lsel[n, n, :] = 1.0
    d = {"wqz": wqz, "wkv": wkv, "gn": gn, "tz": tz, "b31": b31, "lsel": lsel.astype(NPBF),
         "ident": np.eye(128, dtype=np.float32).astype(NPBF)}
    return d


GDN_W = (("wqkv", [1024, 768], F32), ("wzab", [1024, 260], F32), ("gn", [128, 8], F32), ("cw", [128, 24], F32),
         ("hp", [128, 4], F32), ("ong", [128, 256], F32))
MOBA_W = (("wqz", [1024, 512], F32), ("wkv", [1024, 512], F32), ("gn", [128, 16], F32), ("tz", [2, 128, TZL], F32),
          ("b31", [128, 2], F32))


def build_single(nlayers=4, npairs=4, nchunks=4, mini=False):
    nc = bass.Bass("TRN2", target_bir_lowering=False)

    def ext(name, shape, dt, kind="ExternalInput"):
        return nc.dram_tensor(name, shape, dt, kind=kind).ap()

    def internal(name, shape, dt):
        return nc.dram_tensor(name, shape, dt, kind="Internal").ap()

    with ExitStack() as es:
        P = Ctx(nc, es)
        S = P.S
        x_in = ext("x", [8192, 1024], F32)
        ident = ext("ident", [128, 128], BF16)
        cf = ext("cf", [128, 4, 128], F32)
        cm = ext("cm", [128, 10, 128], BF16)
        lsel = ext("lsel", [32, 32, 128], BF16)
        gf = ext("gf", [1024], F32)
        wo = [ext("wo%d" % l, [1024, 1024], F32) for l in range(4)]
        lw = {}
        for l in range(nlayers):
            spec = GDN_W if l < 2 else MOBA_W
            for p in range(npairs):
                lw[(l, p)] = {n: ext("%s_l%d_p%d" % (n, l, p), sh, dt) for n, sh, dt in spec}
        xs = internal("xs", [8192, 1024], F32)
        XH = [internal("XH%d" % i, [4, 8, 128, 2048], BF16) for i in range(2)]
        if mini:
            OG = ext("ogout", [8, 128, 8192], BF16, "ExternalOutput")
        else:
            OG = internal("OG", [8, 128, 8192], BF16)
            y = ext("y", [8192, 1024], F32, "ExternalOutput")
        b_xs = [Buf() for _ in range(4)]
        b_XH = [Buf(), Buf()]
        b_OG = Buf()
        fin = []
        for r in range(nchunks):
            rows = slice(2048 * r, 2048 * (r + 1))
            build_tok(False, False, P=P, tag="t0c%d_" % r,
                      io={"x": x_in[rows, :], "ident": ident, "xo": xs[rows, :], "xh": XH[0][r],
                          "_bufs": {"x": Buf(), "xo": b_xs[r], "xh": b_XH[0]}})
        for layer in range(nlayers):
            cur = layer % 2
            for p in range(npairs):
                ogv = OG[2 * p:2 * p + 2].rearrange("h p (j t) -> h j p t", j=4)
                io = dict(lw[(layer, p)])
                if layer < 2:
                    io.update({"xh": XH[cur], "ident": ident, "cf": cf, "cm": cm, "og": ogv,
                               "_bufs": {"xh": b_XH[cur], "og": b_OG}})
                    fin = build_gdn(ntiles=(4 if mini else 32), P=P, io=io, tag="g%dp%d_" % (layer, p))
                else:
                    io.update({"xh": XH[cur], "xh2": XH[0], "ident": ident, "lsel": lsel, "og": ogv,
                               "_bufs": {"xh": b_XH[cur], "xh2": b_XH[0], "og": b_OG}})
                    fin = build_moba(P=P, io=io, tag="m%dp%d_" % (layer, p))
            if mini:
                break
            final = layer == 3
            for r in range(nchunks):
                rows = slice(2048 * r, 2048 * (r + 1))
                io = {"x": xs[rows, :], "ident": ident, "og": OG[:, :, 2048 * r:2048 * (r + 1)], "wo": wo[layer],
                      "_bufs": {"x": b_xs[r], "xo": b_xs[r], "og": b_OG, "xh": b_XH[1 - cur], "y": Buf()}}
                if final:
                    io["gf"] = gf
                    io["y"] = y[rows, :]
                else:
                    io["xo"] = xs[rows, :]
                    io["xh"] = XH[1 - cur][r]
                f = build_tok(True, final, P=P, io=io, tag="t%dc%d_" % (layer + 1, r))
                if final:
                    fin = (fin if r else []) + f
        S.finalize(fin)
    return nc, S.nops


def single_maps(inp, mini=False, nlayers=4, npairs=4):
    consts = _consts()
    lsel = np.zeros((32, 32, 128), np.float32)
    for n in range(32):
        lsel[n, n, :] = 1.0
    shared = {"ident": consts["ident"], "cf": consts["cf"], "cm": consts["cm"], "lsel": lsel.astype(NPBF),
              "gf": np.ascontiguousarray(inp["final_norm_g"])}
    for l in range(4):
        shared["wo%d" % l] = np.ascontiguousarray(inp["a_w_out"][l] if l < 2 else inp["b_w_out"][l - 2])
    for l in range(nlayers):
        spec = GDN_W if l < 2 else MOBA_W
        for p in range(npairs):
            src = gdn_inputs(inp, l, p) if l < 2 else moba_inputs(inp, l - 2, p)
            for n, _, _ in spec:
                shared["%s_l%d_p%d" % (n, l, p)] = src[n]
    maps = []
    for b in range(2):
        d = dict(shared)
        d["x"] = np.ascontiguousarray(inp["x"][b].astype(np.float32))
        maps.append(d)
    return maps


def kernel(**inp):
    inp = {k: np.asarray(v) for k, v in inp.items()}
    nc, _ = build_single()
    res = run_bass_kernel_spmd(nc, single_maps(inp), core_ids=[0, 1])
    return np.stack([res.results[b]["y"] for b in range(2)]).astype(np.float32)
```
